# Optimizing a Trainium2 kernel written in Bass

```python
import math
import jax, jax.numpy as jnp
from jax import lax
import numpy as np

D_MODEL = 1024
BATCH = 8
SEQ = 2048
DEPTH = 2

HEAD_DIM = 64
NA_HEADS = 8
NA_WIDTH = NA_HEADS * HEAD_DIM
GRID_W = 64
NA_KH_MAX = 8
NA_KW = 16
NA_QCOLS = 16
NA_KCOLS = NA_QCOLS + NA_KW
SW_HEADS = 8
SW_KV_HEADS = 2
SW_GROUP = SW_HEADS // SW_KV_HEADS
SW_WIDTH = SW_HEADS * HEAD_DIM
SW_KV_WIDTH = SW_KV_HEADS * HEAD_DIM
SW_WINDOW = 128
SW_BLOCK = 128
ROT_DIM = HEAD_DIM // 4
ROPE_THETA = 500000.0
OFF_QNA = NA_WIDTH
OFF_KNA = 2 * NA_WIDTH
OFF_VNA = 3 * NA_WIDTH
OFF_QSW = OFF_VNA + SW_WIDTH
OFF_KSW = OFF_QSW + SW_KV_WIDTH
OFF_VSW = OFF_KSW + SW_KV_WIDTH
N_BRANCHES = 2
PROJ_COLS = OFF_VSW + N_BRANCHES * D_MODEL
D_FF_DENSE = 2816
N_EXPERTS = 8
TOP_K = 2
D_FF_EXPERT = 3584
N_DENSE_LAYERS = (DEPTH + 1) // 2
N_MOE_LAYERS = DEPTH // 2
DEEPNORM_ALPHA = (2 * DEPTH) ** 0.25
DEEPNORM_BETA = (8 * DEPTH) ** -0.25
LN_EPS = 1e-5
NEG_INF = -1e30

kernel_name = "hybrid_natten_swa_moe_deepnorm"


def layer_norm(x, g, b):
    xf = x.astype(jnp.float32)
    mu = jnp.mean(xf, axis=-1, keepdims=True)
    var = jnp.mean(jnp.square(xf - mu), axis=-1, keepdims=True)
    return ((xf - mu) * lax.rsqrt(var + LN_EPS)).astype(x.dtype) * g + b


def partial_rotary(x, pos):
    half = ROT_DIM // 2
    inv_freq = 1.0 / (ROPE_THETA ** (jnp.arange(0, ROT_DIM, 2, dtype=jnp.float32) / ROT_DIM))
    ang = pos.astype(jnp.float32)[:, None] * inv_freq[None, :]
    cos = jnp.cos(ang)[None, :, None, :]
    sin = jnp.sin(ang)[None, :, None, :]
    x1 = x[..., :half].astype(jnp.float32)
    x2 = x[..., half:ROT_DIM].astype(jnp.float32)
    rot = jnp.concatenate([x1 * cos - x2 * sin, x2 * cos + x1 * sin], axis=-1).astype(x.dtype)
    return jnp.concatenate([rot, x[..., ROT_DIM:]], axis=-1)


def neighbourhood_attention(q, k, v, rpb):
    B, S, H, D = q.shape
    rows = S // GRID_W
    kh = min(NA_KH_MAX, rows)
    n_cb = GRID_W // NA_QCOLS
    r = np.arange(rows)
    row_start = np.clip(r - kh // 2, 0, rows - kh)
    key_rows = row_start[:, None] + np.arange(kh)[None, :]
    cb_start = np.clip(np.arange(n_cb) * NA_QCOLS - NA_KW // 2, 0, GRID_W - NA_KCOLS)
    key_cols = cb_start[:, None] + np.arange(NA_KCOLS)[None, :]
    key_tok = (key_rows[:, None, :, None] * GRID_W + key_cols[None, :, None, :])
    key_tok = key_tok.reshape(rows, n_cb, kh * NA_KCOLS)
    kg = jnp.take(k, key_tok, axis=1)
    vg = jnp.take(v, key_tok, axis=1)
    qb = q.reshape(B, rows, n_cb, NA_QCOLS, H, D)
    s = jnp.einsum('brjqhd,brjkhd->brjhqk', qb, kg,
                   preferred_element_type=jnp.float32) * (D ** -0.5)
    qcol = np.arange(n_cb)[:, None] * NA_QCOLS + np.arange(NA_QCOLS)[None, :]
    qcs = np.clip(qcol - NA_KW // 2, 0, GRID_W - NA_KW)
    kcol = key_cols[:, None, :]
    valid = (kcol >= qcs[..., None]) & (kcol < qcs[..., None] + NA_KW)
    dr = key_rows - r[:, None] + (NA_KH_MAX - 1)
    dc = np.clip(kcol - qcol[..., None] + (NA_KW - 1), 0, 2 * NA_KW - 2)
    dri = dr[:, None, None, :, None]
    dci = dc[None, :, :, None, :]
    bias = rpb[:, dri, dci].astype(jnp.float32)
    mask = np.broadcast_to(valid[None, :, :, None, :], (rows, n_cb, NA_QCOLS, kh, NA_KCOLS))
    bias = jnp.where(mask[None], bias, NEG_INF)
    bias = bias.reshape(H, rows, n_cb, NA_QCOLS, kh * NA_KCOLS).transpose(1, 2, 0, 3, 4)
    p = jax.nn.softmax(s + bias[None], axis=-1)
    o = jnp.einsum('brjhqk,brjkhd->brjqhd', p.astype(v.dtype), vg)
    return o.reshape(B, S, H * D)


def sliding_window_attention(q, k, v, sink):
    B, S, HQ, D = q.shape
    nb = S // SW_BLOCK
    pad = ((0, 0), (SW_BLOCK, SW_BLOCK), (0, 0), (0, 0))
    kp = jnp.pad(k, pad).reshape(B, nb + 2, SW_BLOCK, SW_KV_HEADS, D)
    vp = jnp.pad(v, pad).reshape(B, nb + 2, SW_BLOCK, SW_KV_HEADS, D)
    kb = jnp.concatenate([kp[:, :-2], kp[:, 1:-1], kp[:, 2:]], axis=2)
    vb = jnp.concatenate([vp[:, :-2], vp[:, 1:-1], vp[:, 2:]], axis=2)
    qb = q.reshape(B, nb, SW_BLOCK, SW_KV_HEADS, SW_GROUP, D)
    s = jnp.einsum('bnqkgd,bnskd->bnkgqs', qb, kb,
                   preferred_element_type=jnp.float32) * (D ** -0.5)
    qpos = np.arange(SW_BLOCK)[:, None] + SW_BLOCK
    kpos = np.arange(3 * SW_BLOCK)[None, :]
    band = np.abs(qpos - kpos) <= SW_WINDOW
    kabs = np.arange(nb)[:, None, None] * SW_BLOCK + kpos[None] - SW_BLOCK
    valid = band[None] & (kabs >= 0) & (kabs < S)
    s = jnp.where(valid[None, :, None, None], s, NEG_INF)
    sink_logit = sink.astype(jnp.float32).reshape(SW_KV_HEADS, SW_GROUP)[None, None, :, :, None, None]
    m = jnp.maximum(jnp.max(s, axis=-1, keepdims=True), sink_logit)
    p = jnp.exp(s - m)
    denom = jnp.sum(p, axis=-1, keepdims=True) + jnp.exp(sink_logit - m)
    o = jnp.einsum('bnkgqs,bnskd->bnqkgd', (p / denom).astype(v.dtype), vb)
    return o.reshape(B, S, HQ * D)


def mixer_block(x, w_in, b_gate, rpb, sink, w_branch_na, w_branch_sw, w_out):
    B, S, _ = x.shape
    proj = x @ w_in
    q_na, k_na, v_na, q_sw, k_sw, v_sw, gate_logits = jnp.split(
        proj, [OFF_QNA, OFF_KNA, OFF_VNA, OFF_QSW, OFF_KSW, OFF_VSW], axis=-1)
    hs = lambda t, h: t.reshape(B, S, h, HEAD_DIM)
    y_na = neighbourhood_attention(hs(q_na, NA_HEADS), hs(k_na, NA_HEADS), hs(v_na, NA_HEADS), rpb)
    pos = jnp.arange(S, dtype=jnp.int32)
    qs = partial_rotary(hs(q_sw, SW_HEADS), pos)
    ks = partial_rotary(hs(k_sw, SW_KV_HEADS), pos)
    y_sw = sliding_window_attention(qs, ks, hs(v_sw, SW_KV_HEADS), sink)
    y_na = y_na @ w_branch_na
    y_sw = y_sw @ w_branch_sw
    gates = jax.nn.sigmoid((gate_logits + b_gate).astype(jnp.float32)).astype(x.dtype)
    g_na, g_sw = jnp.split(gates, N_BRANCHES, axis=-1)
    return (g_na * y_na + g_sw * y_sw) @ w_out


def swiglu(x, w_gate, w_up, w_down):
    return (jax.nn.silu(x @ w_gate) * (x @ w_up)) @ w_down


def moe_swiglu(x, w_router, w_gate, w_up, w_down):
    B, S, D = x.shape
    xt = x.reshape(B * S, D)
    logits = (xt @ w_router).astype(jnp.float32)
    top_vals, top_idx = lax.top_k(logits, TOP_K)
    top_w = jax.nn.softmax(top_vals, axis=-1)
    combine = jnp.sum(jax.nn.one_hot(top_idx, N_EXPERTS, dtype=jnp.float32) * top_w[..., None], axis=1)
    combine = combine.astype(x.dtype)
    y = jnp.zeros_like(xt)
    for e in range(N_EXPERTS):
        y = y + combine[:, e:e + 1] * swiglu(xt, w_gate[e], w_up[e], w_down[e])
    return y.reshape(B, S, D)


def setup_inputs(seed: int = 0) -> dict:
    key = jax.random.key(seed)
    ks = jax.random.split(key, 24)
    f32 = jnp.float32
    nrm = lambda k, shape, scale: jax.random.normal(k, shape, f32) * scale
    beta = DEEPNORM_BETA
    col_scale = np.ones((PROJ_COLS,), np.float32)
    col_scale[OFF_KNA:OFF_VNA] = beta
    col_scale[OFF_KSW:OFF_VSW] = beta
    w_in = nrm(ks[1], (DEPTH, D_MODEL, PROJ_COLS), D_MODEL ** -0.5) * jnp.asarray(col_scale)
    return {
        "x": nrm(ks[0], (BATCH, SEQ, D_MODEL), 1.0),
        "emb_ln_g": 1.0 + nrm(ks[2], (D_MODEL,), 0.02),
        "emb_ln_b": nrm(ks[3], (D_MODEL,), 0.02),
        "w_in": w_in,
        "b_gate": nrm(ks[4], (DEPTH, N_BRANCHES * D_MODEL), 0.1),
        "na_rpb": nrm(ks[5], (DEPTH, NA_HEADS, 2 * NA_KH_MAX - 1, 2 * NA_KW - 1), 0.2),
        "sw_sink": nrm(ks[6], (DEPTH, SW_HEADS), 0.5),
        "w_branch_na": nrm(ks[7], (DEPTH, NA_WIDTH, D_MODEL), beta * NA_WIDTH ** -0.5),
        "w_branch_sw": nrm(ks[8], (DEPTH, SW_WIDTH, D_MODEL), beta * SW_WIDTH ** -0.5),
        "w_out": nrm(ks[9], (DEPTH, D_MODEL, D_MODEL), beta * D_MODEL ** -0.5),
        "ln1_g": 1.0 + nrm(ks[10], (DEPTH, D_MODEL), 0.02),
        "ln1_b": nrm(ks[11], (DEPTH, D_MODEL), 0.02),
        "ffn_w_gate": nrm(ks[12], (N_DENSE_LAYERS, D_MODEL, D_FF_DENSE), beta * D_MODEL ** -0.5),
        "ffn_w_up": nrm(ks[13], (N_DENSE_LAYERS, D_MODEL, D_FF_DENSE), beta * D_MODEL ** -0.5),
        "ffn_w_down": nrm(ks[14], (N_DENSE_LAYERS, D_FF_DENSE, D_MODEL), beta * D_FF_DENSE ** -0.5),
        "moe_router": nrm(ks[15], (N_MOE_LAYERS, D_MODEL, N_EXPERTS), D_MODEL ** -0.5),
        "moe_w_gate": nrm(ks[16], (N_MOE_LAYERS, N_EXPERTS, D_MODEL, D_FF_EXPERT), beta * D_MODEL ** -0.5),
        "moe_w_up": nrm(ks[17], (N_MOE_LAYERS, N_EXPERTS, D_MODEL, D_FF_EXPERT), beta * D_MODEL ** -0.5),
        "moe_w_down": nrm(ks[18], (N_MOE_LAYERS, N_EXPERTS, D_FF_EXPERT, D_MODEL), beta * D_FF_EXPERT ** -0.5),
        "ln2_g": 1.0 + nrm(ks[19], (DEPTH, D_MODEL), 0.02),
        "ln2_b": nrm(ks[20], (DEPTH, D_MODEL), 0.02),
    }


def reference(x, emb_ln_g, emb_ln_b, w_in, b_gate, na_rpb, sw_sink, w_branch_na, w_branch_sw,
              w_out, ln1_g, ln1_b, ffn_w_gate, ffn_w_up, ffn_w_down, moe_router, moe_w_gate,
              moe_w_up, moe_w_down, ln2_g, ln2_b):
    x = layer_norm(x, emb_ln_g, emb_ln_b)
    for layer in range(DEPTH):
        m = mixer_block(x, w_in[layer], b_gate[layer], na_rpb[layer], sw_sink[layer],
                        w_branch_na[layer], w_branch_sw[layer], w_out[layer])
        x = layer_norm(DEEPNORM_ALPHA * x + m, ln1_g[layer], ln1_b[layer])
        i = layer // 2
        if layer % 2 == 0:
            f = swiglu(x, ffn_w_gate[i], ffn_w_up[i], ffn_w_down[i])
        else:
            f = moe_swiglu(x, moe_router[i], moe_w_gate[i], moe_w_up[i], moe_w_down[i])
        x = layer_norm(DEEPNORM_ALPHA * x + f, ln2_g[layer], ln2_b[layer])
    return x
```

```python
import numpy as np
import ml_dtypes
import concourse.bass as bass
import concourse.mybir as mybir
from concourse.bass_utils import run_bass_kernel_spmd

F32 = mybir.dt.float32
BF16 = mybir.dt.bfloat16
AF = mybir.ActivationFunctionType
ALU = mybir.AluOpType
AX = mybir.AxisListType

NCORES = 8
D = 1024
S = 2048
NT = 16
KC = 8
DEPTH = 2
ALPHA = (2 * DEPTH) ** 0.25
EPS = 1e-5
PROJ = 4352
FF_DENSE = 2816
FF_EXP = 3584
NEXP = 8
NEG = -30000.0
import os
VAR = os.environ.get('KVAR', '')


class Prog:
    ENGS = ("pe", "act", "dve", "pool", "sp")

    def __init__(self, nc):
        self.nc = nc
        self.h = {"pe": nc.tensor, "act": nc.scalar, "dve": nc.vector, "pool": nc.gpsimd, "sp": nc.sync}
        self.q = {e: [] for e in self.ENGS}
        self.sem = {}
        self.cnt = {}
        self.waited = {e: {} for e in self.ENGS}
        self.res = {}
        self.dma_sems = {}
        self._ctx = []
        self.cond = None
        self.regs = {}

    def new_sem(self, name):
        cm = self.nc.semaphore(name)
        s = cm.__enter__()
        self._ctx.append(cm)
        return s

    def start_phase(self, name):
        for e in self.ENGS:
            self.sem[e] = self.new_sem(f"{name}_{e}")
            self.cnt[e] = 0

    def dma_sem(self, key):
        if key not in self.dma_sems:
            self.dma_sems[key] = [self.new_sem(f"dma_{key}"), 0]
        return self.dma_sems[key]

    def _need(self, eng, tok):
        sem, val, src = tok
        w = self.waited[eng]
        k = id(sem)
        if w.get(k, 0) >= val:
            return False
        w[k] = val
        return True

    def op(self, eng, fn, reads=(), writes=(), dma=None):
        writes = list(writes) + [r for r in reads if r.startswith("ps") and r not in writes]
        reads = [r for r in reads if not r.startswith("ps")]
        toks = []
        for r in reads:
            st = self.res.get(r)
            if st and st["w"] is not None:
                toks.append(st["w"])
        for w_ in writes:
            st = self.res.get(w_)
            if st:
                if st["w"] is not None and (st["w"][2] != eng or st["w"][3]):
                    toks.append(st["w"])
                for t in st["r"]:
                    if t[2] != eng or t[3]:
                        toks.append(t)
        waits = []
        for t in toks:
            if self._need(eng, t[:3]):
                waits.append((t[0], t[1]))
        if dma is not None:
            ds = self.dma_sem(dma)
            ds[1] += 16
            tok = (ds[0], ds[1], eng, True)
            sem, inc = ds[0], 16
        else:
            self.cnt[eng] += 1
            tok = (self.sem[eng], self.cnt[eng], eng, False)
            sem, inc = self.sem[eng], 1

        def emit(h, waits=waits, fn=fn, sem=sem, inc=inc):
            for s_, v_ in waits:
                h.wait_ge(s_, v_)
            fn(h).then_inc(sem, inc)

        if self.cond is not None:
            self.cond["q"][eng].append(emit)
            d_ = self.cond["inc"][eng]
            d_[id(sem)] = (sem, d_.get(id(sem), (sem, 0))[1] + inc)
        else:
            self.q[eng].append(emit)
        for r in reads:
            self.res.setdefault(r, {"w": None, "r": []})["r"].append(tok)
        for w_ in writes:
            self.res[w_] = {"w": tok, "r": []}
        return tok

    def get_reg(self, eng, h):
        if eng not in self.regs:
            self.regs[eng] = h.alloc_register(f"nreg_{eng}")
        return self.regs[eng]

    def regload(self, engs, ap, reads):
        for e in engs:
            self.op(e, lambda h, e=e: h.reg_load(self.get_reg(e, h), ap), reads=reads)

    def begin_cond(self, thr):
        import copy
        pre = {id(self.sem[e]): self.cnt[e] for e in self.ENGS}
        for k, (s_, c_) in self.dma_sems.items():
            pre[id(s_)] = c_
        c = {"thr": thr, "q": {e: [] for e in self.ENGS}, "inc": {e: {} for e in self.ENGS},
             "waited": copy.deepcopy(self.waited), "pre": pre, "parent": self.cond}
        self.cond = c

    def end_cond(self):
        c = self.cond
        parent = c["parent"]
        self.cond = parent
        for e in self.ENGS:
            q = c["q"][e]
            if not q:
                continue
            incs = list(c["inc"][e].values())

            def emit(h, q=q, incs=incs, e=e, thr=c["thr"], pre=c["pre"]):
                v = h.snap(self.get_reg(e, h), min_val=0, max_val=4096)
                with h.If(v > thr):
                    for f in q:
                        f(h)
                with h.Else():
                    for (s_, n_) in incs:
                        p_ = pre.get(id(s_), 0)
                        if p_ > 0:
                            h.wait_ge(s_, p_)
                    for (s_, n_) in incs:
                        h.sem_inc(s_, n_)
            if parent is not None:
                parent["q"][e].append(emit)
                d_ = parent["inc"][e]
                for (s_, n_) in incs:
                    d_[id(s_)] = (s_, d_.get(id(s_), (s_, 0))[1] + n_)
            else:
                self.q[e].append(emit)
        self.waited = c["waited"]

    def barrier(self):
        toks = []
        for e in self.ENGS:
            if self.cnt[e] > 0:
                toks.append((self.sem[e], self.cnt[e], e))
        for k, (s_, c_) in self.dma_sems.items():
            if c_ > 0:
                toks.append((s_, c_, "dma"))
        for e in self.ENGS:
            waits = [(t[0], t[1]) for t in toks if t[2] != e and self._need(e, t)]
            if waits:
                def emit(h, waits=waits):
                    for s_, v_ in waits:
                        h.wait_ge(s_, v_)
                self.q[e].append(emit)
        self.res = {}

    def final_wait(self, eng="sp"):
        waits = [(s_, c_) for k, (s_, c_) in self.dma_sems.items() if c_ > 0]

        def emit(h, waits=waits):
            for s_, v_ in waits:
                h.wait_ge(s_, v_)
        self.q[eng].append(emit)

    def flush(self):
        nc = self.nc
        with nc.Block() as block:
            for e, reg in (("sp", block.sync), ("pool", block.gpsimd), ("pe", block.tensor),
                           ("act", block.scalar), ("dve", block.vector)):
                q = self.q[e]
                if not q:
                    continue

                def body(h, q=q):
                    for f in q:
                        f(h)
                reg(body)
        for cm in reversed(self._ctx):
            cm.__exit__(None, None, None)


class StopBuild(Exception):
    pass


class Mem:
    def __init__(self, nc):
        self.nc = nc
        self.base = (nc.sbuf_base + 63) // 64 * 64
        self.top = nc.sbuf_top
        self.n = 0

    def at(self, off, shape, dtype, name=None):
        self.n += 1
        nbytes = int(np.prod(shape[1:])) * (2 if dtype == BF16 else 4)
        assert self.base + off + nbytes <= self.top, (name, off, nbytes, self.top - self.base)
        return self.nc.alloc_sbuf_tensor_at(name or f"t{self.n}", list(shape), dtype, offset=self.base + off)


def build(debug=None, n_layers=DEPTH):
    nc = bass.Bass("TRN2", target_bir_lowering=False)
    P = Prog(nc)
    M = Mem(nc)
    dbg_outs = {}

    x_d = nc.dram_tensor("x", [S, D], F32, kind="ExternalInput").ap()
    lnp_d = nc.dram_tensor("lnp", [5, 2, D], F32, kind="ExternalInput").ap()
    ident_d = nc.dram_tensor("ident", [128, 128], F32, kind="ExternalInput").ap()
    out_d = nc.dram_tensor("out", [S, D], F32, kind="ExternalOutput").ap()
    w_in_d = nc.dram_tensor("w_in", [DEPTH, D, PROJ], F32, kind="ExternalInput").ap()
    nabias_d = nc.dram_tensor("nabias", [DEPTH, 128, 8, 1536], F32, kind="ExternalInput").ap()
    cs_d = nc.dram_tensor("cs", [2, 128, S], F32, kind="ExternalInput").ap()
    cst_d = nc.dram_tensor("cst", [128, 1152], F32, kind="ExternalInput").ap()
    sink_d = nc.dram_tensor("sink", [DEPTH, 8], F32, kind="ExternalInput").ap()
    bgate_d = nc.dram_tensor("bgate", [DEPTH, 128, 16], F32, kind="ExternalInput").ap()
    wbr_d = nc.dram_tensor("wbr", [DEPTH, 2, 512, D], F32, kind="ExternalInput").ap()
    wout_d = nc.dram_tensor("wout", [DEPTH, D, D], F32, kind="ExternalInput").ap()
    fg_d = nc.dram_tensor("ffn_g", [D, FF_DENSE], F32, kind="ExternalInput").ap()
    fu_d = nc.dram_tensor("ffn_u", [D, FF_DENSE], F32, kind="ExternalInput").ap()
    fd_d = nc.dram_tensor("ffn_d", [FF_DENSE, D], F32, kind="ExternalInput").ap()
    mg_d = nc.dram_tensor("moe_g", [NEXP, D, FF_EXP], F32, kind="ExternalInput").ap()
    mu_d = nc.dram_tensor("moe_u", [NEXP, D, FF_EXP], F32, kind="ExternalInput").ap()
    md_d = nc.dram_tensor("moe_d", [NEXP, FF_EXP, D], F32, kind="ExternalInput").ap()
    wr_d = nc.dram_tensor("wr", [D, NEXP], F32, kind="ExternalInput").ap()
    cst2_d = nc.dram_tensor("cst2", [128, 128 + 4 + 512], F32, kind="ExternalInput").ap()

    KB = 1024
    X = M.at(0, [128, NT, D], F32, "X")
    xT = M.at(64 * KB, [128, KC, S], BF16, "xT")
    R_Y = 96 * KB
    R_A = 128 * KB
    R_B = 176 * KB
    R_M = 196 * KB
    gb = M.at(R_B + 12 * KB, [128, 2, D], F32, "gb")
    mo = {"o": R_M}

    def misc(shape, dtype, name):
        nb = int(np.prod(shape[1:])) * (2 if dtype == BF16 else 4)
        t = M.at(mo["o"], shape, dtype, name)
        mo["o"] += (nb + 63) // 64 * 64
        return t
    ident = misc([128, 128], F32, "ident")
    identb = misc([128, 128], BF16, "identb")
    stats = misc([128, 2, 12], F32, "stats")
    mv = misc([128, 2, 8], F32, "mv")
    eps_t = misc([128, 4], F32, "eps")
    esink = misc([128, 8], F32, "esink")
    rden = misc([128, 2, 8], F32, "rden")
    ytmp = misc([128, 2, 512], F32, "ytmp")
    maskPN = misc([128, 2, 512], BF16, "maskPN")
    prot = misc([128, 128], BF16, "prot")
    bgate = misc([128, 16], F32, "bgate")
    wr = misc([128, KC, NEXP], F32, "wr")
    comb = misc([128, NT, NEXP], F32, "comb")
    rt = misc([128, 4, 8], F32, "rt")
    maskt = misc([128, NT, NEXP], F32, "maskt")
    rank = misc([128, NT, NEXP], F32, "rank")
    rankm = misc([128, NT, NEXP], F32, "rankm")
    offs = misc([128, NEXP], F32, "offs")
    rsh = misc([128, NT], F32, "rsh")
    cntf = misc([128, NEXP], F32, "cntf")
    cnti = misc([128, NEXP], mybir.dt.int32, "cnti")
    slotid = misc([128, 4], F32, "slotid")
    negslot = misc([128, 4], F32, "negslot")
    ltri = misc([128, 128], F32, "ltri")
    ones = misc([128, 128], F32, "ones")
    Rb = misc([128, 128], F32, "Rb")

    PS = nc.alloc_psum_tensor("PS", [128, 8, 512], F32)
    ps = [PS[:, i, :] for i in range(8)]

    def dbg(name, src_ap, shape, dtype, reads):
        t = nc.dram_tensor(name, list(shape), dtype, kind="ExternalOutput").ap()
        dbg_outs[name] = (shape, dtype)
        P.op("sp", lambda h: h.dma_start(out=t, in_=src_ap), reads=reads, dma="dbg")

    def load_gb(idx):
        src = lnp_d[idx].partition_broadcast(128) if hasattr(lnp_d[idx], "partition_broadcast") else None
        P.op("sp", lambda h: h.dma_start(out=gb[:], in_=src), writes=["gb"], dma="gb")

    def ln_tile(t, tp_banks, evac_eng, router=None, do_T=True, xtok=None):
        ln_A(t)
        ln_B(t, tp_banks, evac_eng, router, do_T, xtok)

    def ln_A(t):
        xt = X[:, t, :]
        sl = t % 2
        st = stats[:, sl, :]
        m = mv[:, sl, :]
        rx = f"X{t}"
        rs = f"st{sl}"
        P.op("dve", lambda h: (h.bn_stats(st[:, 0:6], xt[:, 0:512]), h.bn_stats(st[:, 6:12], xt[:, 512:1024]))[1],
             reads=[rx], writes=[rs])
        P.op("dve", lambda h: h.bn_aggr(m[:, 0:2], st[:, 0:12]), reads=[rs], writes=[rs + "m"])
        P.op("act", lambda h: h.activation(m[:, 2:3], m[:, 1:2], AF.Sqrt, bias=eps_t[:, 0:1], scale=1.0),
             reads=[rs + "m"], writes=[rs + "s"])
        P.op("dve", lambda h: h.reciprocal(m[:, 3:4], m[:, 2:3]), reads=[rs + "s"], writes=[rs + "r"])
        P.op("dve", lambda h: h.scalar_tensor_tensor(m[:, 4:5], m[:, 0:1], -1.0, m[:, 3:4], ALU.mult, ALU.mult),
             reads=[rs + "r", rs + "m"], writes=[rs + "n"])


    def ln_B(t, tp_banks, evac_eng, router=None, do_T=True, xtok=None):
        xt = X[:, t, :]
        sl = t % 2
        m = mv[:, sl, :]
        rx = f"X{t}"
        rs = f"st{sl}"
        P.op("act", lambda h: h.activation(xt, xt, AF.Identity, bias=m[:, 4:5], scale=m[:, 3:4]),
             reads=[rx, rs + "n", rs + "r"], writes=[rx])
        gbe = "pool" if "poolgb" in VAR else "dve"
        gbe0 = "pool" if "poolg" in VAR else "dve"
        P.op(gbe0, lambda h: h.tensor_tensor(xt, xt, gb[:, 0, :], ALU.mult), reads=[rx, "gb"], writes=[rx])
        P.op(gbe, lambda h: h.tensor_tensor(xt, xt, gb[:, 1, :], ALU.add), reads=[rx, "gb"], writes=[rx])
        if xtok is not None:
            P.op("act", lambda h: h.activation(xtok[:, t, :], xt, AF.Copy), reads=[rx], writes=[f"xtok{t}"])
        if not do_T and router is None:
            return
        b0, b1 = tp_banks

        def tp(h):
            last = None
            for c in range(KC):
                bank = ps[b0] if c < 4 else ps[b1]
                last = h.transpose(bank[:, (c % 4) * 128:(c % 4 + 1) * 128], xt[:, c * 128:(c + 1) * 128], ident[:])
            return last
        P.op("pe", tp, reads=[rx, "ident"], writes=[f"ps{b0}", f"ps{b1}"])
        for half, b in ((0, b0), (1, b1)):
            dst = xT[:, half * 4:(half + 1) * 4, t * 128:(t + 1) * 128]
            src = ps[b].rearrange("p (c n) -> p c n", c=4)
            if not do_T:
                pass
            elif evac_eng == "act":
                P.op("act", lambda h, dst=dst, src=src: h.activation(dst, src, AF.Copy),
                     reads=[f"ps{b}"], writes=[f"xT{t}"])
            else:
                P.op("dve", lambda h, dst=dst, src=src: h.tensor_copy(dst, src),
                     reads=[f"ps{b}"], writes=[f"xT{t}"])
            if router is not None:
                x32 = router
                o_eng = "dve" if evac_eng == "act" else "act"
                d32 = x32[:, half * 4:(half + 1) * 4, :]
                if o_eng == "act":
                    P.op("act", lambda h, d32=d32, src=src: h.activation(d32, src, AF.Copy),
                         reads=[f"ps{b}"], writes=[f"x32_{half}"])
                else:
                    P.op("dve", lambda h, d32=d32, src=src: h.tensor_copy(d32, src),
                         reads=[f"ps{b}"], writes=[f"x32_{half}"])
        if router is not None:
            x32 = router
            lg = ps[b0][:, 0:NEXP]

            def rmm(h):
                last = None
                for k in range(KC):
                    last = h.matmul(lg, x32[:, k, :], wr[:, k, :], start=(k == 0), stop=(k == KC - 1))
                return last
            P.op("pe", rmm, reads=["x32_0", "x32_1", "wr"], writes=[f"ps{b0}"])
            P.op("dve", lambda h: h.tensor_copy(rank[:, t, :], lg), reads=[f"ps{b0}"], writes=["lgall"])

    def router_finish():
        L = rank[:]
        A = rankm[:]
        B = Rb[:].rearrange("p (t e) -> p t e", e=NEXP)
        rtf = rt[:].rearrange("p a e -> p (a e)")
        m1, m2, den = rsh[:], rtf[:, 0:16], rtf[:, 16:32]
        bc = lambda v: v.unsqueeze(2).to_broadcast([128, NT, NEXP])
        P.op("dve", lambda h: h.tensor_reduce(m1, L, AX.X, ALU.max), reads=["lgall"], writes=["r_m1"])
        P.op("dve", lambda h: h.tensor_tensor(A, L, bc(m1), ALU.is_equal), reads=["lgall", "r_m1"], writes=["r_A"])
        P.op("dve", lambda h: h.scalar_tensor_tensor(B, A, -1e30, L, ALU.mult, ALU.add), reads=["r_A", "lgall"], writes=["r_B"])
        P.op("dve", lambda h: h.tensor_reduce(m2, B, AX.X, ALU.max), reads=["r_B"], writes=["r_m2"])
        P.op("dve", lambda h: h.tensor_tensor(maskt[:], L, bc(m2), ALU.is_ge), reads=["lgall", "r_m2"], writes=["maskt"])
        P.op("dve", lambda h: h.tensor_tensor(A, L, bc(m1), ALU.subtract), reads=["lgall", "r_m1", "r_A"], writes=["r_A"])
        P.op("act", lambda h: h.activation(B, A, AF.Exp), reads=["r_A", "r_B"], writes=["r_B"])
        P.op("dve", lambda h: h.tensor_tensor(A, maskt[:], B, ALU.mult), reads=["maskt", "r_B", "r_A"], writes=["r_A"])
        P.op("dve", lambda h: h.tensor_reduce(den, A, AX.X, ALU.add), reads=["r_A"], writes=["r_den"])
        P.op("dve", lambda h: h.reciprocal(den, den), reads=["r_den"], writes=["r_den"])
        P.op("dve", lambda h: h.tensor_tensor(comb[:], A, bc(den), ALU.mult), reads=["r_A", "r_den"], writes=["comb"])

    wq = {"n": 0}

    def wload(dst, src, key, writes):
        P.op("pool", lambda h: h.dma_start(out=dst, in_=src, max_dma_last_dim=2048), writes=writes, dma=key)

    def gemm_fm(w_slot, wcol0, dst_fn, evac_fn, banks, wres, xres_fn, tag):
        for tb in range(4):
            b = banks[tb % len(banks)]

            def mm(h, tb=tb, b=b):
                last = None
                for k in range(KC):
                    last = h.matmul(ps[b], w_slot[:, k, wcol0:wcol0 + 128], xT[:, k, tb * 512:(tb + 1) * 512],
                                    start=(k == 0), stop=(k == KC - 1))
                return last
            P.op("pe", mm, reads=[wres] + xres_fn(tb), writes=[f"ps{b}"])
            evac_fn(tb, b)

    def xT_res(tb):
        return [f"xT{t}" for t in range(tb * 4, tb * 4 + 4)]

    P.start_phase("p0")
    P.op("pool", lambda h: h.memset(eps_t[:], EPS), writes=["eps"])
    P.op("sp", lambda h: h.dma_start(out=ident[:], in_=ident_d), writes=["ident"], dma="c_ident")
    if debug != "p0x" or "a" in VAR:
        wload(identb[:], ident_d, "c_identb", ["identb"])
    load_gb(0)
    for t in range(NT):
        P.op("sp", lambda h, t=t: h.dma_start(out=X[:, t, :], in_=x_d[t * 128:(t + 1) * 128, :]),
             writes=[f"X{t}"], dma=f"x{t}")
    P.res.setdefault("st0m", {"w": None, "r": []})
    P.res["st0m"] = {"w": P.res["eps"]["w"], "r": []}
    P.res["st1m"] = {"w": P.res["eps"]["w"], "r": []}
    ln_A(0)
    for t in range(NT):
        if t + 1 < NT:
            ln_A(t + 1)
        ln_B(t, (0, 1) if t % 2 == 0 else (2, 3), "act" if t % 2 == 0 else "dve")
    P.barrier()

    if debug in ("p0", "p0x"):
        dbg("dbg_X", X[:], [128, NT, D], F32, reads=[])
        dbg("dbg_xT", xT[:], [128, KC, S], BF16, reads=[])
        n_layers = 0

    yT = [M.at(R_Y, [128, 4, S], BF16, "yTna"), M.at(R_Y + 16 * KB, [128, 4, S], BF16, "yTsw")]

    def moe_routed(layer):
        xtok = M.at(64 * KB, [128, NT, D], BF16, "xtok")
        wst = [dict(g=M.at(R_A + s_ * 24 * KB, [128, KC, 512], BF16, f"wg{s_}"),
                    u=M.at(R_A + s_ * 24 * KB + 8 * KB, [128, KC, 512], BF16, f"wu{s_}"),
                    d=M.at(R_A + s_ * 24 * KB + 16 * KB, [128, 4, D], BF16, f"wd{s_}")) for s_ in range(2)]
        xg = M.at(R_Y, [128, KC, 512], BF16, "xg")
        yacc = M.at(R_Y + 8 * KB, [128, 4, D], F32, "yacc")
        hTs = [M.at(R_Y + 24 * KB + s_ * 4 * KB, [128, 4, 512], BF16, f"hTs{s_}") for s_ in range(2)]
        yslot = M.at(R_B, [128, 4, D], BF16, "yslot")
        selt = [M.at(R_B + 8 * KB + i * KB, [128, 512], BF16, f"selt{i}") for i in range(2)]
        iota512 = M.at(R_B + 10 * KB, [128, 512], F32, "iota512")
        selT = [maskPN[:, i, :].rearrange("p (s n) -> p s n", s=4) for i in range(2)]

        P.op("sp", lambda h: h.dma_start(out=ltri[:], in_=cst2_d[:, 0:128]), writes=["ltri"], dma="c_ltri")
        P.op("sp", lambda h: h.dma_start(out=slotid[:], in_=cst2_d[:, 128:132]), writes=["slotid"], dma="c_slot")
        P.op("sp", lambda h: h.dma_start(out=iota512[:], in_=cst2_d[:, 132:644]), writes=["iota"], dma="c_iota")
        P.op("pool", lambda h: h.memset(ones[:], 1.0), writes=["ones"])
        P.op("dve", lambda h: h.tensor_scalar(negslot[:], slotid[:], -1.0, None, ALU.mult), reads=["slotid"], writes=["negslot"])
        one_t = ones[:, 0:1]
        mflat = maskt[:].rearrange("p t e -> p (t e)")
        P.op("pe", lambda h: h.matmul(ps[0][:, 0:128], ones[:], mflat, start=True, stop=True),
             reads=["ones", "maskt"], writes=["ps0"])
        P.op("pe", lambda h: h.matmul(ps[1][:, 0:128], ltri[:], mflat, start=True, stop=True),
             reads=["ltri", "maskt"], writes=["ps1"])
        cps = ps[0][:, 0:128].rearrange("p (t e) -> p t e", e=NEXP)
        wps = ps[1][:, 0:128].rearrange("p (t e) -> p t e", e=NEXP)
        P.op("dve", lambda h: h.memset(offs[:], 0.0), writes=["offs"])
        for t in range(NT):
            P.op("dve", lambda h, t=t: h.tensor_tensor(rank[:, t, :], wps[:, t, :], offs[:], ALU.add),
                 reads=["ps1", "offs"], writes=["rank"])
            P.op("dve", lambda h, t=t: h.tensor_tensor(offs[:], cps[:, t, :], offs[:], ALU.add),
                 reads=["ps0", "offs", "rank"], writes=["offs"])
        P.op("dve", lambda h: h.tensor_copy(cnti[:], offs[:]), reads=["offs"], writes=["cnti"])
        rkf = rank[:].rearrange("p t e -> p (t e)")
        rmf = rankm[:].rearrange("p t e -> p (t e)")
        P.op("dve", lambda h: h.scalar_tensor_tensor(rmf, rkf, 1.0, mflat, ALU.add, ALU.mult),
             reads=["rank", "maskt"], writes=["rankm"])
        P.op("dve", lambda h: h.tensor_scalar(rmf, rmf, -1.0, None, ALU.add), reads=["rankm"], writes=["rankm"])
        if debug == "route":
            dbg("dbg_rankm", rankm[:], [128, NT, NEXP], F32, reads=["rankm"])
            dbg("dbg_comb", comb[:], [128, NT, NEXP], F32, reads=["comb"])
            dbg("dbg_cnti", cnti[:], [128, NEXP], mybir.dt.int32, reads=["cnti"])
            raise StopBuild()

        nblk = FF_EXP // 512
        cnt = {"h": 0, "d": 0, "w": 0}

        def load_w(e, fb):
            s_ = cnt["w"] % 2
            cnt["w"] += 1
            gv = mg_d[e].rearrange("(c p) n -> p c n", p=128)
            uv = mu_d[e].rearrange("(c p) n -> p c n", p=128)
            dv = md_d[e][fb * 512:(fb + 1) * 512, :].rearrange("(c p) n -> p c n", p=128)
            wload(wst[s_]["g"][:], gv[:, :, fb * 512:(fb + 1) * 512], f"wg{s_}", [f"wg{s_}"])
            wload(wst[s_]["u"][:], uv[:, :, fb * 512:(fb + 1) * 512], f"wu{s_}", [f"wu{s_}"])
            for hf_ in range(2):
                wload(wst[s_]["d"][:, :, hf_ * 512:(hf_ + 1) * 512], dv[:, :, hf_ * 512:(hf_ + 1) * 512], f"wd{s_}", [f"wd{s_}"])
            return s_

        def hidden(s_, hs, nsl):
            for fc in range(4):
                n = cnt["h"]
                cnt["h"] += 1
                bg, bu = (0, 1) if n % 2 == 0 else (2, 3)

                def mm(h, fc=fc, bg=bg, bu=bu):
                    last = None
                    for k in range(KC):
                        last = h.matmul(ps[bg][:, 0:nsl], wst[s_]["g"][:, k, fc * 128:(fc + 1) * 128], xg[:, k, 0:nsl],
                                        start=(k == 0), stop=(k == KC - 1))
                    for k in range(KC):
                        last = h.matmul(ps[bu][:, 0:nsl], wst[s_]["u"][:, k, fc * 128:(fc + 1) * 128], xg[:, k, 0:nsl],
                                        start=(k == 0), stop=(k == KC - 1))
                    return last
                P.op("pe", mm, reads=[f"wg{s_}", f"wu{s_}"] + [f"xg{c}" for c in range(KC)], writes=[f"ps{bg}", f"ps{bu}"])
                tmp = ytmp[:, n % 2, 0:nsl]
                P.op("act", lambda h, tmp=tmp, bg=bg: h.activation(tmp, ps[bg][:, 0:nsl], AF.Silu), reads=[f"ps{bg}"], writes=[f"sg{n % 2}"])
                P.op("dve", lambda h, tmp=tmp, bu=bu, fc=fc: h.tensor_tensor(hTs[hs][:, fc, 0:nsl], ps[bu][:, 0:nsl], tmp, ALU.mult),
                     reads=[f"ps{bu}", f"sg{n % 2}"], writes=[f"hTs{hs}"])

        def down(s_, hs, first, nst):
            for st_ in range(nst):
                for half in range(2):
                    n = cnt["d"]
                    cnt["d"] += 1
                    b_ = 4 + n % 4

                    def mm(h, st_=st_, half=half, b_=b_):
                        last = None
                        for fc in range(4):
                            last = h.matmul(ps[b_], hTs[hs][:, fc, st_ * 128:(st_ + 1) * 128], wst[s_]["d"][:, fc, half * 512:(half + 1) * 512],
                                            start=(fc == 0), stop=(fc == 3))
                        return last
                    P.op("pe", mm, reads=[f"wd{s_}", f"hTs{hs}"], writes=[f"ps{b_}"])
                    ya = yacc[:, st_, half * 512:(half + 1) * 512]
                    if first:
                        P.op("dve", lambda h, ya=ya, b_=b_: h.tensor_copy(ya, ps[b_]), reads=[f"ps{b_}"], writes=[f"yacc{st_}"])
                    else:
                        P.op("dve", lambda h, ya=ya, b_=b_: h.tensor_tensor(ya, ps[b_], ya, ALU.add),
                             reads=[f"ps{b_}", f"yacc{st_}"], writes=[f"yacc{st_}"])

        def block(e, off, nsl):
            nst = nsl // 128
            P.op("dve", lambda h: h.tensor_scalar(rsh[:], rankm[:, :, e], float(-off), None, ALU.add),
                 reads=["rankm"], writes=["rsh"])
            s0 = load_w(e, 0)
            for t in range(NT):
                sl_ = selt[t % 2]
                P.op("dve", lambda h, sl_=sl_, t=t: h.tensor_scalar(sl_[:, 0:nsl], iota512[:, 0:nsl], rsh[:, t:t + 1], None, ALU.is_equal),
                     reads=["iota", "rsh"], writes=[f"selt{t % 2}"])

                def mm(h, sl_=sl_, t=t):
                    last = None
                    for c in range(KC):
                        last = h.matmul(ps[c][:, 0:nsl], xtok[:, t, c * 128:(c + 1) * 128], sl_[:, 0:nsl], start=(t == 0), stop=(t == NT - 1))
                    return last
                P.op("pe", mm, reads=[f"selt{t % 2}", f"xtok{t}"], writes=[f"ps{c}" for c in range(KC)])
            for c in range(KC):
                if c % 2 == 0:
                    P.op("act", lambda h, c=c: h.activation(xg[:, c, 0:nsl], ps[c][:, 0:nsl], AF.Copy), reads=[f"ps{c}"], writes=[f"xg{c}"])
                else:
                    P.op("dve", lambda h, c=c: h.tensor_copy(xg[:, c, 0:nsl], ps[c][:, 0:nsl]), reads=[f"ps{c}"], writes=[f"xg{c}"])
            stages = [s0]
            hidden(s0, 0, nsl)
            for fb in range(1, nblk):
                stages.append(load_w(e, fb))
                hidden(stages[fb], fb % 2, nsl)
                down(stages[fb - 1], (fb - 1) % 2, fb - 1 == 0, nst)
            down(stages[nblk - 1], (nblk - 1) % 2, False, nst)
            for st_ in range(nst):
                if st_ % 2 == 0:
                    P.op("act", lambda h, st_=st_: h.activation(yslot[:, st_, :], yacc[:, st_, :], AF.Copy),
                         reads=[f"yacc{st_}"], writes=[f"yslot{st_}"])
                else:
                    P.op("pool", lambda h, st_=st_: h.tensor_copy(yslot[:, st_, :], yacc[:, st_, :]),
                         reads=[f"yacc{st_}"], writes=[f"yslot{st_}"])
            def sc_stage1(t):
                rb_ = 4 + t % 2
                P.op("dve", lambda h, t=t: h.tensor_scalar(Rb[:], ones[:], rsh[:, t:t + 1], None, ALU.mult),
                     reads=["ones", "rsh"], writes=["Rb"])
                P.op("pe", lambda h, rb_=rb_: h.matmul(ps[rb_][:, 0:128], Rb[:], ident[:], start=True, stop=True),
                     reads=["Rb", "ident"], writes=[f"ps{rb_}"])
                sT = selT[t % 2]

                if "dvemk" in VAR:
                    def mk(h, sT=sT, rb_=rb_):
                        last = None
                        for st_ in range(nst):
                            last = h.tensor_scalar(sT[:, st_, :], ps[rb_][:, 0:128], slotid[:, st_:st_ + 1], None, ALU.is_equal)
                        return last
                    P.op("dve", mk, reads=[f"ps{rb_}", "slotid"], writes=[f"selT{t % 2}"])
                else:
                    ta = ytmp[:, t % 2, :].rearrange("p (s n) -> p s n", s=4)

                    def mk1(h, rb_=rb_, ta=ta):
                        last = None
                        for st_ in range(nst):
                            last = h.activation(ta[:, st_, :], ps[rb_][:, 0:128], AF.Abs, bias=negslot[:, st_:st_ + 1], scale=1.0)
                        return last

                    def mk2(h, sT=sT, ta=ta):
                        last = None
                        for st_ in range(nst):
                            last = h.activation(sT[:, st_, :], ta[:, st_, :], AF.Relu, bias=one_t, scale=-1.0)
                        return last
                    P.op("act", mk1, reads=[f"ps{rb_}", "negslot"], writes=[f"sg{t % 2}"])
                    P.op("act", mk2, reads=[f"sg{t % 2}", "ones"], writes=[f"selT{t % 2}"])

            def sc_stage2(t):
                sT = selT[t % 2]
                for half in range(2):
                    ob_ = 6 + half

                    def mm(h, sT=sT, half=half, ob_=ob_):
                        last = None
                        for st_ in range(nst):
                            last = h.matmul(ps[ob_], sT[:, st_, :], yslot[:, st_, half * 512:(half + 1) * 512],
                                            start=(st_ == 0), stop=(st_ == nst - 1))
                        return last
                    P.op("pe", mm, reads=[f"selT{t % 2}"] + [f"yslot{i}" for i in range(nst)], writes=[f"ps{ob_}"])
                    xs = X[:, t, half * 512:(half + 1) * 512]
                    P.op("dve", lambda h, xs=xs, ob_=ob_, t=t: h.scalar_tensor_tensor(xs, ps[ob_], comb[:, t, e:e + 1], xs, ALU.mult, ALU.add),
                         reads=[f"ps{ob_}", f"X{t}a", "comb"], writes=[f"X{t}a"])

            sc_stage1(0)
            for t in range(NT):
                if t + 1 < NT:
                    sc_stage1(t + 1)
                sc_stage2(t)

        BLOCKS = [(0, 512), (512, 128), (640, 384), (1024, 512), (1536, 512)]
        for e in range(NEXP):
            P.regload(("pe", "act", "dve", "pool"), cnti[0:1, e:e + 1], reads=["cnti"])
            ncond = 0
            for (off, nsl) in BLOCKS:
                if off > 0:
                    P.begin_cond(off)
                    ncond += 1
                block(e, off, nsl)
            for _ in range(ncond):
                P.end_cond()

    def do_layer(layer):
        P.start_phase(f"l{layer}p1")
        wv = w_in_d[layer].rearrange("(c p) n -> p c n", p=128)
        ws = [M.at(R_B, [128, KC, 512], BF16, "ws0"), M.at(R_B + 8 * KB, [128, KC, 512], BF16, "ws1")]
        PT = [M.at(R_B + 16 * KB + i * 1280, [128, 640], BF16, f"PT{i}") for i in range(3)]
        qT = M.at(R_A, [128, 4, S], BF16, "qT")
        kT = M.at(R_A + 16 * KB, [128, 2, S], BF16, "kT")
        wload(prot[:], cst_d[:, 0:128], "c_prot", ["prot"])
        wload(maskPN[:], cst_d[:, 128:1152].rearrange("p (a n) -> p a n", a=2), "c_mask", ["maskPN"])
        P.op("sp", lambda h: h.dma_start(out=esink[:], in_=sink_d[layer].partition_broadcast(128)),
             writes=["esink"], dma="c_esink")
        P.op("act", lambda h: h.activation(esink[:], esink[:], AF.Exp), reads=["esink"], writes=["esink"])

        def do_group(grp):
            is_na = grp < 2
            base = grp * 768
            nh = 4 if is_na else 2
            if is_na:
                Va = M.at(R_A + 24 * KB, [128, NT, 4, 65], BF16, "VaNA")
                bias = M.at(R_A + 33 * KB, [128, 4, 1536], BF16, "nabias")
                wload(bias[:], nabias_d[layer][:, grp * 4:(grp + 1) * 4, :], "bias", ["bias"])
            else:
                Va = M.at(R_A + 20 * KB, [128, NT, 2, 65], BF16, "VaSW")
                cs = M.at(R_A + 24 * KB + 512, [128, 2, S], F32, "cs")
                rtmp = M.at(R_A + 41 * KB, [128, 2, 512], F32, "rtmp")
                qb = M.at(R_A + 45 * KB, [128, 512], BF16, "qb")
                P.op("sp", lambda h: h.dma_start(out=cs[:], in_=cs_d.rearrange("a p n -> p a n")),
                     writes=["cs"], dma="c_cs")
            P.op("pool", lambda h, Va=Va: h.memset(Va[:, :, :, 64:65], 1.0), writes=["vones"])
            wload(ws[0][:], wv[:, :, base:base + 512], "ws0", ["ws0"])
            wload(ws[1][:, :, 0:256], wv[:, :, base + 512:base + 768], "ws1", ["ws1"])

            def evac_plain(dst, scale, wres):
                def f(tb, b):
                    d = dst[:, tb * 512:(tb + 1) * 512]
                    if tb % 2 == 0:
                        P.op("act", lambda h: h.activation(d, ps[b], AF.Copy, scale=scale),
                             reads=[f"ps{b}"], writes=[wres + str(tb)])
                    else:
                        P.op("dve", lambda h: h.tensor_scalar(d, ps[b], scale, None, ALU.mult),
                             reads=[f"ps{b}"], writes=[wres + str(tb)])
                return f

            def evac_rot(dst, scale, wres):
                def f(tb, b):
                    d = dst[:, tb * 512:(tb + 1) * 512]
                    P.op("act", lambda h: h.activation(qb[:], ps[b], AF.Copy), reads=[f"ps{b}"], writes=["qb"])
                    P.op("pe", lambda h: h.matmul(ps[5], prot[:], qb[:], start=True, stop=True),
                         reads=["qb", "prot"], writes=["ps5"])
                    P.op("dve", lambda h: h.scalar_tensor_tensor(rtmp[:, 0, :], ps[b], scale, cs[:, 0, tb * 512:(tb + 1) * 512],
                                                                 ALU.mult, ALU.mult),
                         reads=[f"ps{b}", "cs"], writes=["rtmp0"])
                    P.op("dve", lambda h: h.scalar_tensor_tensor(rtmp[:, 1, :], ps[5], scale, cs[:, 1, tb * 512:(tb + 1) * 512],
                                                                 ALU.mult, ALU.mult),
                         reads=["ps5", "cs"], writes=["rtmp1"])
                    P.op("pool", lambda h: h.tensor_tensor(d, rtmp[:, 0, :], rtmp[:, 1, :], ALU.add),
                         reads=["rtmp0", "rtmp1"], writes=[wres + str(tb)])
                return f

            if is_na:
                for c in range(2):
                    gemm_fm(ws[0], c * 128, None, evac_plain(qT[:, c, :], 0.125, f"q{c}_"), (6, 7), "ws0", xT_res, "q")
                for c in range(2):
                    gemm_fm(ws[0], 256 + c * 128, None, evac_plain(kT[:, c, :], 1.0, f"k{c}_"), (6, 7), "ws0", xT_res, "k")
            else:
                ev = evac_plain if "norot" in VAR else evac_rot
                for c in range(4):
                    gemm_fm(ws[0], c * 128, None, ev(qT[:, c, :], 0.125, f"q{c}_"), (6, 7), "ws0", xT_res, "q")
                gemm_fm(ws[1], 0, None, ev(kT[:, 0, :], 1.0, "k0_"), (6, 7), "ws1", xT_res, "k")
            vcol0 = 0 if is_na else 128
            vn = nh * 64
            for t in range(NT):
                b = 6 + t % 2

                def mmv(h, t=t, b=b):
                    last = None
                    for k in range(KC):
                        last = h.matmul(ps[b][:, 0:vn], xT[:, k, t * 128:(t + 1) * 128], ws[1][:, k, vcol0:vcol0 + vn],
                                        start=(k == 0), stop=(k == KC - 1))
                    return last
                P.op("pe", mmv, reads=["ws1", f"xT{t}"], writes=[f"ps{b}"])
                dstv = Va[:, t, :, 0:64]
                srcv = ps[b][:, 0:vn].rearrange("p (h d) -> p h d", h=nh)
                if t % 2 == 0:
                    P.op("act", lambda h, dstv=dstv, srcv=srcv: h.activation(dstv, srcv, AF.Copy),
                         reads=[f"ps{b}", "vones"], writes=[f"v{t}"])
                else:
                    P.op("dve", lambda h, dstv=dstv, srcv=srcv: h.tensor_copy(dstv, srcv),
                         reads=[f"ps{b}", "vones"], writes=[f"v{t}"])

            if debug == f"p1proj{grp}" and layer == 0:
                dbg("dbg_q", qT[:], [128, 4, S], BF16, reads=[f"q{c}_{tb}" for c in range(4) for tb in range(4)])
                dbg("dbg_k", kT[:], [128, 2, S], BF16, reads=[f"k{c}_{tb}" for c in range(2) for tb in range(4)])
                dbg("dbg_v", Va[:], [128, NT, nh, 65], BF16, reads=[f"v{t}" for t in range(NT)])
                raise StopBuild()

            steps = []
            if is_na:
                for i in range(NT):
                    if i < 2:
                        js, unm = list(range(3, -1, -1)), True
                    elif i >= 14:
                        js, unm = list(range(15, 11, -1)), True
                    else:
                        js, unm = list(range(i + 2, i - 3, -1)), False
                    for hh in range(4):
                        steps.append((i, hh, js, unm))
            else:
                for n in range(NT):
                    for kv in range(2):
                        js = [j for j in (n - 1, n, n + 1) if 0 <= j < NT]
                        for j in js:
                            steps.append((n, kv, [j], j - n))
            nsteps = len(steps)

            def emit_S(si):
                sb = (0, 2)[si % 2]
                if is_na:
                    i, hh, js, unm = steps[si]
                    c, po = hh // 2, (hh % 2) * 64

                    def f(h):
                        last = None
                        for jj, j in enumerate(js):
                            o = PS[:, sb + jj // 4, (jj % 4) * 128:(jj % 4 + 1) * 128]
                            h.matmul(o, kT[po:po + 64, c, j * 128:(j + 1) * 128], qT[po:po + 64, c, i * 128:(i + 1) * 128],
                                     start=(jj % 4 == 0), stop=False, skip_group_check=True)
                        dl0 = js[0] - i
                        col0 = (640 + 64 * (6 - 2 * dl0)) if unm else (64 * (4 - 2 * dl0))
                        n0 = min(len(js), 4) * 128
                        last = h.matmul(PS[:, sb, 0:n0], identb[:], bias[:, hh, col0:col0 + n0], start=False, stop=True,
                                        skip_group_check=True)
                        if len(js) > 4:
                            last = h.matmul(PS[:, sb + 1, 0:128], identb[:], bias[:, hh, col0 + 512:col0 + 640], start=False, stop=True,
                                            skip_group_check=True)
                        return last
                    rd = ["bias", "identb", f"q{c}_{i // 4}"] + [f"k{c}_{j // 4}" for j in js]
                    P.op("pe", f, reads=rd, writes=[f"ps{sb}", f"ps{sb + 1}"])
                else:
                    n, kv, js, rel = steps[si]
                    j = js[0]
                    po = kv * 64

                    def f(h):
                        o = ps[sb]
                        last = h.matmul(o, kT[po:po + 64, 0, j * 128:(j + 1) * 128], qT[po:po + 64, :, n * 128:(n + 1) * 128],
                                        start=True, stop=(rel == 0))
                        if rel != 0:
                            last = h.matmul(o, identb[:], maskPN[:, 0 if rel < 0 else 1, :], start=False, stop=True)
                        return last
                    rd = ["maskPN", "identb", f"k0_{j // 4}"] + [f"q{c}_{n // 4}" for c in range(4)]
                    P.op("pe", f, reads=rd, writes=[f"ps{sb}"])

            def emit_exp(si):
                sb = (0, 2)[si % 2]
                pt = PT[si % 3]
                if is_na:
                    nj = len(steps[si][2])
                    src = PS[:, sb:sb + 2, :].rearrange("p a n -> p (a n)")[:, 0:nj * 128]
                    P.op("act", lambda h: h.activation(pt[:, 0:nj * 128], src, AF.Exp),
                         reads=[f"ps{sb}", f"ps{sb + 1}"], writes=[f"PT{si % 3}"])
                else:
                    P.op("act", lambda h: h.activation(pt[:, 0:512], ps[sb], AF.Exp),
                         reads=[f"ps{sb}"], writes=[f"PT{si % 3}"])

            def emit_PV(si):
                pt = PT[si % 3]
                if is_na:
                    i, hh, js, unm = steps[si]
                    ob = 4 + i % 2

                    def f(h):
                        last = None
                        for jj, j in enumerate(js):
                            last = h.matmul(ps[ob][:, hh * 65:(hh + 1) * 65], pt[:, jj * 128:(jj + 1) * 128], Va[:, j, hh, :],
                                            start=(jj == 0), stop=(jj == len(js) - 1), skip_group_check=True)
                        return last
                    P.op("pe", f, reads=[f"PT{si % 3}"] + [f"v{j}" for j in js], writes=[f"ps{ob}"])
                    if hh == 3:
                        emit_norm(i, [(ob, 4)])
                else:
                    n, kv, js, rel = steps[si]
                    j = js[0]
                    obs = (4, 5) if n % 2 == 0 else (1, 3)
                    ob = obs[kv]
                    first = (j == max(0, n - 1))
                    lastj = (j == min(NT - 1, n + 1))

                    def f(h):
                        last = None
                        for g in range(4):
                            last = h.matmul(ps[ob][:, g * 65:(g + 1) * 65], pt[:, g * 128:(g + 1) * 128], Va[:, j, kv, :],
                                            start=(first and g == 0), stop=lastj, skip_group_check=True)
                        return last
                    P.op("pe", f, reads=[f"PT{si % 3}", f"v{j}"], writes=[f"ps{ob}"])
                    if kv == 1 and lastj:
                        emit_norm(n, [(obs[0], 4), (obs[1], 4)])

            def emit_norm(i, pieces):
                sl = i % 2
                nheads = sum(p_[1] for p_ in pieces)
                h0 = 0
                for (ob, nh_) in pieces:
                    o3 = ps[ob][:, 0:nh_ * 65].rearrange("p (h d) -> p h d", d=65)
                    rd_ = rden[:, sl, h0:h0 + nh_]
                    if is_na:
                        P.op("dve", lambda h, rd_=rd_, o3=o3: h.reciprocal(rd_, o3[:, :, 64]),
                             reads=[f"ps{ob}"], writes=[f"rden{sl}_{h0}"])
                    else:
                        P.op("dve", lambda h, rd_=rd_, o3=o3, h0=h0, nh_=nh_: h.tensor_tensor(rd_, o3[:, :, 64], esink[:, h0:h0 + nh_], ALU.add),
                             reads=[f"ps{ob}", "esink"], writes=[f"rden{sl}_{h0}"])
                        P.op("dve", lambda h, rd_=rd_: h.reciprocal(rd_, rd_), reads=[f"rden{sl}_{h0}"], writes=[f"rden{sl}_{h0}"])

                    def fn(h, o3=o3, h0=h0, nh_=nh_):
                        last = None
                        for hd in range(nh_):
                            last = h.tensor_scalar(ytmp[:, sl, (h0 + hd) * 64:(h0 + hd + 1) * 64], o3[:, hd, 0:64],
                                                   rden[:, sl, h0 + hd:h0 + hd + 1], None, ALU.mult)
                        return last
                    P.op("dve", fn, reads=[f"ps{ob}", f"rden{sl}_{h0}"], writes=[f"ytmp{sl}_{h0}"])
                    h0 += nh_
                nchunk = nheads // 2
                tb_ = 6 + i % 2
                yt = ytmp[:, sl, :]

                def tp(h):
                    last = None
                    for c in range(nchunk):
                        last = h.transpose(ps[tb_][:, c * 128:(c + 1) * 128], yt[:, c * 128:(c + 1) * 128], ident[:])
                    return last
                P.op("pe", tp, reads=[f"ytmp{sl}_{hh_}" for hh_ in range(0, nheads, 4)] + ["ident"], writes=[f"ps{tb_}"])
                ydst = yT[0 if is_na else 1]
                c0 = grp * 2 if is_na else 0
                dst = ydst[:, c0:c0 + nchunk, i * 128:(i + 1) * 128]
                srcp = ps[tb_][:, 0:nchunk * 128].rearrange("p (c n) -> p c n", c=nchunk)
                P.op("act", lambda h: h.activation(dst, srcp, AF.Copy), reads=[f"ps{tb_}"], writes=[f"yT{i}"])

            if debug and debug.startswith("att"):
                _, g_, n_, what = debug.split("_")
                if int(g_) == grp:
                    for si in range(int(n_)):
                        emit_S(si)
                        if "E" in what:
                            emit_exp(si)
                        if "P" in what:
                            emit_PV(si)
                    raise StopBuild()
            emit_S(0)
            emit_exp(0)
            for si in range(nsteps):
                if si + 1 < nsteps:
                    emit_S(si + 1)
                    emit_exp(si + 1)
                emit_PV(si)
        for grp in range(3):
            do_group(grp)
            if grp >= 1:
                P.barrier()
        if debug == "p1" and layer == 0:
            dbg("dbg_yna", yT[0][:], [128, 4, S], BF16, reads=[])
            dbg("dbg_ysw", yT[1][:], [128, 4, S], BF16, reads=[])
            raise StopBuild()

        P.start_phase(f"l{layer}p2")
        zT = M.at(R_A, [128, KC, S], BF16, "zT")
        wo = M.at(R_A + 32 * KB, [128, KC, D], BF16, "wo")
        p2w = [M.at(R_B + i * 6 * KB, [128, 24, 128], BF16, f"p2w{i}") for i in range(2)]
        load_gb(1 + 2 * layer)
        P.op("sp", lambda h: h.dma_start(out=bgate[:], in_=bgate_d[layer]), writes=["bgate"], dma="c_bgate")
        wov = wout_d[layer].rearrange("(c p) n -> p c n", p=128)
        wbv = [wbr_d[layer, br].rearrange("(c p) n -> p c n", p=128) for br in range(2)]

        def load_p2w(c):
            s_ = c % 2
            cs_ = slice(c * 128, (c + 1) * 128)
            wload(p2w[s_][:, 0:4, :], wbv[0][:, :, cs_], f"p2w{s_}", [])
            wload(p2w[s_][:, 4:8, :], wbv[1][:, :, cs_], f"p2w{s_}", [])
            wload(p2w[s_][:, 8:16, :], wv[:, :, 2304 + c * 128:2304 + (c + 1) * 128], f"p2w{s_}", [])
            wload(p2w[s_][:, 16:24, :], wv[:, :, 3328 + c * 128:3328 + (c + 1) * 128], f"p2w{s_}", [f"p2w{s_}"])

        def load_p2w(c):
            s_ = c % 2
            cs_ = slice(c * 128, (c + 1) * 128)
            for (r0, r1, srcap) in ((0, 4, wbv[0][:, :, cs_]), (4, 8, wbv[1][:, :, cs_]),
                                    (8, 16, wv[:, :, 2304 + c * 128:2304 + (c + 1) * 128]),
                                    (16, 24, wv[:, :, 3328 + c * 128:3328 + (c + 1) * 128])):
                wload(p2w[s_][:, r0:r1, :], srcap, f"p2w{s_}", [f"p2w{s_}"])

        load_p2w(0)
        load_p2w(1)
        wload(wo[:, :, 0:512], wov[:, :, 0:512], "wo", ["wo"])
        wload(wo[:, :, 512:1024], wov[:, :, 512:1024], "wo", ["wo"])
        step = 0
        for c in range(KC):
            s_ = c % 2
            w_ = p2w[s_]
            for tb in range(4):
                bk = (0, 1, 2, 3) if step % 2 == 0 else (4, 5, 6, 7)
                tsl = slice(tb * 512, (tb + 1) * 512)

                def mm(h, w_=w_, bk=bk, tsl=tsl):
                    last = None
                    for k in range(4):
                        last = h.matmul(ps[bk[0]], w_[:, k, :], yT[0][:, k, tsl], start=(k == 0), stop=(k == 3))
                    for k in range(4):
                        last = h.matmul(ps[bk[1]], w_[:, 4 + k, :], yT[1][:, k, tsl], start=(k == 0), stop=(k == 3))
                    for k in range(KC):
                        last = h.matmul(ps[bk[2]], w_[:, 8 + k, :], xT[:, k, tsl], start=(k == 0), stop=(k == KC - 1))
                    for k in range(KC):
                        last = h.matmul(ps[bk[3]], w_[:, 16 + k, :], xT[:, k, tsl], start=(k == 0), stop=(k == KC - 1))
                    return last
                P.op("pe", mm, reads=[f"p2w{s_}"], writes=[f"ps{b_}" for b_ in bk])
                t0_, t1_ = ytmp[:, 0, :], ytmp[:, 1, :]
                P.op("act", lambda h, bk=bk, c=c: h.activation(t0_, ps[bk[2]], AF.Sigmoid, bias=bgate[:, c:c + 1], scale=1.0),
                     reads=[f"ps{bk[2]}", "bgate"], writes=["g0"])
                P.op("dve", lambda h, bk=bk: h.tensor_tensor(t0_, ps[bk[0]], t0_, ALU.mult),
                     reads=[f"ps{bk[0]}", "g0"], writes=["g0"])
                P.op("act", lambda h, bk=bk, c=c: h.activation(t1_, ps[bk[3]], AF.Sigmoid, bias=bgate[:, 8 + c:9 + c], scale=1.0),
                     reads=[f"ps{bk[3]}", "bgate"], writes=["g1"])
                P.op("dve", lambda h, bk=bk: h.tensor_tensor(t1_, ps[bk[1]], t1_, ALU.mult),
                     reads=[f"ps{bk[1]}", "g1"], writes=["g1"])
                P.op("pool", lambda h, c=c, tsl=tsl: h.tensor_tensor(zT[:, c, tsl], t0_, t1_, ALU.add),
                     reads=["g0", "g1"], writes=[f"z{tb}"])
                step += 1
            if c + 2 < KC:
                load_p2w(c + 2)

        if debug == "p2z" and layer == 0:
            dbg("dbg_z", zT[:], [128, KC, S], BF16, reads=[f"z{tb}" for tb in range(4)])
            raise StopBuild()

        x32 = M.at(R_Y, [128, KC, 128], F32, "x32")
        xtok = M.at(64 * KB, [128, NT, D], BF16, "xtok")
        if layer % 2 == 1:
            P.barrier()
        if layer % 2 == 1:
            P.op("sp", lambda h: h.dma_start(out=wr[:], in_=wr_d.rearrange("(c p) e -> p c e", p=128)),
                 writes=["wr"], dma="c_wr")
        def wo_pre(t):
            gb_ = (0, 1) if t % 2 == 0 else (2, 3)

            def mmo(h, t=t, gb_=gb_):
                last = None
                for half in range(2):
                    for k in range(KC):
                        last = h.matmul(ps[gb_[half]], zT[:, k, t * 128:(t + 1) * 128], wo[:, k, half * 512:(half + 1) * 512],
                                        start=(k == 0), stop=(k == KC - 1))
                return last
            P.op("pe", mmo, reads=["wo", f"z{t // 4}"], writes=[f"ps{gb_[0]}", f"ps{gb_[1]}"])
            for half in range(2):
                xs = X[:, t, half * 512:(half + 1) * 512]
                P.op("dve", lambda h, xs=xs, b_=gb_[half]: h.scalar_tensor_tensor(xs, xs, ALPHA, ps[b_], ALU.mult, ALU.add),
                     reads=[f"ps{gb_[half]}", f"X{t}"], writes=[f"X{t}"])
            ln_A(t)

        wo_pre(0)
        for t in range(NT):
            if t + 1 < NT:
                wo_pre(t + 1)
            if layer % 2 == 1:
                ln_B(t, (4, 5) if t % 2 == 0 else (6, 7), "act" if t % 2 == 0 else "dve",
                     router=x32, do_T=False, xtok=xtok)
            else:
                ln_B(t, (4, 5) if t % 2 == 0 else (6, 7), "act" if t % 2 == 0 else "dve")
        if layer % 2 == 1:
            router_finish()
        P.barrier()
        if debug == "p2" and layer == 0:
            dbg("dbg_X", X[:], [128, NT, D], F32, reads=[])
            dbg("dbg_xT", xT[:], [128, KC, S], BF16, reads=[])
            raise StopBuild()

        P.start_phase(f"l{layer}p3")
        load_gb(2 + 2 * layer)
        for t in range(NT):
            P.op("act", lambda h, t=t: h.activation(X[:, t, :], X[:, t, :], AF.Copy, scale=ALPHA),
                 reads=[f"X{t}"], writes=[f"X{t}"])
        if layer % 2 == 0 or "densemoe" in VAR:
            if layer % 2 == 0:
                passes = [(fg_d, fu_d, fd_d, FF_DENSE, None)]
            else:
                passes = [(mg_d[e], mu_d[e], md_d[e], FF_EXP, e) for e in range(NEXP)]
            items = []
            for (g_d, u_d, d_d, F_, e_) in passes:
                nblk = (F_ + 511) // 512
                for fb in range(nblk):
                    items.append((g_d, u_d, d_d, fb, min(4, (F_ - fb * 512) // 128), e_))
            wst = [dict(g=M.at(R_A + s_ * 24 * KB, [128, KC, 512], BF16, f"wg{s_}"),
                        u=M.at(R_A + s_ * 24 * KB + 8 * KB, [128, KC, 512], BF16, f"wu{s_}"),
                        d=M.at(R_A + s_ * 24 * KB + 16 * KB, [128, 4, D], BF16, f"wd{s_}")) for s_ in range(2)]
            hT = [M.at(R_Y + s_ * 16 * KB, [128, 4, S], BF16, f"hT{s_}") for s_ in range(2)]

            def ffn_load(ii):
                g_d, u_d, d_d, fb, nfc, e_ = items[ii]
                s_ = ii % 2
                n_ = nfc * 128
                gv = g_d.rearrange("(c p) n -> p c n", p=128)
                uv = u_d.rearrange("(c p) n -> p c n", p=128)
                dv = d_d[fb * 512:fb * 512 + n_, :].rearrange("(c p) n -> p c n", p=128)
                wload(wst[s_]["g"][:, :, 0:n_], gv[:, :, fb * 512:fb * 512 + n_], f"wg{s_}", [f"wg{s_}"])
                wload(wst[s_]["u"][:, :, 0:n_], uv[:, :, fb * 512:fb * 512 + n_], f"wu{s_}", [f"wu{s_}"])
                for hf_ in range(2):
                    wload(wst[s_]["d"][:, 0:nfc, hf_ * 512:(hf_ + 1) * 512], dv[:, :, hf_ * 512:(hf_ + 1) * 512], f"wd{s_}", [f"wd{s_}"])

            hstep = {"n": 0}

            def ffn_hidden(ii):
                g_d, u_d, d_d, fb, nfc, e_ = items[ii]
                s_ = ii % 2
                for fc in range(nfc):
                    for tb in range(4):
                        n = hstep["n"]
                        hstep["n"] += 1
                        bg, bu = (0, 1) if n % 2 == 0 else (2, 3)
                        tsl = slice(tb * 512, (tb + 1) * 512)

                        def mm(h, fc=fc, tsl=tsl, bg=bg, bu=bu, s_=s_):
                            last = None
                            for k in range(KC):
                                last = h.matmul(ps[bg], wst[s_]["g"][:, k, fc * 128:(fc + 1) * 128], xT[:, k, tsl],
                                                start=(k == 0), stop=(k == KC - 1))
                            for k in range(KC):
                                last = h.matmul(ps[bu], wst[s_]["u"][:, k, fc * 128:(fc + 1) * 128], xT[:, k, tsl],
                                                start=(k == 0), stop=(k == KC - 1))
                            return last
                        P.op("pe", mm, reads=[f"wg{s_}", f"wu{s_}"], writes=[f"ps{bg}", f"ps{bu}"])
                        tmp = ytmp[:, n % 2, :]
                        P.op("act", lambda h, tmp=tmp, bg=bg: h.activation(tmp, ps[bg], AF.Silu),
                             reads=[f"ps{bg}"], writes=[f"sg{n % 2}"])
                        P.op("dve", lambda h, tmp=tmp, bu=bu, fc=fc, tsl=tsl, s_=s_: h.tensor_tensor(hT[s_][:, fc, tsl], ps[bu], tmp, ALU.mult),
                             reads=[f"ps{bu}", f"sg{n % 2}"], writes=[f"hT{s_}_{tb}"])

            dstep = {"n": 0}

            def ffn_down(ii):
                g_d, u_d, d_d, fb, nfc, e_ = items[ii]
                s_ = ii % 2
                for t in range(NT):
                    for half in range(2):
                        n = dstep["n"]
                        dstep["n"] += 1
                        b_ = 4 + n % 4

                        def mm(h, t=t, half=half, b_=b_, s_=s_, nfc=nfc):
                            last = None
                            for fc in range(nfc):
                                last = h.matmul(ps[b_], hT[s_][:, fc, t * 128:(t + 1) * 128], wst[s_]["d"][:, fc, half * 512:(half + 1) * 512],
                                                start=(fc == 0), stop=(fc == nfc - 1))
                            return last
                        P.op("pe", mm, reads=[f"wd{s_}", f"hT{s_}_{t // 4}"], writes=[f"ps{b_}"])
                        xs = X[:, t, half * 512:(half + 1) * 512]
                        sc_ = 1.0 if e_ is None else comb[:, t, e_:e_ + 1]
                        P.op("dve", lambda h, xs=xs, b_=b_, sc_=sc_: h.scalar_tensor_tensor(xs, ps[b_], sc_, xs, ALU.mult, ALU.add),
                             reads=[f"ps{b_}", f"X{t}", "comb"], writes=[f"X{t}"])

            ffn_load(0)
            if len(items) > 1:
                ffn_load(1)
            for ii in range(len(items)):
                ffn_hidden(ii)
                if ii > 0:
                    ffn_down(ii - 1)
                    if ii + 1 < len(items):
                        ffn_load(ii + 1)
            ffn_down(len(items) - 1)

        else:
            moe_routed(layer)
            P.barrier()
        ln_A(0)
        for t in range(NT):
            if t + 1 < NT:
                ln_A(t + 1)
            ln_B(t, (0, 1) if t % 2 == 0 else (2, 3), "act" if t % 2 == 0 else "dve", do_T=(layer + 1 < DEPTH))
            if layer + 1 == DEPTH and debug is None:
                P.op("sp", lambda h, t=t: h.dma_start(out=out_d[t * 128:(t + 1) * 128, :], in_=X[:, t, :]),
                     reads=[f"X{t}"], dma="out")
        P.barrier()
        if debug == f"l{layer}":
            dbg("dbg_X", X[:], [128, NT, D], F32, reads=[])
            raise StopBuild()

    try:
        for layer in range(n_layers):
            do_layer(layer)
    except StopBuild:
        P.barrier()

    P.start_phase("pout")
    if debug is not None or n_layers < DEPTH:
        for t in range(NT):
            P.op("sp", lambda h, t=t: h.dma_start(out=out_d[t * 128:(t + 1) * 128, :], in_=X[:, t, :]),
                 reads=[f"X{t}"], dma="out")
    P.final_wait("sp")
    P.flush()
    return nc, dbg_outs


def _perm_w_in(w_in):
    idx = []
    for hf in range(2):
        idx += list(range(256 * hf, 256 * hf + 256))
        idx += list(range(512 + 256 * hf, 512 + 256 * hf + 256))
        idx += list(range(1024 + 256 * hf, 1024 + 256 * hf + 256))
    for c in range(4):
        idx += list(range(1536 + c * 64, 1536 + c * 64 + 64))
        idx += list(range(1536 + (c + 4) * 64, 1536 + (c + 4) * 64 + 64))
    idx += list(range(2048, 4352))
    return np.ascontiguousarray(w_in[:, :, np.asarray(idx)])


def _na_bias_tables(rpb):
    L = rpb.shape[0]
    kc = np.arange(64)[:, None]
    qc = np.arange(64)[None, :]
    qcs = np.clip(qc - 8, 0, 48)
    colvalid = (kc >= qcs) & (kc < qcs + 16)
    dc = np.clip(kc - qc + 15, 0, 30)
    out = np.full((L, 128, 8, 1536), NEG, np.float32)

    def Cmat(l, h, e, masked):
        if e < -7 or e > 7 or (masked and not (-4 <= e <= 3)):
            return np.full((64, 64), NEG, np.float32)
        return np.where(colvalid, rpb[l, h, e + 7][dc], NEG).astype(np.float32)

    for l in range(L):
        for h in range(8):
            for masked, e_hi, n, col0 in ((True, 4, 10, 0), (False, 6, 14, 640)):
                for idx in range(n):
                    e = e_hi - idx
                    for kl in range(2):
                        out[l, kl * 64:(kl + 1) * 64, h, col0 + idx * 64:col0 + (idx + 1) * 64] = Cmat(l, h, e + kl, masked)
    return out


def _const_tables():
    pos = np.arange(S, dtype=np.float32)
    inv_freq = (1.0 / (500000.0 ** (np.arange(0, 16, 2, dtype=np.float32) / 16.0))).astype(np.float32)
    ang = pos[:, None] * inv_freq[None, :]
    cos = np.cos(ang).astype(np.float32).T
    sin = np.sin(ang).astype(np.float32).T
    cs = np.zeros((2, 128, S), np.float32)
    cs[0] = 1.0
    for half in range(2):
        o = half * 64
        cs[0, o:o + 8] = cos
        cs[0, o + 8:o + 16] = cos
        cs[1, o:o + 8] = -sin
        cs[1, o + 8:o + 16] = sin
    cst = np.zeros((128, 1152), np.float32)
    for m in range(128):
        d = m % 64
        partner = m + 8 if d < 8 else (m - 8 if d < 16 else m)
        cst[partner, m] = 1.0
    ki = np.arange(128)[:, None]
    qi = np.arange(128)[None, :]
    mP = np.where(ki >= qi, 0.0, NEG).astype(np.float32)
    mN = np.where(ki <= qi, 0.0, NEG).astype(np.float32)
    cst[:, 128:640] = np.tile(mP, (1, 4))
    cst[:, 640:1152] = np.tile(mN, (1, 4))
    return cs, cst


def _const2():
    c = np.zeros((128, 644), np.float32)
    k = np.arange(128)[:, None]
    m = np.arange(128)[None, :]
    c[:, 0:128] = (k < m).astype(np.float32)
    c[:, 128:132] = np.arange(4)[None, :] * 128 + np.arange(128)[:, None]
    c[:, 132:644] = np.arange(512)[None, :]
    return c


def make_in_maps(inputs):
    f = lambda a: np.ascontiguousarray(np.asarray(a, dtype=np.float32))
    x = f(inputs["x"])
    lnp = np.stack([
        np.stack([f(inputs["emb_ln_g"]), f(inputs["emb_ln_b"])]),
        np.stack([f(inputs["ln1_g"])[0], f(inputs["ln1_b"])[0]]),
        np.stack([f(inputs["ln2_g"])[0], f(inputs["ln2_b"])[0]]),
        np.stack([f(inputs["ln1_g"])[1], f(inputs["ln1_b"])[1]]),
        np.stack([f(inputs["ln2_g"])[1], f(inputs["ln2_b"])[1]]),
    ]).astype(np.float32)
    ident = np.eye(128, dtype=np.float32)
    cs, cst = _const_tables()
    shared = {"lnp": lnp, "ident": ident, "w_in": _perm_w_in(f(inputs["w_in"])),
              "nabias": _na_bias_tables(f(inputs["na_rpb"])), "cs": cs, "cst": cst,
              "sink": f(inputs["sw_sink"]),
              "bgate": np.ascontiguousarray(f(inputs["b_gate"]).reshape(DEPTH, 16, 128).transpose(0, 2, 1)),
              "wbr": np.ascontiguousarray(np.stack([f(inputs["w_branch_na"]), f(inputs["w_branch_sw"])], axis=1)),
              "wout": f(inputs["w_out"]),
              "ffn_g": f(inputs["ffn_w_gate"])[0], "ffn_u": f(inputs["ffn_w_up"])[0], "ffn_d": f(inputs["ffn_w_down"])[0],
              "moe_g": f(inputs["moe_w_gate"])[0], "moe_u": f(inputs["moe_w_up"])[0], "moe_d": f(inputs["moe_w_down"])[0],
              "wr": f(inputs["moe_router"])[0], "cst2": _const2()}
    maps = []
    for c in range(NCORES):
        m = dict(shared)
        m["x"] = np.ascontiguousarray(x[c])
        maps.append(m)
    return maps


_CACHE = {}


def kernel(**inputs):
    if "nc" not in _CACHE:
        _CACHE["nc"] = build()[0]
    nc = _CACHE["nc"]
    in_maps = make_in_maps(inputs)
    res = run_bass_kernel_spmd(nc, in_maps, core_ids=list(range(NCORES)))
    out = np.stack([np.asarray(r["out"], dtype=np.float32).reshape(S, D) for r in res.results], axis=0)
    return out
```

```python
import numpy as np
import ml_dtypes
import concourse.bass as bass
import concourse.mybir as mybir
from concourse.bass_utils import run_bass_kernel_spmd

F32 = mybir.dt.float32
BF16 = mybir.dt.bfloat16
AF = mybir.ActivationFunctionType
ALU = mybir.AluOpType
AX = mybir.AxisListType

NCORES = 8
D = 1024
S = 2048
NT = 16
KC = 8
DEPTH = 2
ALPHA = (2 * DEPTH) ** 0.25
EPS = 1e-5
PROJ = 4352
FF_DENSE = 2816
FF_EXP = 3584
NEXP = 8
NEG = -30000.0
import os
VAR = os.environ.get('KVAR', '')


class Prog:
    ENGS = ("pe", "act", "dve", "pool", "sp")

    def __init__(self, nc):
        self.nc = nc
        self.h = {"pe": nc.tensor, "act": nc.scalar, "dve": nc.vector, "pool": nc.gpsimd, "sp": nc.sync}
        self.q = {e: [] for e in self.ENGS}
        self.sem = {}
        self.cnt = {}
        self.waited = {e: {} for e in self.ENGS}
        self.res = {}
        self.dma_sems = {}
        self._ctx = []
        self.cond = None
        self.regs = {}

    def new_sem(self, name):
        cm = self.nc.semaphore(name)
        s = cm.__enter__()
        self._ctx.append(cm)
        return s

    def start_phase(self, name):
        for e in self.ENGS:
            self.sem[e] = self.new_sem(f"{name}_{e}")
            self.cnt[e] = 0

    def dma_sem(self, key):
        if key not in self.dma_sems:
            self.dma_sems[key] = [self.new_sem(f"dma_{key}"), 0]
        return self.dma_sems[key]

    def _need(self, eng, tok):
        sem, val, src = tok
        w = self.waited[eng]
        k = id(sem)
        if w.get(k, 0) >= val:
            return False
        w[k] = val
        return True

    def op(self, eng, fn, reads=(), writes=(), dma=None):
        writes = list(writes) + [r for r in reads if r.startswith("ps") and r not in writes]
        reads = [r for r in reads if not r.startswith("ps")]
        toks = []
        for r in reads:
            st = self.res.get(r)
            if st and st["w"] is not None:
                toks.append(st["w"])
        for w_ in writes:
            st = self.res.get(w_)
            if st:
                if st["w"] is not None and (st["w"][2] != eng or st["w"][3]):
                    toks.append(st["w"])
                for t in st["r"]:
                    if t[2] != eng or t[3]:
                        toks.append(t)
        waits = []
        for t in toks:
            if self._need(eng, t[:3]):
                waits.append((t[0], t[1]))
        if dma is not None:
            ds = self.dma_sem(dma)
            ds[1] += 16
            tok = (ds[0], ds[1], eng, True)
            sem, inc = ds[0], 16
        else:
            self.cnt[eng] += 1
            tok = (self.sem[eng], self.cnt[eng], eng, False)
            sem, inc = self.sem[eng], 1

        def emit(h, waits=waits, fn=fn, sem=sem, inc=inc):
            for s_, v_ in waits:
                h.wait_ge(s_, v_)
            fn(h).then_inc(sem, inc)

        if self.cond is not None:
            self.cond["q"][eng].append(emit)
            d_ = self.cond["inc"][eng]
            d_[id(sem)] = (sem, d_.get(id(sem), (sem, 0))[1] + inc)
        else:
            self.q[eng].append(emit)
        for r in reads:
            self.res.setdefault(r, {"w": None, "r": []})["r"].append(tok)
        for w_ in writes:
            self.res[w_] = {"w": tok, "r": []}
        return tok

    def get_reg(self, eng, h):
        if eng not in self.regs:
            self.regs[eng] = h.alloc_register(f"nreg_{eng}")
        return self.regs[eng]

    def regload(self, engs, ap, reads):
        for e in engs:
            self.op(e, lambda h, e=e: h.reg_load(self.get_reg(e, h), ap), reads=reads)

    def begin_cond(self, thr):
        import copy
        pre = {id(self.sem[e]): self.cnt[e] for e in self.ENGS}
        for k, (s_, c_) in self.dma_sems.items():
            pre[id(s_)] = c_
        c = {"thr": thr, "q": {e: [] for e in self.ENGS}, "inc": {e: {} for e in self.ENGS},
             "waited": copy.deepcopy(self.waited), "pre": pre, "parent": self.cond}
        self.cond = c

    def end_cond(self):
        c = self.cond
        parent = c["parent"]
        self.cond = parent
        for e in self.ENGS:
            q = c["q"][e]
            if not q:
                continue
            incs = list(c["inc"][e].values())

            def emit(h, q=q, incs=incs, e=e, thr=c["thr"], pre=c["pre"]):
                v = h.snap(self.get_reg(e, h), min_val=0, max_val=4096)
                with h.If(v > thr):
                    for f in q:
                        f(h)
                with h.Else():
                    for (s_, n_) in incs:
                        p_ = pre.get(id(s_), 0)
                        if p_ > 0:
                            h.wait_ge(s_, p_)
                    for (s_, n_) in incs:
                        h.sem_inc(s_, n_)
            if parent is not None:
                parent["q"][e].append(emit)
                d_ = parent["inc"][e]
                for (s_, n_) in incs:
                    d_[id(s_)] = (s_, d_.get(id(s_), (s_, 0))[1] + n_)
            else:
                self.q[e].append(emit)
        self.waited = c["waited"]

    def barrier(self):
        toks = []
        for e in self.ENGS:
            if self.cnt[e] > 0:
                toks.append((self.sem[e], self.cnt[e], e))
        for k, (s_, c_) in self.dma_sems.items():
            if c_ > 0:
                toks.append((s_, c_, "dma"))
        for e in self.ENGS:
            waits = [(t[0], t[1]) for t in toks if t[2] != e and self._need(e, t)]
            if waits:
                def emit(h, waits=waits):
                    for s_, v_ in waits:
                        h.wait_ge(s_, v_)
                self.q[e].append(emit)
        self.res = {}

    def final_wait(self, eng="sp"):
        waits = [(s_, c_) for k, (s_, c_) in self.dma_sems.items() if c_ > 0]

        def emit(h, waits=waits):
            for s_, v_ in waits:
                h.wait_ge(s_, v_)
        self.q[eng].append(emit)

    def flush(self):
        nc = self.nc
        with nc.Block() as block:
            for e, reg in (("sp", block.sync), ("pool", block.gpsimd), ("pe", block.tensor),
                           ("act", block.scalar), ("dve", block.vector)):
                q = self.q[e]
                if not q:
                    continue

                def body(h, q=q):
                    for f in q:
                        f(h)
                reg(body)
        for cm in reversed(self._ctx):
            cm.__exit__(None, None, None)


class StopBuild(Exception):
    pass


class Mem:
    def __init__(self, nc):
        self.nc = nc
        self.base = (nc.sbuf_base + 63) // 64 * 64
        self.top = nc.sbuf_top
        self.n = 0

    def at(self, off, shape, dtype, name=None):
        self.n += 1
        nbytes = int(np.prod(shape[1:])) * (2 if dtype == BF16 else 4)
        assert self.base + off + nbytes <= self.top, (name, off, nbytes, self.top - self.base)
        return self.nc.alloc_sbuf_tensor_at(name or f"t{self.n}", list(shape), dtype, offset=self.base + off)


def build(debug=None, n_layers=DEPTH):
    nc = bass.Bass("TRN2", target_bir_lowering=False)
    P = Prog(nc)
    M = Mem(nc)
    dbg_outs = {}

    x_d = nc.dram_tensor("x", [S, D], F32, kind="ExternalInput").ap()
    lnp_d = nc.dram_tensor("lnp", [5, 2, D], F32, kind="ExternalInput").ap()
    ident_d = nc.dram_tensor("ident", [128, 128], F32, kind="ExternalInput").ap()
    out_d = nc.dram_tensor("out", [S, D], F32, kind="ExternalOutput").ap()
    w_in_d = nc.dram_tensor("w_in", [DEPTH, D, PROJ], F32, kind="ExternalInput").ap()
    nabias_d = nc.dram_tensor("nabias", [DEPTH, 128, 8, 1536], F32, kind="ExternalInput").ap()
    cs_d = nc.dram_tensor("cs", [2, 128, S], F32, kind="ExternalInput").ap()
    cst_d = nc.dram_tensor("cst", [128, 1152], F32, kind="ExternalInput").ap()
    sink_d = nc.dram_tensor("sink", [DEPTH, 8], F32, kind="ExternalInput").ap()
    bgate_d = nc.dram_tensor("bgate", [DEPTH, 128, 16], F32, kind="ExternalInput").ap()
    wbr_d = nc.dram_tensor("wbr", [DEPTH, 2, 512, D], F32, kind="ExternalInput").ap()
    wout_d = nc.dram_tensor("wout", [DEPTH, D, D], F32, kind="ExternalInput").ap()
    fg_d = nc.dram_tensor("ffn_g", [D, FF_DENSE], F32, kind="ExternalInput").ap()
    fu_d = nc.dram_tensor("ffn_u", [D, FF_DENSE], F32, kind="ExternalInput").ap()
    fd_d = nc.dram_tensor("ffn_d", [FF_DENSE, D], F32, kind="ExternalInput").ap()
    mg_d = nc.dram_tensor("moe_g", [NEXP, D, FF_EXP], F32, kind="ExternalInput").ap()
    mu_d = nc.dram_tensor("moe_u", [NEXP, D, FF_EXP], F32, kind="ExternalInput").ap()
    md_d = nc.dram_tensor("moe_d", [NEXP, FF_EXP, D], F32, kind="ExternalInput").ap()
    wr_d = nc.dram_tensor("wr", [D, NEXP], F32, kind="ExternalInput").ap()
    cst2_d = nc.dram_tensor("cst2", [128, 128 + 4 + 512], F32, kind="ExternalInput").ap()

    KB = 1024
    X = M.at(0, [128, NT, D], F32, "X")
    xT = M.at(64 * KB, [128, KC, S], BF16, "xT")
    R_Y = 96 * KB
    R_A = 128 * KB
    R_B = 176 * KB
    R_M = 196 * KB
    gb = M.at(R_B + 12 * KB, [128, 2, D], F32, "gb")
    mo = {"o": R_M}

    def misc(shape, dtype, name):
        nb = int(np.prod(shape[1:])) * (2 if dtype == BF16 else 4)
        t = M.at(mo["o"], shape, dtype, name)
        mo["o"] += (nb + 63) // 64 * 64
        return t
    ident = misc([128, 128], F32, "ident")
    identb = misc([128, 128], BF16, "identb")
    stats = misc([128, 2, 12], F32, "stats")
    mv = misc([128, 2, 8], F32, "mv")
    eps_t = misc([128, 4], F32, "eps")
    esink = misc([128, 8], F32, "esink")
    rden = misc([128, 2, 8], F32, "rden")
    ytmp = misc([128, 2, 512], F32, "ytmp")
    maskPN = misc([128, 2, 512], BF16, "maskPN")
    prot = misc([128, 128], BF16, "prot")
    bgate = misc([128, 16], F32, "bgate")
    wr = misc([128, KC, NEXP], F32, "wr")
    comb = misc([128, NT, NEXP], F32, "comb")
    rt = misc([128, 4, 8], F32, "rt")
    maskt = misc([128, NT, NEXP], F32, "maskt")
    rank = misc([128, NT, NEXP], F32, "rank")
    rankm = misc([128, NT, NEXP], F32, "rankm")
    offs = misc([128, NEXP], F32, "offs")
    rsh = misc([128, NT], F32, "rsh")
    cntf = misc([128, NEXP], F32, "cntf")
    cnti = misc([128, NEXP], mybir.dt.int32, "cnti")
    slotid = misc([128, 4], F32, "slotid")
    negslot = misc([128, 4], F32, "negslot")
    ltri = misc([128, 128], F32, "ltri")
    ones = misc([128, 128], F32, "ones")
    Rb = misc([128, 128], F32, "Rb")

    PS = nc.alloc_psum_tensor("PS", [128, 8, 512], F32)
    ps = [PS[:, i, :] for i in range(8)]

    def dbg(name, src_ap, shape, dtype, reads):
        t = nc.dram_tensor(name, list(shape), dtype, kind="ExternalOutput").ap()
        dbg_outs[name] = (shape, dtype)
        P.op("sp", lambda h: h.dma_start(out=t, in_=src_ap), reads=reads, dma="dbg")

    def load_gb(idx):
        src = lnp_d[idx].partition_broadcast(128) if hasattr(lnp_d[idx], "partition_broadcast") else None
        P.op("sp", lambda h: h.dma_start(out=gb[:], in_=src), writes=["gb"], dma="gb")

    def ln_tile(t, tp_banks, evac_eng, router=None, do_T=True, xtok=None):
        ln_A(t)
        ln_B(t, tp_banks, evac_eng, router, do_T, xtok)

    def ln_A(t):
        xt = X[:, t, :]
        sl = t % 2
        st = stats[:, sl, :]
        m = mv[:, sl, :]
        rx = f"X{t}"
        rs = f"st{sl}"
        P.op("dve", lambda h: (h.bn_stats(st[:, 0:6], xt[:, 0:512]), h.bn_stats(st[:, 6:12], xt[:, 512:1024]))[1],
             reads=[rx], writes=[rs])
        P.op("dve", lambda h: h.bn_aggr(m[:, 0:2], st[:, 0:12]), reads=[rs], writes=[rs + "m"])
        P.op("act", lambda h: h.activation(m[:, 2:3], m[:, 1:2], AF.Sqrt, bias=eps_t[:, 0:1], scale=1.0),
             reads=[rs + "m"], writes=[rs + "s"])
        P.op("dve", lambda h: h.reciprocal(m[:, 3:4], m[:, 2:3]), reads=[rs + "s"], writes=[rs + "r"])
        P.op("dve", lambda h: h.scalar_tensor_tensor(m[:, 4:5], m[:, 0:1], -1.0, m[:, 3:4], ALU.mult, ALU.mult),
             reads=[rs + "r", rs + "m"], writes=[rs + "n"])


    def ln_B(t, tp_banks, evac_eng, router=None, do_T=True, xtok=None):
        xt = X[:, t, :]
        sl = t % 2
        m = mv[:, sl, :]
        rx = f"X{t}"
        rs = f"st{sl}"
        P.op("act", lambda h: h.activation(xt, xt, AF.Identity, bias=m[:, 4:5], scale=m[:, 3:4]),
             reads=[rx, rs + "n", rs + "r"], writes=[rx])
        gbe = "pool" if "poolgb" in VAR else "dve"
        gbe0 = "pool" if "poolg" in VAR else "dve"
        P.op(gbe0, lambda h: h.tensor_tensor(xt, xt, gb[:, 0, :], ALU.mult), reads=[rx, "gb"], writes=[rx])
        P.op(gbe, lambda h: h.tensor_tensor(xt, xt, gb[:, 1, :], ALU.add), reads=[rx, "gb"], writes=[rx])
        if xtok is not None:
            P.op("act", lambda h: h.activation(xtok[:, t, :], xt, AF.Copy), reads=[rx], writes=[f"xtok{t}"])
        if not do_T and router is None:
            return
        b0, b1 = tp_banks

        def tp(h):
            last = None
            for c in range(KC):
                bank = ps[b0] if c < 4 else ps[b1]
                last = h.transpose(bank[:, (c % 4) * 128:(c % 4 + 1) * 128], xt[:, c * 128:(c + 1) * 128], ident[:])
            return last
        P.op("pe", tp, reads=[rx, "ident"], writes=[f"ps{b0}", f"ps{b1}"])
        for half, b in ((0, b0), (1, b1)):
            dst = xT[:, half * 4:(half + 1) * 4, t * 128:(t + 1) * 128]
            src = ps[b].rearrange("p (c n) -> p c n", c=4)
            if not do_T:
                pass
            elif evac_eng == "act":
                P.op("act", lambda h, dst=dst, src=src: h.activation(dst, src, AF.Copy),
                     reads=[f"ps{b}"], writes=[f"xT{t}"])
            else:
                P.op("dve", lambda h, dst=dst, src=src: h.tensor_copy(dst, src),
                     reads=[f"ps{b}"], writes=[f"xT{t}"])
            if router is not None:
                x32 = router
                o_eng = "dve" if evac_eng == "act" else "act"
                d32 = x32[:, half * 4:(half + 1) * 4, :]
                if o_eng == "act":
                    P.op("act", lambda h, d32=d32, src=src: h.activation(d32, src, AF.Copy),
                         reads=[f"ps{b}"], writes=[f"x32_{half}"])
                else:
                    P.op("dve", lambda h, d32=d32, src=src: h.tensor_copy(d32, src),
                         reads=[f"ps{b}"], writes=[f"x32_{half}"])
        if router is not None:
            x32 = router
            lg = ps[b0][:, 0:NEXP]

            def rmm(h):
                last = None
                for k in range(KC):
                    last = h.matmul(lg, x32[:, k, :], wr[:, k, :], start=(k == 0), stop=(k == KC - 1))
                return last
            P.op("pe", rmm, reads=["x32_0", "x32_1", "wr"], writes=[f"ps{b0}"])
            P.op("dve", lambda h: h.tensor_copy(rank[:, t, :], lg), reads=[f"ps{b0}"], writes=["lgall"])

    def router_finish():
        L = rank[:]
        A = rankm[:]
        B = Rb[:].rearrange("p (t e) -> p t e", e=NEXP)
        rtf = rt[:].rearrange("p a e -> p (a e)")
        m1, m2, den = rsh[:], rtf[:, 0:16], rtf[:, 16:32]
        bc = lambda v: v.unsqueeze(2).to_broadcast([128, NT, NEXP])
        P.op("dve", lambda h: h.tensor_reduce(m1, L, AX.X, ALU.max), reads=["lgall"], writes=["r_m1"])
        P.op("dve", lambda h: h.tensor_tensor(A, L, bc(m1), ALU.is_equal), reads=["lgall", "r_m1"], writes=["r_A"])
        P.op("dve", lambda h: h.scalar_tensor_tensor(B, A, -1e30, L, ALU.mult, ALU.add), reads=["r_A", "lgall"], writes=["r_B"])
        P.op("dve", lambda h: h.tensor_reduce(m2, B, AX.X, ALU.max), reads=["r_B"], writes=["r_m2"])
        P.op("dve", lambda h: h.tensor_tensor(maskt[:], L, bc(m2), ALU.is_ge), reads=["lgall", "r_m2"], writes=["maskt"])
        P.op("dve", lambda h: h.tensor_tensor(A, L, bc(m1), ALU.subtract), reads=["lgall", "r_m1", "r_A"], writes=["r_A"])
        P.op("act", lambda h: h.activation(B, A, AF.Exp), reads=["r_A", "r_B"], writes=["r_B"])
        P.op("dve", lambda h: h.tensor_tensor(A, maskt[:], B, ALU.mult), reads=["maskt", "r_B", "r_A"], writes=["r_A"])
        P.op("dve", lambda h: h.tensor_reduce(den, A, AX.X, ALU.add), reads=["r_A"], writes=["r_den"])
        P.op("dve", lambda h: h.reciprocal(den, den), reads=["r_den"], writes=["r_den"])
        P.op("dve", lambda h: h.tensor_tensor(comb[:], A, bc(den), ALU.mult), reads=["r_A", "r_den"], writes=["comb"])

    wq = {"n": 0}

    def wload(dst, src, key, writes):
        P.op("pool", lambda h: h.dma_start(out=dst, in_=src, max_dma_last_dim=2048), writes=writes, dma=key)

    def gemm_fm(w_slot, wcol0, dst_fn, evac_fn, banks, wres, xres_fn, tag):
        for tb in range(4):
            b = banks[tb % len(banks)]

            def mm(h, tb=tb, b=b):
                last = None
                for k in range(KC):
                    last = h.matmul(ps[b], w_slot[:, k, wcol0:wcol0 + 128], xT[:, k, tb * 512:(tb + 1) * 512],
                                    start=(k == 0), stop=(k == KC - 1))
                return last
            P.op("pe", mm, reads=[wres] + xres_fn(tb), writes=[f"ps{b}"])
            evac_fn(tb, b)

    def xT_res(tb):
        return [f"xT{t}" for t in range(tb * 4, tb * 4 + 4)]

    P.start_phase("p0")
    P.op("pool", lambda h: h.memset(eps_t[:], EPS), writes=["eps"])
    P.op("sp", lambda h: h.dma_start(out=ident[:], in_=ident_d), writes=["ident"], dma="c_ident")
    if debug != "p0x" or "a" in VAR:
        wload(identb[:], ident_d, "c_identb", ["identb"])
    load_gb(0)
    for t in range(NT):
        P.op("sp", lambda h, t=t: h.dma_start(out=X[:, t, :], in_=x_d[t * 128:(t + 1) * 128, :]),
             writes=[f"X{t}"], dma=f"x{t}")
    P.res.setdefault("st0m", {"w": None, "r": []})
    P.res["st0m"] = {"w": P.res["eps"]["w"], "r": []}
    P.res["st1m"] = {"w": P.res["eps"]["w"], "r": []}
    ln_A(0)
    for t in range(NT):
        if t + 1 < NT:
            ln_A(t + 1)
        ln_B(t, (0, 1) if t % 2 == 0 else (2, 3), "act" if t % 2 == 0 else "dve")
    P.barrier()

    if debug in ("p0", "p0x"):
        dbg("dbg_X", X[:], [128, NT, D], F32, reads=[])
        dbg("dbg_xT", xT[:], [128, KC, S], BF16, reads=[])
        n_layers = 0

    yT = [M.at(R_Y, [128, 4, S], BF16, "yTna"), M.at(R_Y + 16 * KB, [128, 4, S], BF16, "yTsw")]

    def moe_routed(layer):
        xtok = M.at(64 * KB, [128, NT, D], BF16, "xtok")
        wst = [dict(g=M.at(R_A + s_ * 24 * KB, [128, KC, 512], BF16, f"wg{s_}"),
                    u=M.at(R_A + s_ * 24 * KB + 8 * KB, [128, KC, 512], BF16, f"wu{s_}"),
                    d=M.at(R_A + s_ * 24 * KB + 16 * KB, [128, 4, D], BF16, f"wd{s_}")) for s_ in range(2)]
        xg = M.at(R_Y, [128, KC, 512], BF16, "xg")
        yacc = M.at(R_Y + 8 * KB, [128, 4, D], F32, "yacc")
        hTs = [M.at(R_Y + 24 * KB + s_ * 4 * KB, [128, 4, 512], BF16, f"hTs{s_}") for s_ in range(2)]
        yslot = M.at(R_B, [128, 4, D], BF16, "yslot")
        selt = [M.at(R_B + 8 * KB + i * KB, [128, 512], BF16, f"selt{i}") for i in range(2)]
        iota512 = M.at(R_B + 10 * KB, [128, 512], F32, "iota512")
        selT = [maskPN[:, i, :].rearrange("p (s n) -> p s n", s=4) for i in range(2)]

        P.op("sp", lambda h: h.dma_start(out=ltri[:], in_=cst2_d[:, 0:128]), writes=["ltri"], dma="c_ltri")
        P.op("sp", lambda h: h.dma_start(out=slotid[:], in_=cst2_d[:, 128:132]), writes=["slotid"], dma="c_slot")
        P.op("sp", lambda h: h.dma_start(out=iota512[:], in_=cst2_d[:, 132:644]), writes=["iota"], dma="c_iota")
        P.op("pool", lambda h: h.memset(ones[:], 1.0), writes=["ones"])
        P.op("dve", lambda h: h.tensor_scalar(negslot[:], slotid[:], -1.0, None, ALU.mult), reads=["slotid"], writes=["negslot"])
        one_t = ones[:, 0:1]
        mflat = maskt[:].rearrange("p t e -> p (t e)")
        P.op("pe", lambda h: h.matmul(ps[0][:, 0:128], ones[:], mflat, start=True, stop=True),
             reads=["ones", "maskt"], writes=["ps0"])
        P.op("pe", lambda h: h.matmul(ps[1][:, 0:128], ltri[:], mflat, start=True, stop=True),
             reads=["ltri", "maskt"], writes=["ps1"])
        cps = ps[0][:, 0:128].rearrange("p (t e) -> p t e", e=NEXP)
        wps = ps[1][:, 0:128].rearrange("p (t e) -> p t e", e=NEXP)
        P.op("dve", lambda h: h.memset(offs[:], 0.0), writes=["offs"])
        for t in range(NT):
            P.op("dve", lambda h, t=t: h.tensor_tensor(rank[:, t, :], wps[:, t, :], offs[:], ALU.add),
                 reads=["ps1", "offs"], writes=["rank"])
            P.op("dve", lambda h, t=t: h.tensor_tensor(offs[:], cps[:, t, :], offs[:], ALU.add),
                 reads=["ps0", "offs", "rank"], writes=["offs"])
        P.op("dve", lambda h: h.tensor_copy(cnti[:], offs[:]), reads=["offs"], writes=["cnti"])
        rkf = rank[:].rearrange("p t e -> p (t e)")
        rmf = rankm[:].rearrange("p t e -> p (t e)")
        P.op("dve", lambda h: h.scalar_tensor_tensor(rmf, rkf, 1.0, mflat, ALU.add, ALU.mult),
             reads=["rank", "maskt"], writes=["rankm"])
        P.op("dve", lambda h: h.tensor_scalar(rmf, rmf, -1.0, None, ALU.add), reads=["rankm"], writes=["rankm"])
        if debug == "route":
            dbg("dbg_rankm", rankm[:], [128, NT, NEXP], F32, reads=["rankm"])
            dbg("dbg_comb", comb[:], [128, NT, NEXP], F32, reads=["comb"])
            dbg("dbg_cnti", cnti[:], [128, NEXP], mybir.dt.int32, reads=["cnti"])
            raise StopBuild()

        nblk = FF_EXP // 512
        cnt = {"h": 0, "d": 0, "w": 0}

        def load_w(e, fb):
            s_ = cnt["w"] % 2
            cnt["w"] += 1
            gv = mg_d[e].rearrange("(c p) n -> p c n", p=128)
            uv = mu_d[e].rearrange("(c p) n -> p c n", p=128)
            dv = md_d[e][fb * 512:(fb + 1) * 512, :].rearrange("(c p) n -> p c n", p=128)
            wload(wst[s_]["g"][:], gv[:, :, fb * 512:(fb + 1) * 512], f"wg{s_}", [f"wg{s_}"])
            wload(wst[s_]["u"][:], uv[:, :, fb * 512:(fb + 1) * 512], f"wu{s_}", [f"wu{s_}"])
            for hf_ in range(2):
                wload(wst[s_]["d"][:, :, hf_ * 512:(hf_ + 1) * 512], dv[:, :, hf_ * 512:(hf_ + 1) * 512], f"wd{s_}", [f"wd{s_}"])
            return s_

        def hidden(s_, hs, nsl):
            for fc in range(4):
                n = cnt["h"]
                cnt["h"] += 1
                bg, bu = (0, 1) if n % 2 == 0 else (2, 3)

                def mm(h, fc=fc, bg=bg, bu=bu):
                    last = None
                    for k in range(KC):
                        last = h.matmul(ps[bg][:, 0:nsl], wst[s_]["g"][:, k, fc * 128:(fc + 1) * 128], xg[:, k, 0:nsl],
                                        start=(k == 0), stop=(k == KC - 1))
                    for k in range(KC):
                        last = h.matmul(ps[bu][:, 0:nsl], wst[s_]["u"][:, k, fc * 128:(fc + 1) * 128], xg[:, k, 0:nsl],
                                        start=(k == 0), stop=(k == KC - 1))
                    return last
                P.op("pe", mm, reads=[f"wg{s_}", f"wu{s_}"] + [f"xg{c}" for c in range(KC)], writes=[f"ps{bg}", f"ps{bu}"])
                tmp = ytmp[:, n % 2, 0:nsl]
                P.op("act", lambda h, tmp=tmp, bg=bg: h.activation(tmp, ps[bg][:, 0:nsl], AF.Silu), reads=[f"ps{bg}"], writes=[f"sg{n % 2}"])
                P.op("dve", lambda h, tmp=tmp, bu=bu, fc=fc: h.tensor_tensor(hTs[hs][:, fc, 0:nsl], ps[bu][:, 0:nsl], tmp, ALU.mult),
                     reads=[f"ps{bu}", f"sg{n % 2}"], writes=[f"hTs{hs}"])

        def down(s_, hs, first, nst):
            for st_ in range(nst):
                for half in range(2):
                    n = cnt["d"]
                    cnt["d"] += 1
                    b_ = 4 + n % 4

                    def mm(h, st_=st_, half=half, b_=b_):
                        last = None
                        for fc in range(4):
                            last = h.matmul(ps[b_], hTs[hs][:, fc, st_ * 128:(st_ + 1) * 128], wst[s_]["d"][:, fc, half * 512:(half + 1) * 512],
                                            start=(fc == 0), stop=(fc == 3))
                        return last
                    P.op("pe", mm, reads=[f"wd{s_}", f"hTs{hs}"], writes=[f"ps{b_}"])
                    ya = yacc[:, st_, half * 512:(half + 1) * 512]
                    if first:
                        P.op("dve", lambda h, ya=ya, b_=b_: h.tensor_copy(ya, ps[b_]), reads=[f"ps{b_}"], writes=[f"yacc{st_}"])
                    else:
                        P.op("dve", lambda h, ya=ya, b_=b_: h.tensor_tensor(ya, ps[b_], ya, ALU.add),
                             reads=[f"ps{b_}", f"yacc{st_}"], writes=[f"yacc{st_}"])

        def block(e, off, nsl):
            nst = nsl // 128
            P.op("dve", lambda h: h.tensor_scalar(rsh[:], rankm[:, :, e], float(-off), None, ALU.add),
                 reads=["rankm"], writes=["rsh"])
            s0 = load_w(e, 0)
            for t in range(NT):
                sl_ = selt[t % 2]
                P.op("dve", lambda h, sl_=sl_, t=t: h.tensor_scalar(sl_[:, 0:nsl], iota512[:, 0:nsl], rsh[:, t:t + 1], None, ALU.is_equal),
                     reads=["iota", "rsh"], writes=[f"selt{t % 2}"])

                def mm(h, sl_=sl_, t=t):
                    last = None
                    for c in range(KC):
                        last = h.matmul(ps[c][:, 0:nsl], xtok[:, t, c * 128:(c + 1) * 128], sl_[:, 0:nsl], start=(t == 0), stop=(t == NT - 1))
                    return last
                P.op("pe", mm, reads=[f"selt{t % 2}", f"xtok{t}"], writes=[f"ps{c}" for c in range(KC)])
            for c in range(KC):
                if c % 2 == 0:
                    P.op("act", lambda h, c=c: h.activation(xg[:, c, 0:nsl], ps[c][:, 0:nsl], AF.Copy), reads=[f"ps{c}"], writes=[f"xg{c}"])
                else:
                    P.op("dve", lambda h, c=c: h.tensor_copy(xg[:, c, 0:nsl], ps[c][:, 0:nsl]), reads=[f"ps{c}"], writes=[f"xg{c}"])
            stages = [s0]
            hidden(s0, 0, nsl)
            for fb in range(1, nblk):
                stages.append(load_w(e, fb))
                hidden(stages[fb], fb % 2, nsl)
                down(stages[fb - 1], (fb - 1) % 2, fb - 1 == 0, nst)
            down(stages[nblk - 1], (nblk - 1) % 2, False, nst)
            for st_ in range(nst):
                if st_ % 2 == 0:
                    P.op("act", lambda h, st_=st_: h.activation(yslot[:, st_, :], yacc[:, st_, :], AF.Copy),
                         reads=[f"yacc{st_}"], writes=[f"yslot{st_}"])
                else:
                    P.op("pool", lambda h, st_=st_: h.tensor_copy(yslot[:, st_, :], yacc[:, st_, :]),
                         reads=[f"yacc{st_}"], writes=[f"yslot{st_}"])
            def sc_stage1(t):
                rb_ = 4 + t % 2
                P.op("dve", lambda h, t=t: h.tensor_scalar(Rb[:], ones[:], rsh[:, t:t + 1], None, ALU.mult),
                     reads=["ones", "rsh"], writes=["Rb"])
                P.op("pe", lambda h, rb_=rb_: h.matmul(ps[rb_][:, 0:128], Rb[:], ident[:], start=True, stop=True),
                     reads=["Rb", "ident"], writes=[f"ps{rb_}"])
                sT = selT[t % 2]

                if "dvemk" in VAR:
                    def mk(h, sT=sT, rb_=rb_):
                        last = None
                        for st_ in range(nst):
                            last = h.tensor_scalar(sT[:, st_, :], ps[rb_][:, 0:128], slotid[:, st_:st_ + 1], None, ALU.is_equal)
                        return last
                    P.op("dve", mk, reads=[f"ps{rb_}", "slotid"], writes=[f"selT{t % 2}"])
                else:
                    ta = ytmp[:, t % 2, :].rearrange("p (s n) -> p s n", s=4)

                    def mk1(h, rb_=rb_, ta=ta):
                        last = None
                        for st_ in range(nst):
                            last = h.activation(ta[:, st_, :], ps[rb_][:, 0:128], AF.Abs, bias=negslot[:, st_:st_ + 1], scale=1.0)
                        return last

                    def mk2(h, sT=sT, ta=ta):
                        return h.activation(sT[:, 0:nst, :], ta[:, 0:nst, :], AF.Relu, bias=one_t, scale=-1.0)
                    P.op("act", mk1, reads=[f"ps{rb_}", "negslot"], writes=[f"sg{t % 2}"])
                    P.op("act", mk2, reads=[f"sg{t % 2}", "ones"], writes=[f"selT{t % 2}"])

            def sc_stage2(t):
                sT = selT[t % 2]
                for half in range(2):
                    ob_ = 6 + half

                    def mm(h, sT=sT, half=half, ob_=ob_):
                        last = None
                        for st_ in range(nst):
                            last = h.matmul(ps[ob_], sT[:, st_, :], yslot[:, st_, half * 512:(half + 1) * 512],
                                            start=(st_ == 0), stop=(st_ == nst - 1))
                        return last
                    P.op("pe", mm, reads=[f"selT{t % 2}"] + [f"yslot{i}" for i in range(nst)], writes=[f"ps{ob_}"])
                    xs = X[:, t, half * 512:(half + 1) * 512]
                    P.op("dve", lambda h, xs=xs, ob_=ob_, t=t: h.scalar_tensor_tensor(xs, ps[ob_], comb[:, t, e:e + 1], xs, ALU.mult, ALU.add),
                         reads=[f"ps{ob_}", f"X{t}a", "comb"], writes=[f"X{t}a"])

            sc_stage1(0)
            for t in range(NT):
                if t + 1 < NT:
                    sc_stage1(t + 1)
                sc_stage2(t)

        BLOCKS = [(0, 512), (512, 128), (640, 384), (1024, 512), (1536, 512)]
        for e in range(NEXP):
            P.regload(("pe", "act", "dve", "pool"), cnti[0:1, e:e + 1], reads=["cnti"])
            ncond = 0
            for (off, nsl) in BLOCKS:
                if off > 0:
                    P.begin_cond(off)
                    ncond += 1
                block(e, off, nsl)
            for _ in range(ncond):
                P.end_cond()

    def do_layer(layer):
        P.start_phase(f"l{layer}p1")
        wv = w_in_d[layer].rearrange("(c p) n -> p c n", p=128)
        ws = [M.at(R_B, [128, KC, 512], BF16, "ws0"), M.at(R_B + 8 * KB, [128, KC, 512], BF16, "ws1")]
        PT = [M.at(R_B + 16 * KB + i * 1280, [128, 640], BF16, f"PT{i}") for i in range(3)]
        qT = M.at(R_A, [128, 4, S], BF16, "qT")
        kT = M.at(R_A + 16 * KB, [128, 2, S], BF16, "kT")
        wload(prot[:], cst_d[:, 0:128], "c_prot", ["prot"])
        wload(maskPN[:], cst_d[:, 128:1152].rearrange("p (a n) -> p a n", a=2), "c_mask", ["maskPN"])
        P.op("sp", lambda h: h.dma_start(out=esink[:], in_=sink_d[layer].partition_broadcast(128)),
             writes=["esink"], dma="c_esink")
        P.op("act", lambda h: h.activation(esink[:], esink[:], AF.Exp), reads=["esink"], writes=["esink"])

        def do_group(grp):
            is_na = grp < 2
            base = grp * 768
            nh = 4 if is_na else 2
            if is_na:
                Va = M.at(R_A + 24 * KB, [128, NT, 4, 65], BF16, "VaNA")
                bias = M.at(R_A + 33 * KB, [128, 4, 1536], BF16, "nabias")
                wload(bias[:], nabias_d[layer][:, grp * 4:(grp + 1) * 4, :], "bias", ["bias"])
            else:
                Va = M.at(R_A + 20 * KB, [128, NT, 2, 65], BF16, "VaSW")
                cs = M.at(R_A + 24 * KB + 512, [128, 2, S], F32, "cs")
                rtmp = M.at(R_A + 41 * KB, [128, 2, 512], F32, "rtmp")
                qb = M.at(R_A + 45 * KB, [128, 512], BF16, "qb")
                qb2 = M.at(R_A + 46 * KB, [128, 512], BF16, "qb2")
                P.op("sp", lambda h: h.dma_start(out=cs[:], in_=cs_d.rearrange("a p n -> p a n")),
                     writes=["cs"], dma="c_cs")
            if grp != 1:
                P.op("pool", lambda h, Va=Va: h.memset(Va[:, :, :, 64:65], 1.0), writes=["vones"])
            wload(ws[0][:], wv[:, :, base:base + 512], "ws0", ["ws0"])
            wload(ws[1][:, :, 0:256], wv[:, :, base + 512:base + 768], "ws1", ["ws1"])

            def evac_plain(dst, scale, wres):
                def f(tb, b):
                    d = dst[:, tb * 512:(tb + 1) * 512]
                    if tb % 2 == 0:
                        P.op("act", lambda h: h.activation(d, ps[b], AF.Copy, scale=scale),
                             reads=[f"ps{b}"], writes=[wres + str(tb)])
                    else:
                        P.op("dve", lambda h: h.tensor_scalar(d, ps[b], scale, None, ALU.mult),
                             reads=[f"ps{b}"], writes=[wres + str(tb)])
                return f

            def evac_rot(dst, scale, wres):
                def f(tb, b):
                    d = dst[:, tb * 512:(tb + 1) * 512]
                    o_ = tb % 2
                    rt_ = rtmp if o_ == 0 else ytmp
                    qb_ = qb if o_ == 0 else qb2
                    pb_ = 5 if o_ == 0 else 4
                    P.op("act", lambda h: h.activation(qb_[:], ps[b], AF.Copy), reads=[f"ps{b}"], writes=[f"qb{o_}"])
                    P.op("pe", lambda h: h.matmul(ps[pb_], prot[:], qb_[:], start=True, stop=True),
                         reads=[f"qb{o_}", "prot"], writes=[f"ps{pb_}"])
                    P.op("dve", lambda h: h.scalar_tensor_tensor(rt_[:, 0, :], ps[b], scale, cs[:, 0, tb * 512:(tb + 1) * 512],
                                                                 ALU.mult, ALU.mult),
                         reads=[f"ps{b}", "cs"], writes=[f"rtmp0_{o_}"])
                    P.op("dve", lambda h: h.scalar_tensor_tensor(rt_[:, 1, :], ps[pb_], scale, cs[:, 1, tb * 512:(tb + 1) * 512],
                                                                 ALU.mult, ALU.mult),
                         reads=[f"ps{pb_}", "cs"], writes=[f"rtmp1_{o_}"])
                    P.op("pool", lambda h: h.tensor_tensor(d, rt_[:, 0, :], rt_[:, 1, :], ALU.add),
                         reads=[f"rtmp0_{o_}", f"rtmp1_{o_}"], writes=[wres + str(tb)])
                return f

            if is_na:
                for c in range(2):
                    gemm_fm(ws[0], c * 128, None, evac_plain(qT[:, c, :], 0.125, f"q{c}_"), (6, 7), "ws0", xT_res, "q")
                for c in range(2):
                    gemm_fm(ws[0], 256 + c * 128, None, evac_plain(kT[:, c, :], 1.0, f"k{c}_"), (6, 7), "ws0", xT_res, "k")
            else:
                ev = evac_plain if "norot" in VAR else evac_rot
                for c in range(4):
                    gemm_fm(ws[0], c * 128, None, ev(qT[:, c, :], 0.125, f"q{c}_"), (6, 7), "ws0", xT_res, "q")
                gemm_fm(ws[1], 0, None, ev(kT[:, 0, :], 1.0, "k0_"), (6, 7), "ws1", xT_res, "k")
            vcol0 = 0 if is_na else 128
            vn = nh * 64
            for t in range(NT):
                b = 6 + t % 2

                def mmv(h, t=t, b=b):
                    last = None
                    for k in range(KC):
                        last = h.matmul(ps[b][:, 0:vn], xT[:, k, t * 128:(t + 1) * 128], ws[1][:, k, vcol0:vcol0 + vn],
                                        start=(k == 0), stop=(k == KC - 1))
                    return last
                P.op("pe", mmv, reads=["ws1", f"xT{t}"], writes=[f"ps{b}"])
                dstv = Va[:, t, :, 0:64]
                srcv = ps[b][:, 0:vn].rearrange("p (h d) -> p h d", h=nh)
                if t % 2 == 0:
                    P.op("act", lambda h, dstv=dstv, srcv=srcv: h.activation(dstv, srcv, AF.Copy),
                         reads=[f"ps{b}", "vones"], writes=[f"v{t}"])
                else:
                    P.op("dve", lambda h, dstv=dstv, srcv=srcv: h.tensor_copy(dstv, srcv),
                         reads=[f"ps{b}", "vones"], writes=[f"v{t}"])

            if debug == f"p1proj{grp}" and layer == 0:
                dbg("dbg_q", qT[:], [128, 4, S], BF16, reads=[f"q{c}_{tb}" for c in range(4) for tb in range(4)])
                dbg("dbg_k", kT[:], [128, 2, S], BF16, reads=[f"k{c}_{tb}" for c in range(2) for tb in range(4)])
                dbg("dbg_v", Va[:], [128, NT, nh, 65], BF16, reads=[f"v{t}" for t in range(NT)])
                raise StopBuild()

            steps = []
            if is_na:
                for i in range(NT):
                    if i < 2:
                        js, unm = list(range(3, -1, -1)), True
                    elif i >= 14:
                        js, unm = list(range(15, 11, -1)), True
                    else:
                        js, unm = list(range(i + 2, i - 3, -1)), False
                    for hh in range(4):
                        steps.append((i, hh, js, unm))
            else:
                for n in range(NT):
                    for kv in range(2):
                        js = [j for j in (n - 1, n, n + 1) if 0 <= j < NT]
                        for j in js:
                            steps.append((n, kv, [j], j - n))
            nsteps = len(steps)

            def emit_S(si):
                sb = (0, 2)[si % 2]
                if is_na:
                    i, hh, js, unm = steps[si]
                    c, po = hh // 2, (hh % 2) * 64

                    def f(h):
                        last = None
                        for jj, j in enumerate(js):
                            o = PS[:, sb + jj // 4, (jj % 4) * 128:(jj % 4 + 1) * 128]
                            h.matmul(o, kT[po:po + 64, c, j * 128:(j + 1) * 128], qT[po:po + 64, c, i * 128:(i + 1) * 128],
                                     start=(jj % 4 == 0), stop=False, skip_group_check=True)
                        dl0 = js[0] - i
                        col0 = (640 + 64 * (6 - 2 * dl0)) if unm else (64 * (4 - 2 * dl0))
                        n0 = min(len(js), 4) * 128
                        last = h.matmul(PS[:, sb, 0:n0], identb[:], bias[:, hh, col0:col0 + n0], start=False, stop=True,
                                        skip_group_check=True)
                        if len(js) > 4:
                            last = h.matmul(PS[:, sb + 1, 0:128], identb[:], bias[:, hh, col0 + 512:col0 + 640], start=False, stop=True,
                                            skip_group_check=True)
                        return last
                    rd = ["bias", "identb", f"q{c}_{i // 4}"] + [f"k{c}_{j // 4}" for j in js]
                    P.op("pe", f, reads=rd, writes=[f"ps{sb}", f"ps{sb + 1}"])
                else:
                    n, kv, js, rel = steps[si]
                    j = js[0]
                    po = kv * 64

                    def f(h):
                        o = ps[sb]
                        last = h.matmul(o, kT[po:po + 64, 0, j * 128:(j + 1) * 128], qT[po:po + 64, :, n * 128:(n + 1) * 128],
                                        start=True, stop=(rel == 0))
                        if rel != 0:
                            last = h.matmul(o, identb[:], maskPN[:, 0 if rel < 0 else 1, :], start=False, stop=True)
                        return last
                    rd = ["maskPN", "identb", f"k0_{j // 4}"] + [f"q{c}_{n // 4}" for c in range(4)]
                    P.op("pe", f, reads=rd, writes=[f"ps{sb}"])

            def emit_exp(si):
                sb = (0, 2)[si % 2]
                pt = PT[si % 3]
                if is_na:
                    nj = len(steps[si][2])
                    src = PS[:, sb:sb + 2, :].rearrange("p a n -> p (a n)")[:, 0:nj * 128]
                    P.op("act", lambda h: h.activation(pt[:, 0:nj * 128], src, AF.Exp),
                         reads=[f"ps{sb}", f"ps{sb + 1}"], writes=[f"PT{si % 3}"])
                else:
                    P.op("act", lambda h: h.activation(pt[:, 0:512], ps[sb], AF.Exp),
                         reads=[f"ps{sb}"], writes=[f"PT{si % 3}"])

            def emit_PV(si):
                pt = PT[si % 3]
                if is_na:
                    i, hh, js, unm = steps[si]
                    ob = 4 + i % 2

                    def f(h):
                        last = None
                        for jj, j in enumerate(js):
                            last = h.matmul(ps[ob][:, hh * 65:(hh + 1) * 65], pt[:, jj * 128:(jj + 1) * 128], Va[:, j, hh, :],
                                            start=(jj == 0), stop=(jj == len(js) - 1), skip_group_check=True)
                        return last
                    P.op("pe", f, reads=[f"PT{si % 3}"] + [f"v{j}" for j in js], writes=[f"ps{ob}"])
                    if hh == 3:
                        emit_norm(i, [(ob, 4)])
                else:
                    n, kv, js, rel = steps[si]
                    j = js[0]
                    obs = (4, 5) if n % 2 == 0 else (1, 3)
                    ob = obs[kv]
                    first = (j == max(0, n - 1))
                    lastj = (j == min(NT - 1, n + 1))

                    def f(h):
                        last = None
                        for g in range(4):
                            last = h.matmul(ps[ob][:, g * 65:(g + 1) * 65], pt[:, g * 128:(g + 1) * 128], Va[:, j, kv, :],
                                            start=(first and g == 0), stop=lastj, skip_group_check=True)
                        return last
                    P.op("pe", f, reads=[f"PT{si % 3}", f"v{j}"], writes=[f"ps{ob}"])
                    if kv == 1 and lastj:
                        emit_norm(n, [(obs[0], 4), (obs[1], 4)])

            def emit_norm(i, pieces):
                sl = i % 2
                nheads = sum(p_[1] for p_ in pieces)
                h0 = 0
                for (ob, nh_) in pieces:
                    o3 = ps[ob][:, 0:nh_ * 65].rearrange("p (h d) -> p h d", d=65)
                    rd_ = rden[:, sl, h0:h0 + nh_]
                    if is_na:
                        P.op("dve", lambda h, rd_=rd_, o3=o3: h.reciprocal(rd_, o3[:, :, 64]),
                             reads=[f"ps{ob}"], writes=[f"rden{sl}_{h0}"])
                    else:
                        P.op("dve", lambda h, rd_=rd_, o3=o3, h0=h0, nh_=nh_: h.tensor_tensor(rd_, o3[:, :, 64], esink[:, h0:h0 + nh_], ALU.add),
                             reads=[f"ps{ob}", "esink"], writes=[f"rden{sl}_{h0}"])
                        P.op("dve", lambda h, rd_=rd_: h.reciprocal(rd_, rd_), reads=[f"rden{sl}_{h0}"], writes=[f"rden{sl}_{h0}"])

                    def fn(h, o3=o3, h0=h0, nh_=nh_):
                        last = None
                        for hd in range(nh_):
                            last = h.tensor_scalar(ytmp[:, sl, (h0 + hd) * 64:(h0 + hd + 1) * 64], o3[:, hd, 0:64],
                                                   rden[:, sl, h0 + hd:h0 + hd + 1], None, ALU.mult)
                        return last
                    P.op("dve", fn, reads=[f"ps{ob}", f"rden{sl}_{h0}"], writes=[f"ytmp{sl}_{h0}"])
                    h0 += nh_
                nchunk = nheads // 2
                tb_ = 6 + i % 2
                yt = ytmp[:, sl, :]

                def tp(h):
                    last = None
                    for c in range(nchunk):
                        last = h.transpose(ps[tb_][:, c * 128:(c + 1) * 128], yt[:, c * 128:(c + 1) * 128], ident[:])
                    return last
                P.op("pe", tp, reads=[f"ytmp{sl}_{hh_}" for hh_ in range(0, nheads, 4)] + ["ident"], writes=[f"ps{tb_}"])
                ydst = yT[0 if is_na else 1]
                c0 = grp * 2 if is_na else 0
                dst = ydst[:, c0:c0 + nchunk, i * 128:(i + 1) * 128]
                srcp = ps[tb_][:, 0:nchunk * 128].rearrange("p (c n) -> p c n", c=nchunk)
                P.op("act", lambda h: h.activation(dst, srcp, AF.Copy), reads=[f"ps{tb_}"], writes=[f"yT{i}"])

            if debug and debug.startswith("att"):
                _, g_, n_, what = debug.split("_")
                if int(g_) == grp:
                    for si in range(int(n_)):
                        emit_S(si)
                        if "E" in what:
                            emit_exp(si)
                        if "P" in what:
                            emit_PV(si)
                    raise StopBuild()
            emit_S(0)
            emit_exp(0)
            for si in range(nsteps):
                if si + 1 < nsteps:
                    emit_S(si + 1)
                    emit_exp(si + 1)
                emit_PV(si)
        for grp in range(3):
            do_group(grp)
            if grp >= 1:
                P.barrier()
        if debug == "p1" and layer == 0:
            dbg("dbg_yna", yT[0][:], [128, 4, S], BF16, reads=[])
            dbg("dbg_ysw", yT[1][:], [128, 4, S], BF16, reads=[])
            raise StopBuild()

        P.start_phase(f"l{layer}p2")
        zT = M.at(R_A, [128, KC, S], BF16, "zT")
        wo = M.at(R_A + 32 * KB, [128, KC, D], BF16, "wo")
        p2w = [M.at(R_B + i * 6 * KB, [128, 24, 128], BF16, f"p2w{i}") for i in range(2)]
        load_gb(1 + 2 * layer)
        P.op("sp", lambda h: h.dma_start(out=bgate[:], in_=bgate_d[layer]), writes=["bgate"], dma="c_bgate")
        wov = wout_d[layer].rearrange("(c p) n -> p c n", p=128)
        wbv = [wbr_d[layer, br].rearrange("(c p) n -> p c n", p=128) for br in range(2)]

        def load_p2w(c):
            s_ = c % 2
            cs_ = slice(c * 128, (c + 1) * 128)
            wload(p2w[s_][:, 0:4, :], wbv[0][:, :, cs_], f"p2w{s_}", [])
            wload(p2w[s_][:, 4:8, :], wbv[1][:, :, cs_], f"p2w{s_}", [])
            wload(p2w[s_][:, 8:16, :], wv[:, :, 2304 + c * 128:2304 + (c + 1) * 128], f"p2w{s_}", [])
            wload(p2w[s_][:, 16:24, :], wv[:, :, 3328 + c * 128:3328 + (c + 1) * 128], f"p2w{s_}", [f"p2w{s_}"])

        def load_p2w(c):
            s_ = c % 2
            cs_ = slice(c * 128, (c + 1) * 128)
            for (r0, r1, srcap) in ((0, 4, wbv[0][:, :, cs_]), (4, 8, wbv[1][:, :, cs_]),
                                    (8, 16, wv[:, :, 2304 + c * 128:2304 + (c + 1) * 128]),
                                    (16, 24, wv[:, :, 3328 + c * 128:3328 + (c + 1) * 128])):
                wload(p2w[s_][:, r0:r1, :], srcap, f"p2w{s_}", [f"p2w{s_}"])

        load_p2w(0)
        load_p2w(1)
        wload(wo[:, :, 0:512], wov[:, :, 0:512], "wo", ["wo"])
        wload(wo[:, :, 512:1024], wov[:, :, 512:1024], "wo", ["wo"])
        step = 0
        for c in range(KC):
            s_ = c % 2
            w_ = p2w[s_]
            for tb in range(4):
                bk = (0, 1, 2, 3) if step % 2 == 0 else (4, 5, 6, 7)
                tsl = slice(tb * 512, (tb + 1) * 512)

                def mm(h, w_=w_, bk=bk, tsl=tsl):
                    last = None
                    for k in range(4):
                        last = h.matmul(ps[bk[0]], w_[:, k, :], yT[0][:, k, tsl], start=(k == 0), stop=(k == 3))
                    for k in range(4):
                        last = h.matmul(ps[bk[1]], w_[:, 4 + k, :], yT[1][:, k, tsl], start=(k == 0), stop=(k == 3))
                    for k in range(KC):
                        last = h.matmul(ps[bk[2]], w_[:, 8 + k, :], xT[:, k, tsl], start=(k == 0), stop=(k == KC - 1))
                    for k in range(KC):
                        last = h.matmul(ps[bk[3]], w_[:, 16 + k, :], xT[:, k, tsl], start=(k == 0), stop=(k == KC - 1))
                    return last
                P.op("pe", mm, reads=[f"p2w{s_}"], writes=[f"ps{b_}" for b_ in bk])
                t0_, t1_ = ytmp[:, 0, :], ytmp[:, 1, :]
                P.op("act", lambda h, bk=bk, c=c: h.activation(t0_, ps[bk[2]], AF.Sigmoid, bias=bgate[:, c:c + 1], scale=1.0),
                     reads=[f"ps{bk[2]}", "bgate"], writes=["g0"])
                P.op("dve", lambda h, bk=bk: h.tensor_tensor(t0_, ps[bk[0]], t0_, ALU.mult),
                     reads=[f"ps{bk[0]}", "g0"], writes=["g0"])
                P.op("act", lambda h, bk=bk, c=c: h.activation(t1_, ps[bk[3]], AF.Sigmoid, bias=bgate[:, 8 + c:9 + c], scale=1.0),
                     reads=[f"ps{bk[3]}", "bgate"], writes=["g1"])
                P.op("dve", lambda h, bk=bk: h.tensor_tensor(t1_, ps[bk[1]], t1_, ALU.mult),
                     reads=[f"ps{bk[1]}", "g1"], writes=["g1"])
                P.op("pool", lambda h, c=c, tsl=tsl: h.tensor_tensor(zT[:, c, tsl], t0_, t1_, ALU.add),
                     reads=["g0", "g1"], writes=[f"z{tb}"])
                step += 1
            if c + 2 < KC:
                load_p2w(c + 2)

        if debug == "p2z" and layer == 0:
            dbg("dbg_z", zT[:], [128, KC, S], BF16, reads=[f"z{tb}" for tb in range(4)])
            raise StopBuild()

        x32 = M.at(R_Y, [128, KC, 128], F32, "x32")
        xtok = M.at(64 * KB, [128, NT, D], BF16, "xtok")
        if layer % 2 == 1:
            P.barrier()
        if layer % 2 == 1:
            P.op("sp", lambda h: h.dma_start(out=wr[:], in_=wr_d.rearrange("(c p) e -> p c e", p=128)),
                 writes=["wr"], dma="c_wr")
        def wo_pre(t):
            gb_ = (0, 1) if t % 2 == 0 else (2, 3)

            def mmo(h, t=t, gb_=gb_):
                last = None
                for half in range(2):
                    for k in range(KC):
                        last = h.matmul(ps[gb_[half]], zT[:, k, t * 128:(t + 1) * 128], wo[:, k, half * 512:(half + 1) * 512],
                                        start=(k == 0), stop=(k == KC - 1))
                return last
            P.op("pe", mmo, reads=["wo", f"z{t // 4}"], writes=[f"ps{gb_[0]}", f"ps{gb_[1]}"])
            for half in range(2):
                xs = X[:, t, half * 512:(half + 1) * 512]
                P.op("dve", lambda h, xs=xs, b_=gb_[half]: h.scalar_tensor_tensor(xs, xs, ALPHA, ps[b_], ALU.mult, ALU.add),
                     reads=[f"ps{gb_[half]}", f"X{t}"], writes=[f"X{t}"])
            ln_A(t)

        wo_pre(0)
        for t in range(NT):
            if t + 1 < NT:
                wo_pre(t + 1)
            if layer % 2 == 1:
                ln_B(t, (4, 5) if t % 2 == 0 else (6, 7), "act" if t % 2 == 0 else "dve",
                     router=x32, do_T=False, xtok=xtok)
            else:
                ln_B(t, (4, 5) if t % 2 == 0 else (6, 7), "act" if t % 2 == 0 else "dve")
        if layer % 2 == 1:
            router_finish()
        P.barrier()
        if debug == "p2" and layer == 0:
            dbg("dbg_X", X[:], [128, NT, D], F32, reads=[])
            dbg("dbg_xT", xT[:], [128, KC, S], BF16, reads=[])
            raise StopBuild()

        P.start_phase(f"l{layer}p3")
        load_gb(2 + 2 * layer)
        for t in range(NT):
            P.op("act", lambda h, t=t: h.activation(X[:, t, :], X[:, t, :], AF.Copy, scale=ALPHA),
                 reads=[f"X{t}"], writes=[f"X{t}"])
        if layer % 2 == 0 or "densemoe" in VAR:
            if layer % 2 == 0:
                passes = [(fg_d, fu_d, fd_d, FF_DENSE, None)]
            else:
                passes = [(mg_d[e], mu_d[e], md_d[e], FF_EXP, e) for e in range(NEXP)]
            items = []
            for (g_d, u_d, d_d, F_, e_) in passes:
                nblk = (F_ + 511) // 512
                for fb in range(nblk):
                    items.append((g_d, u_d, d_d, fb, min(4, (F_ - fb * 512) // 128), e_))
            wst = [dict(g=M.at(R_A + s_ * 24 * KB, [128, KC, 512], BF16, f"wg{s_}"),
                        u=M.at(R_A + s_ * 24 * KB + 8 * KB, [128, KC, 512], BF16, f"wu{s_}"),
                        d=M.at(R_A + s_ * 24 * KB + 16 * KB, [128, 4, D], BF16, f"wd{s_}")) for s_ in range(2)]
            hT = [M.at(R_Y + s_ * 16 * KB, [128, 4, S], BF16, f"hT{s_}") for s_ in range(2)]

            def ffn_load(ii):
                g_d, u_d, d_d, fb, nfc, e_ = items[ii]
                s_ = ii % 2
                n_ = nfc * 128
                gv = g_d.rearrange("(c p) n -> p c n", p=128)
                uv = u_d.rearrange("(c p) n -> p c n", p=128)
                dv = d_d[fb * 512:fb * 512 + n_, :].rearrange("(c p) n -> p c n", p=128)
                wload(wst[s_]["g"][:, :, 0:n_], gv[:, :, fb * 512:fb * 512 + n_], f"wg{s_}", [f"wg{s_}"])
                wload(wst[s_]["u"][:, :, 0:n_], uv[:, :, fb * 512:fb * 512 + n_], f"wu{s_}", [f"wu{s_}"])
                for hf_ in range(2):
                    wload(wst[s_]["d"][:, 0:nfc, hf_ * 512:(hf_ + 1) * 512], dv[:, :, hf_ * 512:(hf_ + 1) * 512], f"wd{s_}", [f"wd{s_}"])

            hstep = {"n": 0}

            def ffn_hidden(ii):
                g_d, u_d, d_d, fb, nfc, e_ = items[ii]
                s_ = ii % 2
                for fc in range(nfc):
                    for tb in range(4):
                        n = hstep["n"]
                        hstep["n"] += 1
                        bg, bu = (0, 1) if n % 2 == 0 else (2, 3)
                        tsl = slice(tb * 512, (tb + 1) * 512)

                        def mm(h, fc=fc, tsl=tsl, bg=bg, bu=bu, s_=s_):
                            last = None
                            for k in range(KC):
                                last = h.matmul(ps[bg], wst[s_]["g"][:, k, fc * 128:(fc + 1) * 128], xT[:, k, tsl],
                                                start=(k == 0), stop=(k == KC - 1))
                            for k in range(KC):
                                last = h.matmul(ps[bu], wst[s_]["u"][:, k, fc * 128:(fc + 1) * 128], xT[:, k, tsl],
                                                start=(k == 0), stop=(k == KC - 1))
                            return last
                        P.op("pe", mm, reads=[f"wg{s_}", f"wu{s_}"], writes=[f"ps{bg}", f"ps{bu}"])
                        tmp = ytmp[:, n % 2, :]
                        P.op("act", lambda h, tmp=tmp, bg=bg: h.activation(tmp, ps[bg], AF.Silu),
                             reads=[f"ps{bg}"], writes=[f"sg{n % 2}"])
                        P.op("dve", lambda h, tmp=tmp, bu=bu, fc=fc, tsl=tsl, s_=s_: h.tensor_tensor(hT[s_][:, fc, tsl], ps[bu], tmp, ALU.mult),
                             reads=[f"ps{bu}", f"sg{n % 2}"], writes=[f"hT{s_}_{tb}"])

            dstep = {"n": 0}

            def ffn_down(ii):
                g_d, u_d, d_d, fb, nfc, e_ = items[ii]
                s_ = ii % 2
                for t in range(NT):
                    for half in range(2):
                        n = dstep["n"]
                        dstep["n"] += 1
                        b_ = 4 + n % 4

                        def mm(h, t=t, half=half, b_=b_, s_=s_, nfc=nfc):
                            last = None
                            for fc in range(nfc):
                                last = h.matmul(ps[b_], hT[s_][:, fc, t * 128:(t + 1) * 128], wst[s_]["d"][:, fc, half * 512:(half + 1) * 512],
                                                start=(fc == 0), stop=(fc == nfc - 1))
                            return last
                        P.op("pe", mm, reads=[f"wd{s_}", f"hT{s_}_{t // 4}"], writes=[f"ps{b_}"])
                        xs = X[:, t, half * 512:(half + 1) * 512]
                        sc_ = 1.0 if e_ is None else comb[:, t, e_:e_ + 1]
                        P.op("dve", lambda h, xs=xs, b_=b_, sc_=sc_: h.scalar_tensor_tensor(xs, ps[b_], sc_, xs, ALU.mult, ALU.add),
                             reads=[f"ps{b_}", f"X{t}", "comb"], writes=[f"X{t}"])

            ffn_load(0)
            if len(items) > 1:
                ffn_load(1)
            for ii in range(len(items)):
                ffn_hidden(ii)
                if ii > 0:
                    ffn_down(ii - 1)
                    if ii + 1 < len(items):
                        ffn_load(ii + 1)
            ffn_down(len(items) - 1)

        else:
            moe_routed(layer)
            P.barrier()
        ln_A(0)
        for t in range(NT):
            if t + 1 < NT:
                ln_A(t + 1)
            ln_B(t, (0, 1) if t % 2 == 0 else (2, 3), "act" if t % 2 == 0 else "dve", do_T=(layer + 1 < DEPTH))
            if layer + 1 == DEPTH and debug is None:
                P.op("sp", lambda h, t=t: h.dma_start(out=out_d[t * 128:(t + 1) * 128, :], in_=X[:, t, :]),
                     reads=[f"X{t}"], dma="out")
        P.barrier()
        if debug == f"l{layer}":
            dbg("dbg_X", X[:], [128, NT, D], F32, reads=[])
            raise StopBuild()

    try:
        for layer in range(n_layers):
            do_layer(layer)
    except StopBuild:
        P.barrier()

    P.start_phase("pout")
    if debug is not None or n_layers < DEPTH:
        for t in range(NT):
            P.op("sp", lambda h, t=t: h.dma_start(out=out_d[t * 128:(t + 1) * 128, :], in_=X[:, t, :]),
                 reads=[f"X{t}"], dma="out")
    P.final_wait("sp")
    P.flush()
    return nc, dbg_outs


def _perm_w_in(w_in):
    idx = []
    for hf in range(2):
        idx += list(range(256 * hf, 256 * hf + 256))
        idx += list(range(512 + 256 * hf, 512 + 256 * hf + 256))
        idx += list(range(1024 + 256 * hf, 1024 + 256 * hf + 256))
    for c in range(4):
        idx += list(range(1536 + c * 64, 1536 + c * 64 + 64))
        idx += list(range(1536 + (c + 4) * 64, 1536 + (c + 4) * 64 + 64))
    idx += list(range(2048, 4352))
    return np.ascontiguousarray(w_in[:, :, np.asarray(idx)])


def _na_bias_tables(rpb):
    L = rpb.shape[0]
    kc = np.arange(64)[:, None]
    qc = np.arange(64)[None, :]
    qcs = np.clip(qc - 8, 0, 48)
    colvalid = (kc >= qcs) & (kc < qcs + 16)
    dc = np.clip(kc - qc + 15, 0, 30)
    out = np.full((L, 128, 8, 1536), NEG, np.float32)

    def Cmat(l, h, e, masked):
        if e < -7 or e > 7 or (masked and not (-4 <= e <= 3)):
            return np.full((64, 64), NEG, np.float32)
        return np.where(colvalid, rpb[l, h, e + 7][dc], NEG).astype(np.float32)

    for l in range(L):
        for h in range(8):
            for masked, e_hi, n, col0 in ((True, 4, 10, 0), (False, 6, 14, 640)):
                for idx in range(n):
                    e = e_hi - idx
                    for kl in range(2):
                        out[l, kl * 64:(kl + 1) * 64, h, col0 + idx * 64:col0 + (idx + 1) * 64] = Cmat(l, h, e + kl, masked)
    return out


def _const_tables():
    pos = np.arange(S, dtype=np.float32)
    inv_freq = (1.0 / (500000.0 ** (np.arange(0, 16, 2, dtype=np.float32) / 16.0))).astype(np.float32)
    ang = pos[:, None] * inv_freq[None, :]
    cos = np.cos(ang).astype(np.float32).T
    sin = np.sin(ang).astype(np.float32).T
    cs = np.zeros((2, 128, S), np.float32)
    cs[0] = 1.0
    for half in range(2):
        o = half * 64
        cs[0, o:o + 8] = cos
        cs[0, o + 8:o + 16] = cos
        cs[1, o:o + 8] = -sin
        cs[1, o + 8:o + 16] = sin
    cst = np.zeros((128, 1152), np.float32)
    for m in range(128):
        d = m % 64
        partner = m + 8 if d < 8 else (m - 8 if d < 16 else m)
        cst[partner, m] = 1.0
    ki = np.arange(128)[:, None]
    qi = np.arange(128)[None, :]
    mP = np.where(ki >= qi, 0.0, NEG).astype(np.float32)
    mN = np.where(ki <= qi, 0.0, NEG).astype(np.float32)
    cst[:, 128:640] = np.tile(mP, (1, 4))
    cst[:, 640:1152] = np.tile(mN, (1, 4))
    return cs, cst


def _const2():
    c = np.zeros((128, 644), np.float32)
    k = np.arange(128)[:, None]
    m = np.arange(128)[None, :]
    c[:, 0:128] = (k < m).astype(np.float32)
    c[:, 128:132] = np.arange(4)[None, :] * 128 + np.arange(128)[:, None]
    c[:, 132:644] = np.arange(512)[None, :]
    return c


def make_in_maps(inputs):
    f = lambda a: np.ascontiguousarray(np.asarray(a, dtype=np.float32))
    x = f(inputs["x"])
    lnp = np.stack([
        np.stack([f(inputs["emb_ln_g"]), f(inputs["emb_ln_b"])]),
        np.stack([f(inputs["ln1_g"])[0], f(inputs["ln1_b"])[0]]),
        np.stack([f(inputs["ln2_g"])[0], f(inputs["ln2_b"])[0]]),
        np.stack([f(inputs["ln1_g"])[1], f(inputs["ln1_b"])[1]]),
        np.stack([f(inputs["ln2_g"])[1], f(inputs["ln2_b"])[1]]),
    ]).astype(np.float32)
    ident = np.eye(128, dtype=np.float32)
    cs, cst = _const_tables()
    shared = {"lnp": lnp, "ident": ident, "w_in": _perm_w_in(f(inputs["w_in"])),
              "nabias": _na_bias_tables(f(inputs["na_rpb"])), "cs": cs, "cst": cst,
              "sink": f(inputs["sw_sink"]),
              "bgate": np.ascontiguousarray(f(inputs["b_gate"]).reshape(DEPTH, 16, 128).transpose(0, 2, 1)),
              "wbr": np.ascontiguousarray(np.stack([f(inputs["w_branch_na"]), f(inputs["w_branch_sw"])], axis=1)),
              "wout": f(inputs["w_out"]),
              "ffn_g": f(inputs["ffn_w_gate"])[0], "ffn_u": f(inputs["ffn_w_up"])[0], "ffn_d": f(inputs["ffn_w_down"])[0],
              "moe_g": f(inputs["moe_w_gate"])[0], "moe_u": f(inputs["moe_w_up"])[0], "moe_d": f(inputs["moe_w_down"])[0],
              "wr": f(inputs["moe_router"])[0], "cst2": _const2()}
    maps = []
    for c in range(NCORES):
        m = dict(shared)
        m["x"] = np.ascontiguousarray(x[c])
        maps.append(m)
    return maps


_CACHE = {}


def kernel(**inputs):
    if "nc" not in _CACHE:
        _CACHE["nc"] = build()[0]
    nc = _CACHE["nc"]
    in_maps = make_in_maps(inputs)
    res = run_bass_kernel_spmd(nc, in_maps, core_ids=list(range(NCORES)))
    out = np.stack([np.asarray(r["out"], dtype=np.float32).reshape(S, D) for r in res.results], axis=0)
    return out
```

```python
import numpy as np
import ml_dtypes
import concourse.bass as bass
import concourse.mybir as mybir
from concourse.bass_utils import run_bass_kernel_spmd

F32 = mybir.dt.float32
BF16 = mybir.dt.bfloat16
AF = mybir.ActivationFunctionType
ALU = mybir.AluOpType
AX = mybir.AxisListType

NCORES = 8
D = 1024
S = 2048
NT = 16
KC = 8
DEPTH = 2
ALPHA = (2 * DEPTH) ** 0.25
EPS = 1e-5
PROJ = 4352
FF_DENSE = 2816
FF_EXP = 3584
NEXP = 8
NEG = -30000.0
import os
VAR = os.environ.get('KVAR', '')


class Prog:
    ENGS = ("pe", "act", "dve", "pool", "sp")

    def __init__(self, nc):
        self.nc = nc
        self.h = {"pe": nc.tensor, "act": nc.scalar, "dve": nc.vector, "pool": nc.gpsimd, "sp": nc.sync}
        self.q = {e: [] for e in self.ENGS}
        self.sem = {}
        self.cnt = {}
        self.waited = {e: {} for e in self.ENGS}
        self.res = {}
        self.dma_sems = {}
        self._ctx = []
        self.cond = None
        self.regs = {}

    def new_sem(self, name):
        cm = self.nc.semaphore(name)
        s = cm.__enter__()
        self._ctx.append(cm)
        return s

    def start_phase(self, name):
        for e in self.ENGS:
            self.sem[e] = self.new_sem(f"{name}_{e}")
            self.cnt[e] = 0

    def dma_sem(self, key):
        if key not in self.dma_sems:
            self.dma_sems[key] = [self.new_sem(f"dma_{key}"), 0]
        return self.dma_sems[key]

    def _need(self, eng, tok):
        sem, val, src = tok
        w = self.waited[eng]
        k = id(sem)
        if w.get(k, 0) >= val:
            return False
        w[k] = val
        return True

    def op(self, eng, fn, reads=(), writes=(), dma=None):
        writes = list(writes) + [r for r in reads if r.startswith("ps") and r not in writes]
        reads = [r for r in reads if not r.startswith("ps")]
        toks = []
        for r in reads:
            st = self.res.get(r)
            if st and st["w"] is not None:
                toks.append(st["w"])
        for w_ in writes:
            st = self.res.get(w_)
            if st:
                if st["w"] is not None and (st["w"][2] != eng or st["w"][3]):
                    toks.append(st["w"])
                for t in st["r"]:
                    if t[2] != eng or t[3]:
                        toks.append(t)
        waits = []
        for t in toks:
            if self._need(eng, t[:3]):
                waits.append((t[0], t[1]))
        if dma is not None:
            ds = self.dma_sem(dma)
            ds[1] += 16
            tok = (ds[0], ds[1], eng, True)
            sem, inc = ds[0], 16
        else:
            self.cnt[eng] += 1
            tok = (self.sem[eng], self.cnt[eng], eng, False)
            sem, inc = self.sem[eng], 1

        def emit(h, waits=waits, fn=fn, sem=sem, inc=inc):
            for s_, v_ in waits:
                h.wait_ge(s_, v_)
            fn(h).then_inc(sem, inc)

        if self.cond is not None:
            self.cond["q"][eng].append(emit)
            d_ = self.cond["inc"][eng]
            d_[id(sem)] = (sem, d_.get(id(sem), (sem, 0))[1] + inc)
        else:
            self.q[eng].append(emit)
        for r in reads:
            self.res.setdefault(r, {"w": None, "r": []})["r"].append(tok)
        for w_ in writes:
            self.res[w_] = {"w": tok, "r": []}
        return tok

    def get_reg(self, eng, h):
        if eng not in self.regs:
            self.regs[eng] = h.alloc_register(f"nreg_{eng}")
        return self.regs[eng]

    def regload(self, engs, ap, reads):
        for e in engs:
            self.op(e, lambda h, e=e: h.reg_load(self.get_reg(e, h), ap), reads=reads)

    def begin_cond(self, thr):
        import copy
        pre = {id(self.sem[e]): self.cnt[e] for e in self.ENGS}
        for k, (s_, c_) in self.dma_sems.items():
            pre[id(s_)] = c_
        c = {"thr": thr, "q": {e: [] for e in self.ENGS}, "inc": {e: {} for e in self.ENGS},
             "waited": copy.deepcopy(self.waited), "pre": pre, "parent": self.cond}
        self.cond = c

    def end_cond(self):
        c = self.cond
        parent = c["parent"]
        self.cond = parent
        for e in self.ENGS:
            q = c["q"][e]
            if not q:
                continue
            incs = list(c["inc"][e].values())

            def emit(h, q=q, incs=incs, e=e, thr=c["thr"], pre=c["pre"]):
                v = h.snap(self.get_reg(e, h), min_val=0, max_val=4096)
                with h.If(v > thr):
                    for f in q:
                        f(h)
                with h.Else():
                    for (s_, n_) in incs:
                        p_ = pre.get(id(s_), 0)
                        if p_ > 0:
                            h.wait_ge(s_, p_)
                    for (s_, n_) in incs:
                        h.sem_inc(s_, n_)
            if parent is not None:
                parent["q"][e].append(emit)
                d_ = parent["inc"][e]
                for (s_, n_) in incs:
                    d_[id(s_)] = (s_, d_.get(id(s_), (s_, 0))[1] + n_)
            else:
                self.q[e].append(emit)
        self.waited = c["waited"]

    def barrier(self):
        toks = []
        for e in self.ENGS:
            if self.cnt[e] > 0:
                toks.append((self.sem[e], self.cnt[e], e))
        for k, (s_, c_) in self.dma_sems.items():
            if c_ > 0:
                toks.append((s_, c_, "dma"))
        for e in self.ENGS:
            waits = [(t[0], t[1]) for t in toks if t[2] != e and self._need(e, t)]
            if waits:
                def emit(h, waits=waits):
                    for s_, v_ in waits:
                        h.wait_ge(s_, v_)
                self.q[e].append(emit)
        self.res = {}

    def final_wait(self, eng="sp"):
        waits = [(s_, c_) for k, (s_, c_) in self.dma_sems.items() if c_ > 0]

        def emit(h, waits=waits):
            for s_, v_ in waits:
                h.wait_ge(s_, v_)
        self.q[eng].append(emit)

    def flush(self):
        nc = self.nc
        with nc.Block() as block:
            for e, reg in (("sp", block.sync), ("pool", block.gpsimd), ("pe", block.tensor),
                           ("act", block.scalar), ("dve", block.vector)):
                q = self.q[e]
                if not q:
                    continue

                def body(h, q=q):
                    for f in q:
                        f(h)
                reg(body)
        for cm in reversed(self._ctx):
            cm.__exit__(None, None, None)


class StopBuild(Exception):
    pass


class Mem:
    def __init__(self, nc):
        self.nc = nc
        self.base = (nc.sbuf_base + 63) // 64 * 64
        self.top = nc.sbuf_top
        self.n = 0

    def at(self, off, shape, dtype, name=None):
        self.n += 1
        nbytes = int(np.prod(shape[1:])) * (2 if dtype == BF16 else 4)
        assert self.base + off + nbytes <= self.top, (name, off, nbytes, self.top - self.base)
        return self.nc.alloc_sbuf_tensor_at(name or f"t{self.n}", list(shape), dtype, offset=self.base + off)


def build(debug=None, n_layers=DEPTH):
    nc = bass.Bass("TRN2", target_bir_lowering=False)
    P = Prog(nc)
    M = Mem(nc)
    dbg_outs = {}

    x_d = nc.dram_tensor("x", [S, D], F32, kind="ExternalInput").ap()
    lnp_d = nc.dram_tensor("lnp", [5, 2, D], F32, kind="ExternalInput").ap()
    ident_d = nc.dram_tensor("ident", [128, 128], F32, kind="ExternalInput").ap()
    out_d = nc.dram_tensor("out", [S, D], F32, kind="ExternalOutput").ap()
    w_in_d = nc.dram_tensor("w_in", [DEPTH, D, PROJ], F32, kind="ExternalInput").ap()
    nabias_d = nc.dram_tensor("nabias", [DEPTH, 128, 8, 1536], F32, kind="ExternalInput").ap()
    cs_d = nc.dram_tensor("cs", [2, 128, S], F32, kind="ExternalInput").ap()
    cst_d = nc.dram_tensor("cst", [128, 1152], F32, kind="ExternalInput").ap()
    sink_d = nc.dram_tensor("sink", [DEPTH, 8], F32, kind="ExternalInput").ap()
    bgate_d = nc.dram_tensor("bgate", [DEPTH, 128, 16], F32, kind="ExternalInput").ap()
    wbr_d = nc.dram_tensor("wbr", [DEPTH, 2, 512, D], F32, kind="ExternalInput").ap()
    wout_d = nc.dram_tensor("wout", [DEPTH, D, D], F32, kind="ExternalInput").ap()
    fg_d = nc.dram_tensor("ffn_g", [D, FF_DENSE], F32, kind="ExternalInput").ap()
    fu_d = nc.dram_tensor("ffn_u", [D, FF_DENSE], F32, kind="ExternalInput").ap()
    fd_d = nc.dram_tensor("ffn_d", [FF_DENSE, D], F32, kind="ExternalInput").ap()
    mg_d = nc.dram_tensor("moe_g", [NEXP, D, FF_EXP], F32, kind="ExternalInput").ap()
    mu_d = nc.dram_tensor("moe_u", [NEXP, D, FF_EXP], F32, kind="ExternalInput").ap()
    md_d = nc.dram_tensor("moe_d", [NEXP, FF_EXP, D], F32, kind="ExternalInput").ap()
    wr_d = nc.dram_tensor("wr", [D, NEXP], F32, kind="ExternalInput").ap()
    cst2_d = nc.dram_tensor("cst2", [128, 128 + 4 + 512], F32, kind="ExternalInput").ap()

    KB = 1024
    X = M.at(0, [128, NT, D], F32, "X")
    xT = M.at(64 * KB, [128, KC, S], BF16, "xT")
    R_Y = 96 * KB
    R_A = 128 * KB
    R_B = 176 * KB
    R_M = 196 * KB
    gb = M.at(R_B + 12 * KB, [128, 2, D], F32, "gb")
    mo = {"o": R_M}

    def misc(shape, dtype, name):
        nb = int(np.prod(shape[1:])) * (2 if dtype == BF16 else 4)
        t = M.at(mo["o"], shape, dtype, name)
        mo["o"] += (nb + 63) // 64 * 64
        return t
    ident = misc([128, 128], F32, "ident")
    identb = misc([128, 128], BF16, "identb")
    stats = misc([128, 2, 12], F32, "stats")
    mv = misc([128, 2, 8], F32, "mv")
    eps_t = misc([128, 4], F32, "eps")
    esink = misc([128, 8], F32, "esink")
    rden = misc([128, 2, 8], F32, "rden")
    ytmp = misc([128, 2, 512], F32, "ytmp")
    maskPN = misc([128, 2, 512], BF16, "maskPN")
    prot = misc([128, 128], BF16, "prot")
    bgate = misc([128, 16], F32, "bgate")
    wr = misc([128, KC, NEXP], F32, "wr")
    comb = misc([128, NT, NEXP], F32, "comb")
    rt = misc([128, 4, 8], F32, "rt")
    maskt = misc([128, NT, NEXP], F32, "maskt")
    rank = misc([128, NT, NEXP], F32, "rank")
    rankm = misc([128, NT, NEXP], F32, "rankm")
    offs = misc([128, NEXP], F32, "offs")
    rsh = misc([128, NT], F32, "rsh")
    cntf = misc([128, NEXP], F32, "cntf")
    cnti = misc([128, NEXP], mybir.dt.int32, "cnti")
    slotid = misc([128, 4], F32, "slotid")
    negslot = misc([128, 4], F32, "negslot")
    ltri = misc([128, 128], F32, "ltri")
    ones = misc([128, 128], F32, "ones")
    Rb = misc([128, 128], F32, "Rb")

    PS = nc.alloc_psum_tensor("PS", [128, 8, 512], F32)
    ps = [PS[:, i, :] for i in range(8)]

    def dbg(name, src_ap, shape, dtype, reads):
        t = nc.dram_tensor(name, list(shape), dtype, kind="ExternalOutput").ap()
        dbg_outs[name] = (shape, dtype)
        P.op("sp", lambda h: h.dma_start(out=t, in_=src_ap), reads=reads, dma="dbg")

    def load_gb(idx):
        src = lnp_d[idx].partition_broadcast(128) if hasattr(lnp_d[idx], "partition_broadcast") else None
        P.op("sp", lambda h: h.dma_start(out=gb[:], in_=src), writes=["gb"], dma="gb")

    def ln_tile(t, tp_banks, evac_eng, router=None, do_T=True, xtok=None):
        ln_A(t)
        ln_B(t, tp_banks, evac_eng, router, do_T, xtok)

    def ln_A(t):
        xt = X[:, t, :]
        sl = t % 2
        st = stats[:, sl, :]
        m = mv[:, sl, :]
        rx = f"X{t}"
        rs = f"st{sl}"
        P.op("dve", lambda h: (h.bn_stats(st[:, 0:6], xt[:, 0:512]), h.bn_stats(st[:, 6:12], xt[:, 512:1024]))[1],
             reads=[rx], writes=[rs])
        P.op("dve", lambda h: h.bn_aggr(m[:, 0:2], st[:, 0:12]), reads=[rs], writes=[rs + "m"])
        P.op("act", lambda h: h.activation(m[:, 2:3], m[:, 1:2], AF.Sqrt, bias=eps_t[:, 0:1], scale=1.0),
             reads=[rs + "m"], writes=[rs + "s"])
        P.op("dve", lambda h: h.reciprocal(m[:, 3:4], m[:, 2:3]), reads=[rs + "s"], writes=[rs + "r"])
        P.op("dve", lambda h: h.scalar_tensor_tensor(m[:, 4:5], m[:, 0:1], -1.0, m[:, 3:4], ALU.mult, ALU.mult),
             reads=[rs + "r", rs + "m"], writes=[rs + "n"])


    def ln_B(t, tp_banks, evac_eng, router=None, do_T=True, xtok=None):
        xt = X[:, t, :]
        sl = t % 2
        m = mv[:, sl, :]
        rx = f"X{t}"
        rs = f"st{sl}"
        P.op("act", lambda h: h.activation(xt, xt, AF.Identity, bias=m[:, 4:5], scale=m[:, 3:4]),
             reads=[rx, rs + "n", rs + "r"], writes=[rx])
        gbe = "pool" if "poolgb" in VAR else "dve"
        gbe0 = "pool" if "poolg" in VAR else "dve"
        P.op(gbe0, lambda h: h.tensor_tensor(xt, xt, gb[:, 0, :], ALU.mult), reads=[rx, "gb"], writes=[rx])
        P.op(gbe, lambda h: h.tensor_tensor(xt, xt, gb[:, 1, :], ALU.add), reads=[rx, "gb"], writes=[rx])
        if xtok is not None:
            P.op("act", lambda h: h.activation(xtok[:, t, :], xt, AF.Copy), reads=[rx], writes=[f"xtok{t}"])
        if not do_T and router is None:
            return
        b0, b1 = tp_banks

        def tp(h):
            last = None
            for c in range(KC):
                bank = ps[b0] if c < 4 else ps[b1]
                last = h.transpose(bank[:, (c % 4) * 128:(c % 4 + 1) * 128], xt[:, c * 128:(c + 1) * 128], ident[:])
            return last
        P.op("pe", tp, reads=[rx, "ident"], writes=[f"ps{b0}", f"ps{b1}"])
        for half, b in ((0, b0), (1, b1)):
            dst = xT[:, half * 4:(half + 1) * 4, t * 128:(t + 1) * 128]
            src = ps[b].rearrange("p (c n) -> p c n", c=4)
            if not do_T:
                pass
            elif evac_eng == "act":
                P.op("act", lambda h, dst=dst, src=src: h.activation(dst, src, AF.Copy),
                     reads=[f"ps{b}"], writes=[f"xT{t}"])
            else:
                P.op("dve", lambda h, dst=dst, src=src: h.tensor_copy(dst, src),
                     reads=[f"ps{b}"], writes=[f"xT{t}"])
            if router is not None:
                x32 = router
                o_eng = "dve" if evac_eng == "act" else "act"
                d32 = x32[:, half * 4:(half + 1) * 4, :]
                if o_eng == "act":
                    P.op("act", lambda h, d32=d32, src=src: h.activation(d32, src, AF.Copy),
                         reads=[f"ps{b}"], writes=[f"x32_{half}"])
                else:
                    P.op("dve", lambda h, d32=d32, src=src: h.tensor_copy(d32, src),
                         reads=[f"ps{b}"], writes=[f"x32_{half}"])
        if router is not None:
            x32 = router
            lg = ps[b0][:, 0:NEXP]

            def rmm(h):
                last = None
                for k in range(KC):
                    last = h.matmul(lg, x32[:, k, :], wr[:, k, :], start=(k == 0), stop=(k == KC - 1))
                return last
            P.op("pe", rmm, reads=["x32_0", "x32_1", "wr"], writes=[f"ps{b0}"])
            P.op("dve", lambda h: h.tensor_copy(rank[:, t, :], lg), reads=[f"ps{b0}"], writes=["lgall"])

    def router_finish():
        L = rank[:]
        A = rankm[:]
        B = Rb[:].rearrange("p (t e) -> p t e", e=NEXP)
        rtf = rt[:].rearrange("p a e -> p (a e)")
        m1, m2, den = rsh[:], rtf[:, 0:16], rtf[:, 16:32]
        bc = lambda v: v.unsqueeze(2).to_broadcast([128, NT, NEXP])
        P.op("dve", lambda h: h.tensor_reduce(m1, L, AX.X, ALU.max), reads=["lgall"], writes=["r_m1"])
        P.op("dve", lambda h: h.tensor_tensor(A, L, bc(m1), ALU.is_equal), reads=["lgall", "r_m1"], writes=["r_A"])
        P.op("dve", lambda h: h.scalar_tensor_tensor(B, A, -1e30, L, ALU.mult, ALU.add), reads=["r_A", "lgall"], writes=["r_B"])
        P.op("dve", lambda h: h.tensor_reduce(m2, B, AX.X, ALU.max), reads=["r_B"], writes=["r_m2"])
        P.op("dve", lambda h: h.tensor_tensor(maskt[:], L, bc(m2), ALU.is_ge), reads=["lgall", "r_m2"], writes=["maskt"])
        P.op("dve", lambda h: h.tensor_tensor(A, L, bc(m1), ALU.subtract), reads=["lgall", "r_m1", "r_A"], writes=["r_A"])
        P.op("act", lambda h: h.activation(B, A, AF.Exp), reads=["r_A", "r_B"], writes=["r_B"])
        P.op("dve", lambda h: h.tensor_tensor(A, maskt[:], B, ALU.mult), reads=["maskt", "r_B", "r_A"], writes=["r_A"])
        P.op("dve", lambda h: h.tensor_reduce(den, A, AX.X, ALU.add), reads=["r_A"], writes=["r_den"])
        P.op("dve", lambda h: h.reciprocal(den, den), reads=["r_den"], writes=["r_den"])
        P.op("dve", lambda h: h.tensor_tensor(comb[:], A, bc(den), ALU.mult), reads=["r_A", "r_den"], writes=["comb"])

    wq = {"n": 0}

    def wload(dst, src, key, writes):
        P.op("pool", lambda h: h.dma_start(out=dst, in_=src, max_dma_last_dim=2048), writes=writes, dma=key)

    def gemm_fm(w_slot, wcol0, dst_fn, evac_fn, banks, wres, xres_fn, tag):
        for tb in range(4):
            b = banks[tb % len(banks)]

            def mm(h, tb=tb, b=b):
                last = None
                for k in range(KC):
                    last = h.matmul(ps[b], w_slot[:, k, wcol0:wcol0 + 128], xT[:, k, tb * 512:(tb + 1) * 512],
                                    start=(k == 0), stop=(k == KC - 1))
                return last
            P.op("pe", mm, reads=[wres] + xres_fn(tb), writes=[f"ps{b}"])
            evac_fn(tb, b)

    def xT_res(tb):
        return [f"xT{t}" for t in range(tb * 4, tb * 4 + 4)]

    P.start_phase("p0")
    P.op("pool", lambda h: h.memset(eps_t[:], EPS), writes=["eps"])
    P.op("sp", lambda h: h.dma_start(out=ident[:], in_=ident_d), writes=["ident"], dma="c_ident")
    if debug != "p0x" or "a" in VAR:
        wload(identb[:], ident_d, "c_identb", ["identb"])
    load_gb(0)
    for t in range(NT):
        P.op("sp", lambda h, t=t: h.dma_start(out=X[:, t, :], in_=x_d[t * 128:(t + 1) * 128, :]),
             writes=[f"X{t}"], dma=f"x{t}")
    P.res.setdefault("st0m", {"w": None, "r": []})
    P.res["st0m"] = {"w": P.res["eps"]["w"], "r": []}
    P.res["st1m"] = {"w": P.res["eps"]["w"], "r": []}
    ln_A(0)
    for t in range(NT):
        if t + 1 < NT:
            ln_A(t + 1)
        ln_B(t, (0, 1) if t % 2 == 0 else (2, 3), "act" if t % 2 == 0 else "dve")
    P.barrier()

    if debug in ("p0", "p0x"):
        dbg("dbg_X", X[:], [128, NT, D], F32, reads=[])
        dbg("dbg_xT", xT[:], [128, KC, S], BF16, reads=[])
        n_layers = 0

    yT = [M.at(R_Y, [128, 4, S], BF16, "yTna"), M.at(R_Y + 16 * KB, [128, 4, S], BF16, "yTsw")]
    xtok_g = M.at(64 * KB, [128, NT, D], BF16, "xtok")

    def moe_routed(layer):
        xtok = xtok_g
        wst = [dict(g=M.at(R_A + s_ * 24 * KB, [128, KC, 512], BF16, f"wg{s_}"),
                    u=M.at(R_A + s_ * 24 * KB + 8 * KB, [128, KC, 512], BF16, f"wu{s_}"),
                    d=M.at(R_A + s_ * 24 * KB + 16 * KB, [128, 4, D], BF16, f"wd{s_}")) for s_ in range(2)]
        xg = M.at(R_Y, [128, KC, 512], BF16, "xg")
        yacc = M.at(R_Y + 8 * KB, [128, 4, D], F32, "yacc")
        hTs = [M.at(R_Y + 24 * KB + s_ * 4 * KB, [128, 4, 512], BF16, f"hTs{s_}") for s_ in range(2)]
        yslot = M.at(R_B, [128, 4, D], BF16, "yslot")
        selt = [M.at(R_B + 8 * KB + i * KB, [128, 512], BF16, f"selt{i}") for i in range(2)]
        iota512 = M.at(R_B + 10 * KB, [128, 512], F32, "iota512")
        selT = [maskPN[:, i, :].rearrange("p (s n) -> p s n", s=4) for i in range(2)]

        P.op("sp", lambda h: h.dma_start(out=ltri[:], in_=cst2_d[:, 0:128]), writes=["ltri"], dma="c_ltri")
        P.op("sp", lambda h: h.dma_start(out=slotid[:], in_=cst2_d[:, 128:132]), writes=["slotid"], dma="c_slot")
        P.op("sp", lambda h: h.dma_start(out=iota512[:], in_=cst2_d[:, 132:644]), writes=["iota"], dma="c_iota")
        P.op("pool", lambda h: h.memset(ones[:], 1.0), writes=["ones"])
        P.op("dve", lambda h: h.tensor_scalar(negslot[:], slotid[:], -1.0, None, ALU.mult), reads=["slotid"], writes=["negslot"])
        one_t = ones[:, 0:1]
        mflat = maskt[:].rearrange("p t e -> p (t e)")
        P.op("pe", lambda h: h.matmul(ps[0][:, 0:128], ones[:], mflat, start=True, stop=True),
             reads=["ones", "maskt"], writes=["ps0"])
        P.op("pe", lambda h: h.matmul(ps[1][:, 0:128], ltri[:], mflat, start=True, stop=True),
             reads=["ltri", "maskt"], writes=["ps1"])
        cps = ps[0][:, 0:128].rearrange("p (t e) -> p t e", e=NEXP)
        wps = ps[1][:, 0:128].rearrange("p (t e) -> p t e", e=NEXP)
        P.op("dve", lambda h: h.memset(offs[:], 0.0), writes=["offs"])
        for t in range(NT):
            P.op("dve", lambda h, t=t: h.tensor_tensor(rank[:, t, :], wps[:, t, :], offs[:], ALU.add),
                 reads=["ps1", "offs"], writes=["rank"])
            P.op("dve", lambda h, t=t: h.tensor_tensor(offs[:], cps[:, t, :], offs[:], ALU.add),
                 reads=["ps0", "offs", "rank"], writes=["offs"])
        P.op("dve", lambda h: h.tensor_copy(cnti[:], offs[:]), reads=["offs"], writes=["cnti"])
        rkf = rank[:].rearrange("p t e -> p (t e)")
        rmf = rankm[:].rearrange("p t e -> p (t e)")
        P.op("dve", lambda h: h.scalar_tensor_tensor(rmf, rkf, 1.0, mflat, ALU.add, ALU.mult),
             reads=["rank", "maskt"], writes=["rankm"])
        P.op("dve", lambda h: h.tensor_scalar(rmf, rmf, -1.0, None, ALU.add), reads=["rankm"], writes=["rankm"])
        if debug == "route":
            dbg("dbg_rankm", rankm[:], [128, NT, NEXP], F32, reads=["rankm"])
            dbg("dbg_comb", comb[:], [128, NT, NEXP], F32, reads=["comb"])
            dbg("dbg_cnti", cnti[:], [128, NEXP], mybir.dt.int32, reads=["cnti"])
            raise StopBuild()

        nblk = FF_EXP // 512
        cnt = {"h": 0, "d": 0, "w": 0}

        def load_w(e, fb):
            s_ = cnt["w"] % 2
            cnt["w"] += 1
            gv = mg_d[e].rearrange("(c p) n -> p c n", p=128)
            uv = mu_d[e].rearrange("(c p) n -> p c n", p=128)
            dv = md_d[e][fb * 512:(fb + 1) * 512, :].rearrange("(c p) n -> p c n", p=128)
            wload(wst[s_]["g"][:], gv[:, :, fb * 512:(fb + 1) * 512], f"wg{s_}", [f"wg{s_}"])
            wload(wst[s_]["u"][:], uv[:, :, fb * 512:(fb + 1) * 512], f"wu{s_}", [f"wu{s_}"])
            for hf_ in range(2):
                wload(wst[s_]["d"][:, :, hf_ * 512:(hf_ + 1) * 512], dv[:, :, hf_ * 512:(hf_ + 1) * 512], f"wd{s_}", [f"wd{s_}"])
            return s_

        def hidden(s_, hs, nsl):
            for fc in range(4):
                n = cnt["h"]
                cnt["h"] += 1
                bg, bu = (0, 1) if n % 2 == 0 else (2, 3)

                def mm(h, fc=fc, bg=bg, bu=bu):
                    last = None
                    for k in range(KC):
                        last = h.matmul(ps[bg][:, 0:nsl], wst[s_]["g"][:, k, fc * 128:(fc + 1) * 128], xg[:, k, 0:nsl],
                                        start=(k == 0), stop=(k == KC - 1))
                    for k in range(KC):
                        last = h.matmul(ps[bu][:, 0:nsl], wst[s_]["u"][:, k, fc * 128:(fc + 1) * 128], xg[:, k, 0:nsl],
                                        start=(k == 0), stop=(k == KC - 1))
                    return last
                P.op("pe", mm, reads=[f"wg{s_}", f"wu{s_}"] + [f"xg{c}" for c in range(KC)], writes=[f"ps{bg}", f"ps{bu}"])
                tmp = ytmp[:, n % 2, 0:nsl]
                P.op("act", lambda h, tmp=tmp, bg=bg: h.activation(tmp, ps[bg][:, 0:nsl], AF.Silu), reads=[f"ps{bg}"], writes=[f"sg{n % 2}"])
                P.op("dve", lambda h, tmp=tmp, bu=bu, fc=fc: h.tensor_tensor(hTs[hs][:, fc, 0:nsl], ps[bu][:, 0:nsl], tmp, ALU.mult),
                     reads=[f"ps{bu}", f"sg{n % 2}"], writes=[f"hTs{hs}"])

        def down(s_, hs, first, nst):
            for st_ in range(nst):
                for half in range(2):
                    n = cnt["d"]
                    cnt["d"] += 1
                    b_ = 4 + n % 4

                    def mm(h, st_=st_, half=half, b_=b_):
                        last = None
                        for fc in range(4):
                            last = h.matmul(ps[b_], hTs[hs][:, fc, st_ * 128:(st_ + 1) * 128], wst[s_]["d"][:, fc, half * 512:(half + 1) * 512],
                                            start=(fc == 0), stop=(fc == 3))
                        return last
                    P.op("pe", mm, reads=[f"wd{s_}", f"hTs{hs}"], writes=[f"ps{b_}"])
                    ya = yacc[:, st_, half * 512:(half + 1) * 512]
                    if first:
                        P.op("dve", lambda h, ya=ya, b_=b_: h.tensor_copy(ya, ps[b_]), reads=[f"ps{b_}"], writes=[f"yacc{st_}"])
                    else:
                        P.op("dve", lambda h, ya=ya, b_=b_: h.tensor_tensor(ya, ps[b_], ya, ALU.add),
                             reads=[f"ps{b_}", f"yacc{st_}"], writes=[f"yacc{st_}"])

        def block(e, off, nsl):
            nst = nsl // 128
            P.op("dve", lambda h: h.tensor_scalar(rsh[:], rankm[:, :, e], float(-off), None, ALU.add),
                 reads=["rankm"], writes=["rsh"])
            s0 = load_w(e, 0)
            for t in range(NT):
                sl_ = selt[t % 2]
                P.op("dve", lambda h, sl_=sl_, t=t: h.tensor_scalar(sl_[:, 0:nsl], iota512[:, 0:nsl], rsh[:, t:t + 1], None, ALU.is_equal),
                     reads=["iota", "rsh"], writes=[f"selt{t % 2}"])

                def mm(h, sl_=sl_, t=t):
                    last = None
                    for c in range(KC):
                        last = h.matmul(ps[c][:, 0:nsl], xtok[:, t, c * 128:(c + 1) * 128], sl_[:, 0:nsl], start=(t == 0), stop=(t == NT - 1))
                    return last
                P.op("pe", mm, reads=[f"selt{t % 2}", f"xtok{t}"], writes=[f"ps{c}" for c in range(KC)])
            for c in range(KC):
                if c % 2 == 0:
                    P.op("act", lambda h, c=c: h.activation(xg[:, c, 0:nsl], ps[c][:, 0:nsl], AF.Copy), reads=[f"ps{c}"], writes=[f"xg{c}"])
                else:
                    P.op("dve", lambda h, c=c: h.tensor_copy(xg[:, c, 0:nsl], ps[c][:, 0:nsl]), reads=[f"ps{c}"], writes=[f"xg{c}"])
            stages = [s0]
            hidden(s0, 0, nsl)
            for fb in range(1, nblk):
                stages.append(load_w(e, fb))
                hidden(stages[fb], fb % 2, nsl)
                down(stages[fb - 1], (fb - 1) % 2, fb - 1 == 0, nst)
            down(stages[nblk - 1], (nblk - 1) % 2, False, nst)
            for st_ in range(nst):
                if st_ % 2 == 0:
                    P.op("act", lambda h, st_=st_: h.activation(yslot[:, st_, :], yacc[:, st_, :], AF.Copy),
                         reads=[f"yacc{st_}"], writes=[f"yslot{st_}"])
                else:
                    P.op("pool", lambda h, st_=st_: h.tensor_copy(yslot[:, st_, :], yacc[:, st_, :]),
                         reads=[f"yacc{st_}"], writes=[f"yslot{st_}"])
            def sc_stage1(t):
                rb_ = 4 + t % 2
                P.op("dve", lambda h, t=t: h.tensor_scalar(Rb[:], ones[:], rsh[:, t:t + 1], None, ALU.mult),
                     reads=["ones", "rsh"], writes=["Rb"])
                P.op("pe", lambda h, rb_=rb_: h.matmul(ps[rb_][:, 0:128], Rb[:], ident[:], start=True, stop=True),
                     reads=["Rb", "ident"], writes=[f"ps{rb_}"])
                sT = selT[t % 2]

                if "dvemk" in VAR:
                    def mk(h, sT=sT, rb_=rb_):
                        last = None
                        for st_ in range(nst):
                            last = h.tensor_scalar(sT[:, st_, :], ps[rb_][:, 0:128], slotid[:, st_:st_ + 1], None, ALU.is_equal)
                        return last
                    P.op("dve", mk, reads=[f"ps{rb_}", "slotid"], writes=[f"selT{t % 2}"])
                else:
                    ta = ytmp[:, t % 2, :].rearrange("p (s n) -> p s n", s=4)

                    def mk1(h, rb_=rb_, ta=ta):
                        last = None
                        for st_ in range(nst):
                            last = h.activation(ta[:, st_, :], ps[rb_][:, 0:128], AF.Abs, bias=negslot[:, st_:st_ + 1], scale=1.0)
                        return last

                    def mk2(h, sT=sT, ta=ta):
                        return h.activation(sT[:, 0:nst, :], ta[:, 0:nst, :], AF.Relu, bias=one_t, scale=-1.0)
                    P.op("act", mk1, reads=[f"ps{rb_}", "negslot"], writes=[f"sg{t % 2}"])
                    P.op("act", mk2, reads=[f"sg{t % 2}", "ones"], writes=[f"selT{t % 2}"])

            def sc_stage2(t):
                sT = selT[t % 2]
                for half in range(2):
                    ob_ = 6 + half

                    def mm(h, sT=sT, half=half, ob_=ob_):
                        last = None
                        for st_ in range(nst):
                            last = h.matmul(ps[ob_], sT[:, st_, :], yslot[:, st_, half * 512:(half + 1) * 512],
                                            start=(st_ == 0), stop=(st_ == nst - 1))
                        return last
                    P.op("pe", mm, reads=[f"selT{t % 2}"] + [f"yslot{i}" for i in range(nst)], writes=[f"ps{ob_}"])
                    xs = X[:, t, half * 512:(half + 1) * 512]
                    P.op("dve", lambda h, xs=xs, ob_=ob_, t=t: h.scalar_tensor_tensor(xs, ps[ob_], comb[:, t, e:e + 1], xs, ALU.mult, ALU.add),
                         reads=[f"ps{ob_}", f"X{t}a", "comb"], writes=[f"X{t}a"])

            sc_stage1(0)
            for t in range(NT):
                if t + 1 < NT:
                    sc_stage1(t + 1)
                sc_stage2(t)

        BLOCKS = [(0, 512), (512, 128), (640, 384), (1024, 512), (1536, 512)]
        for e in range(NEXP):
            P.regload(("pe", "act", "dve", "pool"), cnti[0:1, e:e + 1], reads=["cnti"])
            ncond = 0
            for (off, nsl) in BLOCKS:
                if off > 0:
                    P.begin_cond(off)
                    ncond += 1
                block(e, off, nsl)
            for _ in range(ncond):
                P.end_cond()

    def do_layer(layer):
        P.start_phase(f"l{layer}p1")
        wv = w_in_d[layer].rearrange("(c p) n -> p c n", p=128)
        ws = [M.at(R_B, [128, KC, 512], BF16, "ws0"), M.at(R_B + 8 * KB, [128, KC, 512], BF16, "ws1")]
        PT = [M.at(R_B + 16 * KB + i * 1280, [128, 640], BF16, f"PT{i}") for i in range(3)]
        qT = M.at(R_A, [128, 4, S], BF16, "qT")
        kT = M.at(R_A + 16 * KB, [128, 2, S], BF16, "kT")
        VaNA = M.at(R_A + 24 * KB, [128, NT, 4, 65], BF16, "VaNA")
        wload(prot[:], cst_d[:, 0:128], "c_prot", ["prot"])
        wload(maskPN[:], cst_d[:, 128:1152].rearrange("p (a n) -> p a n", a=2), "c_mask", ["maskPN"])
        P.op("sp", lambda h: h.dma_start(out=esink[:], in_=sink_d[layer].partition_broadcast(128)),
             writes=["esink"], dma="c_esink")
        P.op("act", lambda h: h.activation(esink[:], esink[:], AF.Exp), reads=["esink"], writes=["esink"])

        def do_group(grp):
            is_na = grp < 2
            base = grp * 768
            nh = 4 if is_na else 2
            if is_na:
                Va = VaNA
                bias = M.at(R_A + 33 * KB, [128, 4, 1536], BF16, "nabias")
                wload(bias[:], nabias_d[layer][:, grp * 4:(grp + 1) * 4, :], "bias", ["bias"])
            else:
                Va = M.at(R_A + 20 * KB, [128, NT, 2, 65], BF16, "VaSW")
                cs = M.at(R_A + 24 * KB + 512, [128, 2, S], F32, "cs")
                rtmp = M.at(R_A + 41 * KB, [128, 2, 512], F32, "rtmp")
                qb = M.at(R_A + 45 * KB, [128, 512], BF16, "qb")
                qb2 = M.at(R_A + 46 * KB, [128, 512], BF16, "qb2")
                P.op("sp", lambda h: h.dma_start(out=cs[:], in_=cs_d.rearrange("a p n -> p a n")),
                     writes=["cs"], dma="c_cs")
            if grp != 1:
                P.op("pool", lambda h, Va=Va: h.memset(Va[:, :, :, 64:65], 1.0), writes=["vones"])
            wload(ws[0][:], wv[:, :, base:base + 512], "ws0", ["ws0"])
            wload(ws[1][:, :, 0:256], wv[:, :, base + 512:base + 768], "ws1", ["ws1"])

            def evac_plain(dst, scale, wres):
                def f(tb, b):
                    d = dst[:, tb * 512:(tb + 1) * 512]
                    if tb % 2 == 0:
                        P.op("act", lambda h: h.activation(d, ps[b], AF.Copy, scale=scale),
                             reads=[f"ps{b}"], writes=[wres + str(tb)])
                    else:
                        P.op("dve", lambda h: h.tensor_scalar(d, ps[b], scale, None, ALU.mult),
                             reads=[f"ps{b}"], writes=[wres + str(tb)])
                return f

            def evac_rot(dst, scale, wres):
                def f(tb, b):
                    d = dst[:, tb * 512:(tb + 1) * 512]
                    o_ = tb % 2
                    rt_ = rtmp if o_ == 0 else ytmp
                    qb_ = qb if o_ == 0 else qb2
                    pb_ = 5 if o_ == 0 else 4
                    P.op("act", lambda h: h.activation(qb_[:], ps[b], AF.Copy), reads=[f"ps{b}"], writes=[f"qb{o_}"])
                    P.op("pe", lambda h: h.matmul(ps[pb_], prot[:], qb_[:], start=True, stop=True),
                         reads=[f"qb{o_}", "prot"], writes=[f"ps{pb_}"])
                    P.op("dve", lambda h: h.scalar_tensor_tensor(rt_[:, 0, :], ps[b], scale, cs[:, 0, tb * 512:(tb + 1) * 512],
                                                                 ALU.mult, ALU.mult),
                         reads=[f"ps{b}", "cs"], writes=[f"rtmp0_{o_}"])
                    P.op("dve", lambda h: h.scalar_tensor_tensor(rt_[:, 1, :], ps[pb_], scale, cs[:, 1, tb * 512:(tb + 1) * 512],
                                                                 ALU.mult, ALU.mult),
                         reads=[f"ps{pb_}", "cs"], writes=[f"rtmp1_{o_}"])
                    P.op("pool", lambda h: h.tensor_tensor(d, rt_[:, 0, :], rt_[:, 1, :], ALU.add),
                         reads=[f"rtmp0_{o_}", f"rtmp1_{o_}"], writes=[wres + str(tb)])
                return f

            if is_na:
                for c in range(2):
                    gemm_fm(ws[0], c * 128, None, evac_plain(qT[:, c, :], 0.125, f"q{c}_"), (6, 7), "ws0", xT_res, "q")
                for c in range(2):
                    gemm_fm(ws[0], 256 + c * 128, None, evac_plain(kT[:, c, :], 1.0, f"k{c}_"), (6, 7), "ws0", xT_res, "k")
            else:
                ev = evac_plain if "norot" in VAR else evac_rot
                for c in range(4):
                    gemm_fm(ws[0], c * 128, None, ev(qT[:, c, :], 0.125, f"q{c}_"), (6, 7), "ws0", xT_res, "q")
                gemm_fm(ws[1], 0, None, ev(kT[:, 0, :], 1.0, "k0_"), (6, 7), "ws1", xT_res, "k")
            vcol0 = 0 if is_na else 128
            vn = nh * 64
            for t in range(NT):
                b = 6 + t % 2

                def mmv(h, t=t, b=b):
                    last = None
                    for k in range(KC):
                        last = h.matmul(ps[b][:, 0:vn], xT[:, k, t * 128:(t + 1) * 128], ws[1][:, k, vcol0:vcol0 + vn],
                                        start=(k == 0), stop=(k == KC - 1))
                    return last
                P.op("pe", mmv, reads=["ws1", f"xT{t}"], writes=[f"ps{b}"])
                dstv = Va[:, t, :, 0:64]
                srcv = ps[b][:, 0:vn].rearrange("p (h d) -> p h d", h=nh)
                if t % 2 == 0:
                    P.op("act", lambda h, dstv=dstv, srcv=srcv: h.activation(dstv, srcv, AF.Copy),
                         reads=[f"ps{b}", "vones"], writes=[f"v{t}"])
                else:
                    P.op("dve", lambda h, dstv=dstv, srcv=srcv: h.tensor_copy(dstv, srcv),
                         reads=[f"ps{b}", "vones"], writes=[f"v{t}"])

            if debug == f"p1proj{grp}" and layer == 0:
                dbg("dbg_q", qT[:], [128, 4, S], BF16, reads=[f"q{c}_{tb}" for c in range(4) for tb in range(4)])
                dbg("dbg_k", kT[:], [128, 2, S], BF16, reads=[f"k{c}_{tb}" for c in range(2) for tb in range(4)])
                dbg("dbg_v", Va[:], [128, NT, nh, 65], BF16, reads=[f"v{t}" for t in range(NT)])
                raise StopBuild()

            steps = []
            if is_na:
                for i in range(NT):
                    if i < 2:
                        js, unm = list(range(3, -1, -1)), True
                    elif i >= 14:
                        js, unm = list(range(15, 11, -1)), True
                    else:
                        js, unm = list(range(i + 2, i - 3, -1)), False
                    for hh in range(4):
                        steps.append((i, hh, js, unm))
            else:
                for n in range(NT):
                    for kv in range(2):
                        js = [j for j in (n - 1, n, n + 1) if 0 <= j < NT]
                        for j in js:
                            steps.append((n, kv, [j], j - n))
            nsteps = len(steps)

            def emit_S(si):
                sb = (0, 2)[si % 2]
                if is_na:
                    i, hh, js, unm = steps[si]
                    c, po = hh // 2, (hh % 2) * 64

                    def f(h):
                        last = None
                        for jj, j in enumerate(js):
                            o = PS[:, sb + jj // 4, (jj % 4) * 128:(jj % 4 + 1) * 128]
                            h.matmul(o, kT[po:po + 64, c, j * 128:(j + 1) * 128], qT[po:po + 64, c, i * 128:(i + 1) * 128],
                                     start=(jj % 4 == 0), stop=False, skip_group_check=True)
                        dl0 = js[0] - i
                        col0 = (640 + 64 * (6 - 2 * dl0)) if unm else (64 * (4 - 2 * dl0))
                        n0 = min(len(js), 4) * 128
                        last = h.matmul(PS[:, sb, 0:n0], identb[:], bias[:, hh, col0:col0 + n0], start=False, stop=True,
                                        skip_group_check=True)
                        if len(js) > 4:
                            last = h.matmul(PS[:, sb + 1, 0:128], identb[:], bias[:, hh, col0 + 512:col0 + 640], start=False, stop=True,
                                            skip_group_check=True)
                        return last
                    rd = ["bias", "identb", f"q{c}_{i // 4}"] + [f"k{c}_{j // 4}" for j in js]
                    P.op("pe", f, reads=rd, writes=[f"ps{sb}", f"ps{sb + 1}"])
                else:
                    n, kv, js, rel = steps[si]
                    j = js[0]
                    po = kv * 64

                    def f(h):
                        o = ps[sb]
                        last = h.matmul(o, kT[po:po + 64, 0, j * 128:(j + 1) * 128], qT[po:po + 64, :, n * 128:(n + 1) * 128],
                                        start=True, stop=(rel == 0))
                        if rel != 0:
                            last = h.matmul(o, identb[:], maskPN[:, 0 if rel < 0 else 1, :], start=False, stop=True)
                        return last
                    rd = ["maskPN", "identb", f"k0_{j // 4}"] + [f"q{c}_{n // 4}" for c in range(4)]
                    P.op("pe", f, reads=rd, writes=[f"ps{sb}"])

            def emit_exp(si):
                sb = (0, 2)[si % 2]
                pt = PT[si % 3]
                if is_na:
                    nj = len(steps[si][2])
                    src = PS[:, sb:sb + 2, :].rearrange("p a n -> p (a n)")[:, 0:nj * 128]
                    P.op("act", lambda h: h.activation(pt[:, 0:nj * 128], src, AF.Exp),
                         reads=[f"ps{sb}", f"ps{sb + 1}"], writes=[f"PT{si % 3}"])
                else:
                    P.op("act", lambda h: h.activation(pt[:, 0:512], ps[sb], AF.Exp),
                         reads=[f"ps{sb}"], writes=[f"PT{si % 3}"])

            def emit_PV(si):
                pt = PT[si % 3]
                if is_na:
                    i, hh, js, unm = steps[si]
                    ob = 4 + i % 2

                    def f(h):
                        last = None
                        for jj, j in enumerate(js):
                            last = h.matmul(ps[ob][:, hh * 65:(hh + 1) * 65], pt[:, jj * 128:(jj + 1) * 128], Va[:, j, hh, :],
                                            start=(jj == 0), stop=(jj == len(js) - 1), skip_group_check=True)
                        return last
                    P.op("pe", f, reads=[f"PT{si % 3}"] + [f"v{j}" for j in js], writes=[f"ps{ob}"])
                    if hh == 3:
                        emit_norm(i, [(ob, 4)])
                else:
                    n, kv, js, rel = steps[si]
                    j = js[0]
                    obs = (4, 5) if n % 2 == 0 else (1, 3)
                    ob = obs[kv]
                    first = (j == max(0, n - 1))
                    lastj = (j == min(NT - 1, n + 1))

                    def f(h):
                        last = None
                        for g in range(4):
                            last = h.matmul(ps[ob][:, g * 65:(g + 1) * 65], pt[:, g * 128:(g + 1) * 128], Va[:, j, kv, :],
                                            start=(first and g == 0), stop=lastj, skip_group_check=True)
                        return last
                    P.op("pe", f, reads=[f"PT{si % 3}", f"v{j}"], writes=[f"ps{ob}"])
                    if kv == 1 and lastj:
                        emit_norm(n, [(obs[0], 4), (obs[1], 4)])

            def emit_norm(i, pieces):
                sl = i % 2
                nheads = sum(p_[1] for p_ in pieces)
                h0 = 0
                for (ob, nh_) in pieces:
                    o3 = ps[ob][:, 0:nh_ * 65].rearrange("p (h d) -> p h d", d=65)
                    rd_ = rden[:, sl, h0:h0 + nh_]
                    if is_na:
                        P.op("dve", lambda h, rd_=rd_, o3=o3: h.reciprocal(rd_, o3[:, :, 64]),
                             reads=[f"ps{ob}"], writes=[f"rden{sl}_{h0}"])
                    else:
                        P.op("dve", lambda h, rd_=rd_, o3=o3, h0=h0, nh_=nh_: h.tensor_tensor(rd_, o3[:, :, 64], esink[:, h0:h0 + nh_], ALU.add),
                             reads=[f"ps{ob}", "esink"], writes=[f"rden{sl}_{h0}"])
                        P.op("dve", lambda h, rd_=rd_: h.reciprocal(rd_, rd_), reads=[f"rden{sl}_{h0}"], writes=[f"rden{sl}_{h0}"])

                    def fn(h, o3=o3, h0=h0, nh_=nh_):
                        last = None
                        for hd in range(nh_):
                            last = h.tensor_scalar(ytmp[:, sl, (h0 + hd) * 64:(h0 + hd + 1) * 64], o3[:, hd, 0:64],
                                                   rden[:, sl, h0 + hd:h0 + hd + 1], None, ALU.mult)
                        return last
                    P.op("dve", fn, reads=[f"ps{ob}", f"rden{sl}_{h0}"], writes=[f"ytmp{sl}_{h0}"])
                    h0 += nh_
                nchunk = nheads // 2
                tb_ = 6 + i % 2
                yt = ytmp[:, sl, :]

                def tp(h):
                    last = None
                    for c in range(nchunk):
                        last = h.transpose(ps[tb_][:, c * 128:(c + 1) * 128], yt[:, c * 128:(c + 1) * 128], ident[:])
                    return last
                P.op("pe", tp, reads=[f"ytmp{sl}_{hh_}" for hh_ in range(0, nheads, 4)] + ["ident"], writes=[f"ps{tb_}"])
                ydst = yT[0 if is_na else 1]
                c0 = grp * 2 if is_na else 0
                dst = ydst[:, c0:c0 + nchunk, i * 128:(i + 1) * 128]
                srcp = ps[tb_][:, 0:nchunk * 128].rearrange("p (c n) -> p c n", c=nchunk)
                P.op("act", lambda h: h.activation(dst, srcp, AF.Copy), reads=[f"ps{tb_}"], writes=[f"yT{i}"])

            if debug and debug.startswith("att"):
                _, g_, n_, what = debug.split("_")
                if int(g_) == grp:
                    for si in range(int(n_)):
                        emit_S(si)
                        if "E" in what:
                            emit_exp(si)
                        if "P" in what:
                            emit_PV(si)
                    raise StopBuild()
            emit_S(0)
            emit_exp(0)
            for si in range(nsteps):
                if si + 1 < nsteps:
                    emit_S(si + 1)
                    emit_exp(si + 1)
                emit_PV(si)
        for grp in range(3):
            do_group(grp)
            if grp >= 1:
                P.barrier()
        if debug == "p1" and layer == 0:
            dbg("dbg_yna", yT[0][:], [128, 4, S], BF16, reads=[])
            dbg("dbg_ysw", yT[1][:], [128, 4, S], BF16, reads=[])
            raise StopBuild()

        P.start_phase(f"l{layer}p2")
        zT = M.at(R_A, [128, KC, S], BF16, "zT")
        wo = M.at(R_A + 32 * KB, [128, KC, D], BF16, "wo")
        p2w = [M.at(R_B + i * 6 * KB, [128, 24, 128], BF16, f"p2w{i}") for i in range(2)]
        load_gb(1 + 2 * layer)
        P.op("sp", lambda h: h.dma_start(out=bgate[:], in_=bgate_d[layer]), writes=["bgate"], dma="c_bgate")
        wov = wout_d[layer].rearrange("(c p) n -> p c n", p=128)
        wbv = [wbr_d[layer, br].rearrange("(c p) n -> p c n", p=128) for br in range(2)]

        def load_p2w(c):
            s_ = c % 2
            cs_ = slice(c * 128, (c + 1) * 128)
            wload(p2w[s_][:, 0:4, :], wbv[0][:, :, cs_], f"p2w{s_}", [])
            wload(p2w[s_][:, 4:8, :], wbv[1][:, :, cs_], f"p2w{s_}", [])
            wload(p2w[s_][:, 8:16, :], wv[:, :, 2304 + c * 128:2304 + (c + 1) * 128], f"p2w{s_}", [])
            wload(p2w[s_][:, 16:24, :], wv[:, :, 3328 + c * 128:3328 + (c + 1) * 128], f"p2w{s_}", [f"p2w{s_}"])

        def load_p2w(c):
            s_ = c % 2
            cs_ = slice(c * 128, (c + 1) * 128)
            for (r0, r1, srcap) in ((0, 4, wbv[0][:, :, cs_]), (4, 8, wbv[1][:, :, cs_]),
                                    (8, 16, wv[:, :, 2304 + c * 128:2304 + (c + 1) * 128]),
                                    (16, 24, wv[:, :, 3328 + c * 128:3328 + (c + 1) * 128])):
                wload(p2w[s_][:, r0:r1, :], srcap, f"p2w{s_}", [f"p2w{s_}"])

        load_p2w(0)
        load_p2w(1)
        wload(wo[:, :, 0:512], wov[:, :, 0:512], "wo", ["wo"])
        wload(wo[:, :, 512:1024], wov[:, :, 512:1024], "wo", ["wo"])
        step = 0
        for c in range(KC):
            s_ = c % 2
            w_ = p2w[s_]
            for tb in range(4):
                bk = (0, 1, 2, 3) if step % 2 == 0 else (4, 5, 6, 7)
                tsl = slice(tb * 512, (tb + 1) * 512)

                def mm(h, w_=w_, bk=bk, tsl=tsl):
                    last = None
                    for k in range(4):
                        last = h.matmul(ps[bk[0]], w_[:, k, :], yT[0][:, k, tsl], start=(k == 0), stop=(k == 3))
                    for k in range(4):
                        last = h.matmul(ps[bk[1]], w_[:, 4 + k, :], yT[1][:, k, tsl], start=(k == 0), stop=(k == 3))
                    for k in range(KC):
                        last = h.matmul(ps[bk[2]], w_[:, 8 + k, :], xT[:, k, tsl], start=(k == 0), stop=(k == KC - 1))
                    for k in range(KC):
                        last = h.matmul(ps[bk[3]], w_[:, 16 + k, :], xT[:, k, tsl], start=(k == 0), stop=(k == KC - 1))
                    return last
                P.op("pe", mm, reads=[f"p2w{s_}"], writes=[f"ps{b_}" for b_ in bk])
                t0_, t1_ = ytmp[:, 0, :], ytmp[:, 1, :]
                P.op("act", lambda h, bk=bk, c=c: h.activation(t0_, ps[bk[2]], AF.Sigmoid, bias=bgate[:, c:c + 1], scale=1.0),
                     reads=[f"ps{bk[2]}", "bgate"], writes=["g0"])
                P.op("dve", lambda h, bk=bk: h.tensor_tensor(t0_, ps[bk[0]], t0_, ALU.mult),
                     reads=[f"ps{bk[0]}", "g0"], writes=["g0"])
                P.op("act", lambda h, bk=bk, c=c: h.activation(t1_, ps[bk[3]], AF.Sigmoid, bias=bgate[:, 8 + c:9 + c], scale=1.0),
                     reads=[f"ps{bk[3]}", "bgate"], writes=["g1"])
                P.op("dve", lambda h, bk=bk: h.tensor_tensor(t1_, ps[bk[1]], t1_, ALU.mult),
                     reads=[f"ps{bk[1]}", "g1"], writes=["g1"])
                P.op("pool", lambda h, c=c, tsl=tsl: h.tensor_tensor(zT[:, c, tsl], t0_, t1_, ALU.add),
                     reads=["g0", "g1"], writes=[f"z{tb}"])
                step += 1
            if c + 2 < KC:
                load_p2w(c + 2)

        if debug == "p2z" and layer == 0:
            dbg("dbg_z", zT[:], [128, KC, S], BF16, reads=[f"z{tb}" for tb in range(4)])
            raise StopBuild()

        x32 = M.at(R_Y, [128, KC, 128], F32, "x32")
        xtok = xtok_g
        if layer % 2 == 1:
            P.barrier()
        if layer % 2 == 1:
            P.op("sp", lambda h: h.dma_start(out=wr[:], in_=wr_d.rearrange("(c p) e -> p c e", p=128)),
                 writes=["wr"], dma="c_wr")
        def wo_pre(t):
            gb_ = (0, 1) if t % 2 == 0 else (2, 3)

            def mmo(h, t=t, gb_=gb_):
                last = None
                for half in range(2):
                    for k in range(KC):
                        last = h.matmul(ps[gb_[half]], zT[:, k, t * 128:(t + 1) * 128], wo[:, k, half * 512:(half + 1) * 512],
                                        start=(k == 0), stop=(k == KC - 1))
                return last
            P.op("pe", mmo, reads=["wo", f"z{t // 4}"], writes=[f"ps{gb_[0]}", f"ps{gb_[1]}"])
            for half in range(2):
                xs = X[:, t, half * 512:(half + 1) * 512]
                P.op("dve", lambda h, xs=xs, b_=gb_[half]: h.scalar_tensor_tensor(xs, xs, ALPHA, ps[b_], ALU.mult, ALU.add),
                     reads=[f"ps{gb_[half]}", f"X{t}"], writes=[f"X{t}"])
            ln_A(t)

        wo_pre(0)
        for t in range(NT):
            if t + 1 < NT:
                wo_pre(t + 1)
            if layer % 2 == 1:
                ln_B(t, (4, 5) if t % 2 == 0 else (6, 7), "act" if t % 2 == 0 else "dve",
                     router=x32, do_T=False, xtok=xtok)
            else:
                ln_B(t, (4, 5) if t % 2 == 0 else (6, 7), "act" if t % 2 == 0 else "dve")
        if layer % 2 == 1:
            router_finish()
        P.barrier()
        if debug == "p2" and layer == 0:
            dbg("dbg_X", X[:], [128, NT, D], F32, reads=[])
            dbg("dbg_xT", xT[:], [128, KC, S], BF16, reads=[])
            raise StopBuild()

        P.start_phase(f"l{layer}p3")
        load_gb(2 + 2 * layer)
        for t in range(NT):
            P.op("act", lambda h, t=t: h.activation(X[:, t, :], X[:, t, :], AF.Copy, scale=ALPHA),
                 reads=[f"X{t}"], writes=[f"X{t}"])
        if layer % 2 == 0 or "densemoe" in VAR:
            if layer % 2 == 0:
                passes = [(fg_d, fu_d, fd_d, FF_DENSE, None)]
            else:
                passes = [(mg_d[e], mu_d[e], md_d[e], FF_EXP, e) for e in range(NEXP)]
            items = []
            for (g_d, u_d, d_d, F_, e_) in passes:
                nblk = (F_ + 511) // 512
                for fb in range(nblk):
                    items.append((g_d, u_d, d_d, fb, min(4, (F_ - fb * 512) // 128), e_))
            wst = [dict(g=M.at(R_A + s_ * 24 * KB, [128, KC, 512], BF16, f"wg{s_}"),
                        u=M.at(R_A + s_ * 24 * KB + 8 * KB, [128, KC, 512], BF16, f"wu{s_}"),
                        d=M.at(R_A + s_ * 24 * KB + 16 * KB, [128, 4, D], BF16, f"wd{s_}")) for s_ in range(2)]
            hT = [M.at(R_Y + s_ * 16 * KB, [128, 4, S], BF16, f"hT{s_}") for s_ in range(2)]

            def ffn_load(ii):
                g_d, u_d, d_d, fb, nfc, e_ = items[ii]
                s_ = ii % 2
                n_ = nfc * 128
                gv = g_d.rearrange("(c p) n -> p c n", p=128)
                uv = u_d.rearrange("(c p) n -> p c n", p=128)
                dv = d_d[fb * 512:fb * 512 + n_, :].rearrange("(c p) n -> p c n", p=128)
                wload(wst[s_]["g"][:, :, 0:n_], gv[:, :, fb * 512:fb * 512 + n_], f"wg{s_}", [f"wg{s_}"])
                wload(wst[s_]["u"][:, :, 0:n_], uv[:, :, fb * 512:fb * 512 + n_], f"wu{s_}", [f"wu{s_}"])
                for hf_ in range(2):
                    wload(wst[s_]["d"][:, 0:nfc, hf_ * 512:(hf_ + 1) * 512], dv[:, :, hf_ * 512:(hf_ + 1) * 512], f"wd{s_}", [f"wd{s_}"])

            hstep = {"n": 0}

            def ffn_hidden(ii):
                g_d, u_d, d_d, fb, nfc, e_ = items[ii]
                s_ = ii % 2
                for fc in range(nfc):
                    for tb in range(4):
                        n = hstep["n"]
                        hstep["n"] += 1
                        bg, bu = (0, 1) if n % 2 == 0 else (2, 3)
                        tsl = slice(tb * 512, (tb + 1) * 512)

                        def mm(h, fc=fc, tsl=tsl, bg=bg, bu=bu, s_=s_):
                            last = None
                            for k in range(KC):
                                last = h.matmul(ps[bg], wst[s_]["g"][:, k, fc * 128:(fc + 1) * 128], xT[:, k, tsl],
                                                start=(k == 0), stop=(k == KC - 1))
                            for k in range(KC):
                                last = h.matmul(ps[bu], wst[s_]["u"][:, k, fc * 128:(fc + 1) * 128], xT[:, k, tsl],
                                                start=(k == 0), stop=(k == KC - 1))
                            return last
                        P.op("pe", mm, reads=[f"wg{s_}", f"wu{s_}"], writes=[f"ps{bg}", f"ps{bu}"])
                        tmp = ytmp[:, n % 2, :]
                        P.op("act", lambda h, tmp=tmp, bg=bg: h.activation(tmp, ps[bg], AF.Silu),
                             reads=[f"ps{bg}"], writes=[f"sg{n % 2}"])
                        P.op("dve", lambda h, tmp=tmp, bu=bu, fc=fc, tsl=tsl, s_=s_: h.tensor_tensor(hT[s_][:, fc, tsl], ps[bu], tmp, ALU.mult),
                             reads=[f"ps{bu}", f"sg{n % 2}"], writes=[f"hT{s_}_{tb}"])

            dstep = {"n": 0}

            def ffn_down(ii):
                g_d, u_d, d_d, fb, nfc, e_ = items[ii]
                s_ = ii % 2
                for t in range(NT):
                    for half in range(2):
                        n = dstep["n"]
                        dstep["n"] += 1
                        b_ = 4 + n % 4

                        def mm(h, t=t, half=half, b_=b_, s_=s_, nfc=nfc):
                            last = None
                            for fc in range(nfc):
                                last = h.matmul(ps[b_], hT[s_][:, fc, t * 128:(t + 1) * 128], wst[s_]["d"][:, fc, half * 512:(half + 1) * 512],
                                                start=(fc == 0), stop=(fc == nfc - 1))
                            return last
                        P.op("pe", mm, reads=[f"wd{s_}", f"hT{s_}_{t // 4}"], writes=[f"ps{b_}"])
                        xs = X[:, t, half * 512:(half + 1) * 512]
                        sc_ = 1.0 if e_ is None else comb[:, t, e_:e_ + 1]
                        P.op("dve", lambda h, xs=xs, b_=b_, sc_=sc_: h.scalar_tensor_tensor(xs, ps[b_], sc_, xs, ALU.mult, ALU.add),
                             reads=[f"ps{b_}", f"X{t}", "comb"], writes=[f"X{t}"])

            ffn_load(0)
            if len(items) > 1:
                ffn_load(1)
            for ii in range(len(items)):
                ffn_hidden(ii)
                if ii > 0:
                    ffn_down(ii - 1)
                    if ii + 1 < len(items):
                        ffn_load(ii + 1)
            ffn_down(len(items) - 1)

        else:
            moe_routed(layer)
            P.barrier()
        ln_A(0)
        for t in range(NT):
            if t + 1 < NT:
                ln_A(t + 1)
            ln_B(t, (0, 1) if t % 2 == 0 else (2, 3), "act" if t % 2 == 0 else "dve", do_T=(layer + 1 < DEPTH))
            if layer + 1 == DEPTH and debug is None:
                P.op("sp", lambda h, t=t: h.dma_start(out=out_d[t * 128:(t + 1) * 128, :], in_=X[:, t, :]),
                     reads=[f"X{t}"], dma="out")
        P.barrier()
        if debug == f"l{layer}":
            dbg("dbg_X", X[:], [128, NT, D], F32, reads=[])
            raise StopBuild()

    try:
        for layer in range(n_layers):
            do_layer(layer)
    except StopBuild:
        P.barrier()

    P.start_phase("pout")
    if debug is not None or n_layers < DEPTH:
        for t in range(NT):
            P.op("sp", lambda h, t=t: h.dma_start(out=out_d[t * 128:(t + 1) * 128, :], in_=X[:, t, :]),
                 reads=[f"X{t}"], dma="out")
    P.final_wait("sp")
    P.flush()
    return nc, dbg_outs


def _perm_w_in(w_in):
    idx = []
    for hf in range(2):
        idx += list(range(256 * hf, 256 * hf + 256))
        idx += list(range(512 + 256 * hf, 512 + 256 * hf + 256))
        idx += list(range(1024 + 256 * hf, 1024 + 256 * hf + 256))
    for c in range(4):
        idx += list(range(1536 + c * 64, 1536 + c * 64 + 64))
        idx += list(range(1536 + (c + 4) * 64, 1536 + (c + 4) * 64 + 64))
    idx += list(range(2048, 4352))
    return np.ascontiguousarray(w_in[:, :, np.asarray(idx)])


def _na_bias_tables(rpb):
    L = rpb.shape[0]
    kc = np.arange(64)[:, None]
    qc = np.arange(64)[None, :]
    qcs = np.clip(qc - 8, 0, 48)
    colvalid = (kc >= qcs) & (kc < qcs + 16)
    dc = np.clip(kc - qc + 15, 0, 30)
    out = np.full((L, 128, 8, 1536), NEG, np.float32)

    def Cmat(l, h, e, masked):
        if e < -7 or e > 7 or (masked and not (-4 <= e <= 3)):
            return np.full((64, 64), NEG, np.float32)
        return np.where(colvalid, rpb[l, h, e + 7][dc], NEG).astype(np.float32)

    for l in range(L):
        for h in range(8):
            for masked, e_hi, n, col0 in ((True, 4, 10, 0), (False, 6, 14, 640)):
                for idx in range(n):
                    e = e_hi - idx
                    for kl in range(2):
                        out[l, kl * 64:(kl + 1) * 64, h, col0 + idx * 64:col0 + (idx + 1) * 64] = Cmat(l, h, e + kl, masked)
    return out


def _const_tables():
    pos = np.arange(S, dtype=np.float32)
    inv_freq = (1.0 / (500000.0 ** (np.arange(0, 16, 2, dtype=np.float32) / 16.0))).astype(np.float32)
    ang = pos[:, None] * inv_freq[None, :]
    cos = np.cos(ang).astype(np.float32).T
    sin = np.sin(ang).astype(np.float32).T
    cs = np.zeros((2, 128, S), np.float32)
    cs[0] = 1.0
    for half in range(2):
        o = half * 64
        cs[0, o:o + 8] = cos
        cs[0, o + 8:o + 16] = cos
        cs[1, o:o + 8] = -sin
        cs[1, o + 8:o + 16] = sin
    cst = np.zeros((128, 1152), np.float32)
    for m in range(128):
        d = m % 64
        partner = m + 8 if d < 8 else (m - 8 if d < 16 else m)
        cst[partner, m] = 1.0
    ki = np.arange(128)[:, None]
    qi = np.arange(128)[None, :]
    mP = np.where(ki >= qi, 0.0, NEG).astype(np.float32)
    mN = np.where(ki <= qi, 0.0, NEG).astype(np.float32)
    cst[:, 128:640] = np.tile(mP, (1, 4))
    cst[:, 640:1152] = np.tile(mN, (1, 4))
    return cs, cst


def _const2():
    c = np.zeros((128, 644), np.float32)
    k = np.arange(128)[:, None]
    m = np.arange(128)[None, :]
    c[:, 0:128] = (k < m).astype(np.float32)
    c[:, 128:132] = np.arange(4)[None, :] * 128 + np.arange(128)[:, None]
    c[:, 132:644] = np.arange(512)[None, :]
    return c


def make_in_maps(inputs):
    f = lambda a: np.ascontiguousarray(np.asarray(a, dtype=np.float32))
    x = f(inputs["x"])
    lnp = np.stack([
        np.stack([f(inputs["emb_ln_g"]), f(inputs["emb_ln_b"])]),
        np.stack([f(inputs["ln1_g"])[0], f(inputs["ln1_b"])[0]]),
        np.stack([f(inputs["ln2_g"])[0], f(inputs["ln2_b"])[0]]),
        np.stack([f(inputs["ln1_g"])[1], f(inputs["ln1_b"])[1]]),
        np.stack([f(inputs["ln2_g"])[1], f(inputs["ln2_b"])[1]]),
    ]).astype(np.float32)
    ident = np.eye(128, dtype=np.float32)
    cs, cst = _const_tables()
    shared = {"lnp": lnp, "ident": ident, "w_in": _perm_w_in(f(inputs["w_in"])),
              "nabias": _na_bias_tables(f(inputs["na_rpb"])), "cs": cs, "cst": cst,
              "sink": f(inputs["sw_sink"]),
              "bgate": np.ascontiguousarray(f(inputs["b_gate"]).reshape(DEPTH, 16, 128).transpose(0, 2, 1)),
              "wbr": np.ascontiguousarray(np.stack([f(inputs["w_branch_na"]), f(inputs["w_branch_sw"])], axis=1)),
              "wout": f(inputs["w_out"]),
              "ffn_g": f(inputs["ffn_w_gate"])[0], "ffn_u": f(inputs["ffn_w_up"])[0], "ffn_d": f(inputs["ffn_w_down"])[0],
              "moe_g": f(inputs["moe_w_gate"])[0], "moe_u": f(inputs["moe_w_up"])[0], "moe_d": f(inputs["moe_w_down"])[0],
              "wr": f(inputs["moe_router"])[0], "cst2": _const2()}
    maps = []
    for c in range(NCORES):
        m = dict(shared)
        m["x"] = np.ascontiguousarray(x[c])
        maps.append(m)
    return maps


_CACHE = {}


def kernel(**inputs):
    if "nc" not in _CACHE:
        _CACHE["nc"] = build()[0]
    nc = _CACHE["nc"]
    in_maps = make_in_maps(inputs)
    res = run_bass_kernel_spmd(nc, in_maps, core_ids=list(range(NCORES)))
    out = np.stack([np.asarray(r["out"], dtype=np.float32).reshape(S, D) for r in res.results], axis=0)
    return out
```

```python
import numpy as np
import ml_dtypes
import concourse.bass as bass
import concourse.mybir as mybir
from concourse.bass_utils import run_bass_kernel_spmd

F32 = mybir.dt.float32
BF16 = mybir.dt.bfloat16
AF = mybir.ActivationFunctionType
ALU = mybir.AluOpType
AX = mybir.AxisListType

NCORES = 8
D = 1024
S = 2048
NT = 16
KC = 8
DEPTH = 2
ALPHA = (2 * DEPTH) ** 0.25
EPS = 1e-5
PROJ = 4352
FF_DENSE = 2816
FF_EXP = 3584
NEXP = 8
NEG = -30000.0
import os
VAR = os.environ.get('KVAR', '')


class Prog:
    ENGS = ("pe", "act", "dve", "pool", "sp")

    def __init__(self, nc):
        self.nc = nc
        self.h = {"pe": nc.tensor, "act": nc.scalar, "dve": nc.vector, "pool": nc.gpsimd, "sp": nc.sync}
        self.q = {e: [] for e in self.ENGS}
        self.sem = {}
        self.cnt = {}
        self.waited = {e: {} for e in self.ENGS}
        self.res = {}
        self.dma_sems = {}
        self._ctx = []
        self.cond = None
        self.regs = {}

    def new_sem(self, name):
        cm = self.nc.semaphore(name)
        s = cm.__enter__()
        self._ctx.append(cm)
        return s

    def start_phase(self, name):
        for e in self.ENGS:
            self.sem[e] = self.new_sem(f"{name}_{e}")
            self.cnt[e] = 0

    def dma_sem(self, key):
        if key not in self.dma_sems:
            self.dma_sems[key] = [self.new_sem(f"dma_{key}"), 0]
        return self.dma_sems[key]

    def _need(self, eng, tok):
        sem, val, src = tok
        w = self.waited[eng]
        k = id(sem)
        if w.get(k, 0) >= val:
            return False
        w[k] = val
        return True

    def op(self, eng, fn, reads=(), writes=(), dma=None):
        writes = list(writes) + [r for r in reads if r.startswith("ps") and r not in writes]
        reads = [r for r in reads if not r.startswith("ps")]
        toks = []
        for r in reads:
            st = self.res.get(r)
            if st and st["w"] is not None:
                toks.append(st["w"])
        for w_ in writes:
            st = self.res.get(w_)
            if st:
                if st["w"] is not None and (st["w"][2] != eng or st["w"][3]):
                    toks.append(st["w"])
                for t in st["r"]:
                    if t[2] != eng or t[3]:
                        toks.append(t)
        waits = []
        for t in toks:
            if self._need(eng, t[:3]):
                waits.append((t[0], t[1]))
        if dma is not None:
            ds = self.dma_sem(dma)
            ds[1] += 16
            tok = (ds[0], ds[1], eng, True)
            sem, inc = ds[0], 16
        else:
            self.cnt[eng] += 1
            tok = (self.sem[eng], self.cnt[eng], eng, False)
            sem, inc = self.sem[eng], 1

        def emit(h, waits=waits, fn=fn, sem=sem, inc=inc):
            for s_, v_ in waits:
                h.wait_ge(s_, v_)
            fn(h).then_inc(sem, inc)

        if self.cond is not None:
            self.cond["q"][eng].append(emit)
            d_ = self.cond["inc"][eng]
            d_[id(sem)] = (sem, d_.get(id(sem), (sem, 0))[1] + inc)
        else:
            self.q[eng].append(emit)
        for r in reads:
            self.res.setdefault(r, {"w": None, "r": []})["r"].append(tok)
        for w_ in writes:
            self.res[w_] = {"w": tok, "r": []}
        return tok

    def get_reg(self, eng, h):
        if eng not in self.regs:
            self.regs[eng] = h.alloc_register(f"nreg_{eng}")
        return self.regs[eng]

    def regload(self, engs, ap, reads):
        for e in engs:
            self.op(e, lambda h, e=e: h.reg_load(self.get_reg(e, h), ap), reads=reads)

    def begin_cond(self, thr):
        import copy
        pre = {id(self.sem[e]): self.cnt[e] for e in self.ENGS}
        for k, (s_, c_) in self.dma_sems.items():
            pre[id(s_)] = c_
        c = {"thr": thr, "q": {e: [] for e in self.ENGS}, "inc": {e: {} for e in self.ENGS},
             "waited": copy.deepcopy(self.waited), "pre": pre, "parent": self.cond}
        self.cond = c

    def end_cond(self):
        c = self.cond
        parent = c["parent"]
        self.cond = parent
        for e in self.ENGS:
            q = c["q"][e]
            if not q:
                continue
            incs = list(c["inc"][e].values())

            def emit(h, q=q, incs=incs, e=e, thr=c["thr"], pre=c["pre"]):
                v = h.snap(self.get_reg(e, h), min_val=0, max_val=4096)
                with h.If(v > thr):
                    for f in q:
                        f(h)
                with h.Else():
                    for (s_, n_) in incs:
                        p_ = pre.get(id(s_), 0)
                        if p_ > 0:
                            h.wait_ge(s_, p_)
                    for (s_, n_) in incs:
                        h.sem_inc(s_, n_)
            if parent is not None:
                parent["q"][e].append(emit)
                d_ = parent["inc"][e]
                for (s_, n_) in incs:
                    d_[id(s_)] = (s_, d_.get(id(s_), (s_, 0))[1] + n_)
            else:
                self.q[e].append(emit)
        self.waited = c["waited"]

    def barrier(self):
        toks = []
        for e in self.ENGS:
            if self.cnt[e] > 0:
                toks.append((self.sem[e], self.cnt[e], e))
        for k, (s_, c_) in self.dma_sems.items():
            if c_ > 0:
                toks.append((s_, c_, "dma"))
        for e in self.ENGS:
            waits = [(t[0], t[1]) for t in toks if t[2] != e and self._need(e, t)]
            if waits:
                def emit(h, waits=waits):
                    for s_, v_ in waits:
                        h.wait_ge(s_, v_)
                self.q[e].append(emit)
        self.res = {}

    def final_wait(self, eng="sp"):
        waits = [(s_, c_) for k, (s_, c_) in self.dma_sems.items() if c_ > 0]

        def emit(h, waits=waits):
            for s_, v_ in waits:
                h.wait_ge(s_, v_)
        self.q[eng].append(emit)

    def flush(self):
        nc = self.nc
        with nc.Block() as block:
            for e, reg in (("sp", block.sync), ("pool", block.gpsimd), ("pe", block.tensor),
                           ("act", block.scalar), ("dve", block.vector)):
                q = self.q[e]
                if not q:
                    continue

                def body(h, q=q):
                    for f in q:
                        f(h)
                reg(body)
        for cm in reversed(self._ctx):
            cm.__exit__(None, None, None)


class StopBuild(Exception):
    pass


class Mem:
    def __init__(self, nc):
        self.nc = nc
        self.base = (nc.sbuf_base + 63) // 64 * 64
        self.top = nc.sbuf_top
        self.n = 0

    def at(self, off, shape, dtype, name=None):
        self.n += 1
        nbytes = int(np.prod(shape[1:])) * (2 if dtype == BF16 else 4)
        assert self.base + off + nbytes <= self.top, (name, off, nbytes, self.top - self.base)
        return self.nc.alloc_sbuf_tensor_at(name or f"t{self.n}", list(shape), dtype, offset=self.base + off)


def build(debug=None, n_layers=DEPTH):
    nc = bass.Bass("TRN2", target_bir_lowering=False)
    P = Prog(nc)
    M = Mem(nc)
    dbg_outs = {}

    x_d = nc.dram_tensor("x", [S, D], F32, kind="ExternalInput").ap()
    lnp_d = nc.dram_tensor("lnp", [5, 2, D], F32, kind="ExternalInput").ap()
    ident_d = nc.dram_tensor("ident", [128, 128], F32, kind="ExternalInput").ap()
    out_d = nc.dram_tensor("out", [S, D], F32, kind="ExternalOutput").ap()
    w_in_d = nc.dram_tensor("w_in", [DEPTH, D, PROJ], F32, kind="ExternalInput").ap()
    nabias_d = nc.dram_tensor("nabias", [DEPTH, 128, 8, 1536], F32, kind="ExternalInput").ap()
    cs_d = nc.dram_tensor("cs", [2, 128, S], F32, kind="ExternalInput").ap()
    cst_d = nc.dram_tensor("cst", [128, 1152], F32, kind="ExternalInput").ap()
    sink_d = nc.dram_tensor("sink", [DEPTH, 8], F32, kind="ExternalInput").ap()
    bgate_d = nc.dram_tensor("bgate", [DEPTH, 128, 16], F32, kind="ExternalInput").ap()
    wbr_d = nc.dram_tensor("wbr", [DEPTH, 2, 512, D], F32, kind="ExternalInput").ap()
    wout_d = nc.dram_tensor("wout", [DEPTH, D, D], F32, kind="ExternalInput").ap()
    fg_d = nc.dram_tensor("ffn_g", [D, FF_DENSE], F32, kind="ExternalInput").ap()
    fu_d = nc.dram_tensor("ffn_u", [D, FF_DENSE], F32, kind="ExternalInput").ap()
    fd_d = nc.dram_tensor("ffn_d", [FF_DENSE, D], F32, kind="ExternalInput").ap()
    mg_d = nc.dram_tensor("moe_g", [NEXP, D, FF_EXP], F32, kind="ExternalInput").ap()
    mu_d = nc.dram_tensor("moe_u", [NEXP, D, FF_EXP], F32, kind="ExternalInput").ap()
    md_d = nc.dram_tensor("moe_d", [NEXP, FF_EXP, D], F32, kind="ExternalInput").ap()
    wr_d = nc.dram_tensor("wr", [D, NEXP], F32, kind="ExternalInput").ap()
    cst2_d = nc.dram_tensor("cst2", [128, 128 + 4 + 512], F32, kind="ExternalInput").ap()

    KB = 1024
    X = M.at(0, [128, NT, D], F32, "X")
    xT = M.at(64 * KB, [128, KC, S], BF16, "xT")
    R_Y = 96 * KB
    R_A = 128 * KB
    R_B = 176 * KB
    R_M = 196 * KB
    gb = M.at(R_B + 12 * KB, [128, 2, D], F32, "gb")
    mo = {"o": R_M}

    def misc(shape, dtype, name):
        nb = int(np.prod(shape[1:])) * (2 if dtype == BF16 else 4)
        t = M.at(mo["o"], shape, dtype, name)
        mo["o"] += (nb + 63) // 64 * 64
        return t
    ident = misc([128, 128], F32, "ident")
    identb = misc([128, 128], BF16, "identb")
    stats = misc([128, 2, 12], F32, "stats")
    mv = misc([128, 2, 8], F32, "mv")
    eps_t = misc([128, 4], F32, "eps")
    esink = misc([128, 8], F32, "esink")
    rden = misc([128, 2, 8], F32, "rden")
    ytmp = misc([128, 2, 512], F32, "ytmp")
    maskPN = misc([128, 2, 512], BF16, "maskPN")
    prot = misc([128, 128], BF16, "prot")
    bgate = misc([128, 16], F32, "bgate")
    wr = misc([128, KC, NEXP], F32, "wr")
    comb = misc([128, NT, NEXP], F32, "comb")
    rt = misc([128, 4, 8], F32, "rt")
    maskt = misc([128, NT, NEXP], F32, "maskt")
    rank = misc([128, NT, NEXP], F32, "rank")
    rankm = misc([128, NT, NEXP], F32, "rankm")
    offs = misc([128, NEXP], F32, "offs")
    rsh = misc([128, NT], F32, "rsh")
    cntf = misc([128, NEXP], F32, "cntf")
    cnti = misc([128, NEXP], mybir.dt.int32, "cnti")
    slotid = misc([128, 4], F32, "slotid")
    negslot = misc([128, 4], F32, "negslot")
    ltri = misc([128, 128], F32, "ltri")
    ones = misc([128, 128], F32, "ones")
    Rb = misc([128, 128], F32, "Rb")

    PS = nc.alloc_psum_tensor("PS", [128, 8, 512], F32)
    ps = [PS[:, i, :] for i in range(8)]

    def dbg(name, src_ap, shape, dtype, reads):
        t = nc.dram_tensor(name, list(shape), dtype, kind="ExternalOutput").ap()
        dbg_outs[name] = (shape, dtype)
        P.op("sp", lambda h: h.dma_start(out=t, in_=src_ap), reads=reads, dma="dbg")

    def load_gb(idx):
        src = lnp_d[idx].partition_broadcast(128) if hasattr(lnp_d[idx], "partition_broadcast") else None
        P.op("sp", lambda h: h.dma_start(out=gb[:], in_=src), writes=["gb"], dma="gb")

    def ln_tile(t, tp_banks, evac_eng, router=None, do_T=True, xtok=None):
        ln_A(t)
        ln_B(t, tp_banks, evac_eng, router, do_T, xtok)

    def ln_A(t):
        xt = X[:, t, :]
        sl = t % 2
        st = stats[:, sl, :]
        m = mv[:, sl, :]
        rx = f"X{t}"
        rs = f"st{sl}"
        P.op("dve", lambda h: (h.bn_stats(st[:, 0:6], xt[:, 0:512]), h.bn_stats(st[:, 6:12], xt[:, 512:1024]))[1],
             reads=[rx], writes=[rs])
        P.op("dve", lambda h: h.bn_aggr(m[:, 0:2], st[:, 0:12]), reads=[rs], writes=[rs + "m"])
        P.op("act", lambda h: h.activation(m[:, 2:3], m[:, 1:2], AF.Sqrt, bias=eps_t[:, 0:1], scale=1.0),
             reads=[rs + "m"], writes=[rs + "s"])
        P.op("dve", lambda h: h.reciprocal(m[:, 3:4], m[:, 2:3]), reads=[rs + "s"], writes=[rs + "r"])
        P.op("dve", lambda h: h.scalar_tensor_tensor(m[:, 4:5], m[:, 0:1], -1.0, m[:, 3:4], ALU.mult, ALU.mult),
             reads=[rs + "r", rs + "m"], writes=[rs + "n"])


    def ln_B(t, tp_banks, evac_eng, router=None, do_T=True, xtok=None):
        xt = X[:, t, :]
        sl = t % 2
        m = mv[:, sl, :]
        rx = f"X{t}"
        rs = f"st{sl}"
        P.op("act", lambda h: h.activation(xt, xt, AF.Identity, bias=m[:, 4:5], scale=m[:, 3:4]),
             reads=[rx, rs + "n", rs + "r"], writes=[rx])
        gbe = "pool" if "poolgb" in VAR else "dve"
        gbe0 = "pool" if "poolg" in VAR else "dve"
        P.op(gbe0, lambda h: h.tensor_tensor(xt, xt, gb[:, 0, :], ALU.mult), reads=[rx, "gb"], writes=[rx])
        P.op(gbe, lambda h: h.tensor_tensor(xt, xt, gb[:, 1, :], ALU.add), reads=[rx, "gb"], writes=[rx])
        if xtok is not None:
            P.op("act", lambda h: h.activation(xtok[:, t, :], xt, AF.Copy), reads=[rx], writes=[f"xtok{t}"])
        if not do_T and router is None:
            return
        b0, b1 = tp_banks

        def tp(h):
            last = None
            for c in range(KC):
                bank = ps[b0] if c < 4 else ps[b1]
                last = h.transpose(bank[:, (c % 4) * 128:(c % 4 + 1) * 128], xt[:, c * 128:(c + 1) * 128], ident[:])
            return last
        P.op("pe", tp, reads=[rx, "ident"], writes=[f"ps{b0}", f"ps{b1}"])
        for half, b in ((0, b0), (1, b1)):
            dst = xT[:, half * 4:(half + 1) * 4, t * 128:(t + 1) * 128]
            src = ps[b].rearrange("p (c n) -> p c n", c=4)
            if not do_T:
                pass
            elif evac_eng == "act":
                P.op("act", lambda h, dst=dst, src=src: h.activation(dst, src, AF.Copy),
                     reads=[f"ps{b}"], writes=[f"xT{t}"])
            else:
                P.op("dve", lambda h, dst=dst, src=src: h.tensor_copy(dst, src),
                     reads=[f"ps{b}"], writes=[f"xT{t}"])
            if router is not None:
                x32 = router
                o_eng = "dve" if evac_eng == "act" else "act"
                d32 = x32[:, half * 4:(half + 1) * 4, :]
                if o_eng == "act":
                    P.op("act", lambda h, d32=d32, src=src: h.activation(d32, src, AF.Copy),
                         reads=[f"ps{b}"], writes=[f"x32_{half}"])
                else:
                    P.op("dve", lambda h, d32=d32, src=src: h.tensor_copy(d32, src),
                         reads=[f"ps{b}"], writes=[f"x32_{half}"])
        if router is not None:
            x32 = router
            lg = ps[b0][:, 0:NEXP]

            def rmm(h):
                last = None
                for k in range(KC):
                    last = h.matmul(lg, x32[:, k, :], wr[:, k, :], start=(k == 0), stop=(k == KC - 1))
                return last
            P.op("pe", rmm, reads=["x32_0", "x32_1", "wr"], writes=[f"ps{b0}"])
            P.op("dve", lambda h: h.tensor_copy(rank[:, t, :], lg), reads=[f"ps{b0}"], writes=["lgall"])

    def router_finish():
        L = rank[:]
        A = rankm[:]
        B = Rb[:].rearrange("p (t e) -> p t e", e=NEXP)
        rtf = rt[:].rearrange("p a e -> p (a e)")
        m1, m2, den = rsh[:], rtf[:, 0:16], rtf[:, 16:32]
        bc = lambda v: v.unsqueeze(2).to_broadcast([128, NT, NEXP])
        P.op("dve", lambda h: h.tensor_reduce(m1, L, AX.X, ALU.max), reads=["lgall"], writes=["r_m1"])
        P.op("dve", lambda h: h.tensor_tensor(A, L, bc(m1), ALU.is_equal), reads=["lgall", "r_m1"], writes=["r_A"])
        P.op("dve", lambda h: h.scalar_tensor_tensor(B, A, -1e30, L, ALU.mult, ALU.add), reads=["r_A", "lgall"], writes=["r_B"])
        P.op("dve", lambda h: h.tensor_reduce(m2, B, AX.X, ALU.max), reads=["r_B"], writes=["r_m2"])
        P.op("dve", lambda h: h.tensor_tensor(maskt[:], L, bc(m2), ALU.is_ge), reads=["lgall", "r_m2"], writes=["maskt"])
        P.op("dve", lambda h: h.tensor_tensor(A, L, bc(m1), ALU.subtract), reads=["lgall", "r_m1", "r_A"], writes=["r_A"])
        P.op("act", lambda h: h.activation(B, A, AF.Exp), reads=["r_A", "r_B"], writes=["r_B"])
        P.op("dve", lambda h: h.tensor_tensor(A, maskt[:], B, ALU.mult), reads=["maskt", "r_B", "r_A"], writes=["r_A"])
        P.op("dve", lambda h: h.tensor_reduce(den, A, AX.X, ALU.add), reads=["r_A"], writes=["r_den"])
        P.op("dve", lambda h: h.reciprocal(den, den), reads=["r_den"], writes=["r_den"])
        P.op("dve", lambda h: h.tensor_tensor(comb[:], A, bc(den), ALU.mult), reads=["r_A", "r_den"], writes=["comb"])

    wq = {"n": 0}

    def wload(dst, src, key, writes):
        P.op("pool", lambda h: h.dma_start(out=dst, in_=src, max_dma_last_dim=2048), writes=writes, dma=key)

    def gemm_fm(w_slot, wcol0, dst_fn, evac_fn, banks, wres, xres_fn, tag):
        for tb in range(4):
            b = banks[tb % len(banks)]

            def mm(h, tb=tb, b=b):
                last = None
                for k in range(KC):
                    last = h.matmul(ps[b], w_slot[:, k, wcol0:wcol0 + 128], xT[:, k, tb * 512:(tb + 1) * 512],
                                    start=(k == 0), stop=(k == KC - 1))
                return last
            P.op("pe", mm, reads=[wres] + xres_fn(tb), writes=[f"ps{b}"])
            evac_fn(tb, b)

    def xT_res(tb):
        return [f"xT{t}" for t in range(tb * 4, tb * 4 + 4)]

    P.start_phase("p0")
    P.op("pool", lambda h: h.memset(eps_t[:], EPS), writes=["eps"])
    P.op("sp", lambda h: h.dma_start(out=ident[:], in_=ident_d), writes=["ident"], dma="c_ident")
    if debug != "p0x" or "a" in VAR:
        wload(identb[:], ident_d, "c_identb", ["identb"])
    load_gb(0)
    for t in range(NT):
        P.op("sp", lambda h, t=t: h.dma_start(out=X[:, t, :], in_=x_d[t * 128:(t + 1) * 128, :]),
             writes=[f"X{t}"], dma=f"x{t}")
    P.res.setdefault("st0m", {"w": None, "r": []})
    P.res["st0m"] = {"w": P.res["eps"]["w"], "r": []}
    P.res["st1m"] = {"w": P.res["eps"]["w"], "r": []}
    ln_A(0)
    for t in range(NT):
        if t + 1 < NT:
            ln_A(t + 1)
        ln_B(t, (0, 1) if t % 2 == 0 else (2, 3), "act" if t % 2 == 0 else "dve")
    P.barrier()

    if debug in ("p0", "p0x"):
        dbg("dbg_X", X[:], [128, NT, D], F32, reads=[])
        dbg("dbg_xT", xT[:], [128, KC, S], BF16, reads=[])
        n_layers = 0

    yT = [M.at(R_Y, [128, 4, S], BF16, "yTna"), M.at(R_Y + 16 * KB, [128, 4, S], BF16, "yTsw")]
    xtok_g = M.at(64 * KB, [128, NT, D], BF16, "xtok")

    def moe_routed(layer):
        xtok = xtok_g
        wst = [dict(g=M.at(R_A + s_ * 24 * KB, [128, KC, 512], BF16, f"wg{s_}"),
                    u=M.at(R_A + s_ * 24 * KB + 8 * KB, [128, KC, 512], BF16, f"wu{s_}"),
                    d=M.at(R_A + s_ * 24 * KB + 16 * KB, [128, 4, D], BF16, f"wd{s_}")) for s_ in range(2)]
        xg = M.at(R_Y, [128, KC, 512], BF16, "xg")
        yacc = M.at(R_Y + 8 * KB, [128, 4, D], F32, "yacc")
        hTs = [M.at(R_Y + 24 * KB + s_ * 4 * KB, [128, 4, 512], BF16, f"hTs{s_}") for s_ in range(2)]
        yslot = M.at(R_B, [128, 4, D], BF16, "yslot")
        selt = [M.at(R_B + 8 * KB + i * KB, [128, 512], BF16, f"selt{i}") for i in range(2)]
        iota512 = M.at(R_B + 10 * KB, [128, 512], F32, "iota512")
        selT = [maskPN[:, i, :].rearrange("p (s n) -> p s n", s=4) for i in range(2)]

        P.op("sp", lambda h: h.dma_start(out=ltri[:], in_=cst2_d[:, 0:128]), writes=["ltri"], dma="c_ltri")
        P.op("sp", lambda h: h.dma_start(out=slotid[:], in_=cst2_d[:, 128:132]), writes=["slotid"], dma="c_slot")
        P.op("sp", lambda h: h.dma_start(out=iota512[:], in_=cst2_d[:, 132:644]), writes=["iota"], dma="c_iota")
        P.op("pool", lambda h: h.memset(ones[:], 1.0), writes=["ones"])
        P.op("dve", lambda h: h.tensor_scalar(negslot[:], slotid[:], -1.0, None, ALU.mult), reads=["slotid"], writes=["negslot"])
        one_t = ones[:, 0:1]
        mflat = maskt[:].rearrange("p t e -> p (t e)")
        P.op("pe", lambda h: h.matmul(ps[0][:, 0:128], ones[:], mflat, start=True, stop=True),
             reads=["ones", "maskt"], writes=["ps0"])
        P.op("pe", lambda h: h.matmul(ps[1][:, 0:128], ltri[:], mflat, start=True, stop=True),
             reads=["ltri", "maskt"], writes=["ps1"])
        cps = ps[0][:, 0:128].rearrange("p (t e) -> p t e", e=NEXP)
        wps = ps[1][:, 0:128].rearrange("p (t e) -> p t e", e=NEXP)
        P.op("dve", lambda h: h.memset(offs[:], 0.0), writes=["offs"])
        for t in range(NT):
            P.op("dve", lambda h, t=t: h.tensor_tensor(rank[:, t, :], wps[:, t, :], offs[:], ALU.add),
                 reads=["ps1", "offs"], writes=["rank"])
            P.op("dve", lambda h, t=t: h.tensor_tensor(offs[:], cps[:, t, :], offs[:], ALU.add),
                 reads=["ps0", "offs", "rank"], writes=["offs"])
        P.op("dve", lambda h: h.tensor_copy(cnti[:], offs[:]), reads=["offs"], writes=["cnti"])
        rkf = rank[:].rearrange("p t e -> p (t e)")
        rmf = rankm[:].rearrange("p t e -> p (t e)")
        P.op("dve", lambda h: h.scalar_tensor_tensor(rmf, rkf, 1.0, mflat, ALU.add, ALU.mult),
             reads=["rank", "maskt"], writes=["rankm"])
        P.op("dve", lambda h: h.tensor_scalar(rmf, rmf, -1.0, None, ALU.add), reads=["rankm"], writes=["rankm"])
        if debug == "route":
            dbg("dbg_rankm", rankm[:], [128, NT, NEXP], F32, reads=["rankm"])
            dbg("dbg_comb", comb[:], [128, NT, NEXP], F32, reads=["comb"])
            dbg("dbg_cnti", cnti[:], [128, NEXP], mybir.dt.int32, reads=["cnti"])
            raise StopBuild()

        nblk = FF_EXP // 512
        cnt = {"h": 0, "d": 0, "w": 0}

        def load_w(e, fb):
            s_ = cnt["w"] % 2
            cnt["w"] += 1
            gv = mg_d[e].rearrange("(c p) n -> p c n", p=128)
            uv = mu_d[e].rearrange("(c p) n -> p c n", p=128)
            dv = md_d[e][fb * 512:(fb + 1) * 512, :].rearrange("(c p) n -> p c n", p=128)
            wload(wst[s_]["g"][:], gv[:, :, fb * 512:(fb + 1) * 512], f"wg{s_}", [f"wg{s_}"])
            wload(wst[s_]["u"][:], uv[:, :, fb * 512:(fb + 1) * 512], f"wu{s_}", [f"wu{s_}"])
            for hf_ in range(2):
                wload(wst[s_]["d"][:, :, hf_ * 512:(hf_ + 1) * 512], dv[:, :, hf_ * 512:(hf_ + 1) * 512], f"wd{s_}", [f"wd{s_}"])
            return s_

        def hidden(s_, hs, nsl):
            for fc in range(4):
                n = cnt["h"]
                cnt["h"] += 1
                bg, bu = (0, 1) if n % 2 == 0 else (2, 3)

                def mm(h, fc=fc, bg=bg, bu=bu):
                    last = None
                    for k in range(KC):
                        last = h.matmul(ps[bg][:, 0:nsl], wst[s_]["g"][:, k, fc * 128:(fc + 1) * 128], xg[:, k, 0:nsl],
                                        start=(k == 0), stop=(k == KC - 1))
                    for k in range(KC):
                        last = h.matmul(ps[bu][:, 0:nsl], wst[s_]["u"][:, k, fc * 128:(fc + 1) * 128], xg[:, k, 0:nsl],
                                        start=(k == 0), stop=(k == KC - 1))
                    return last
                P.op("pe", mm, reads=[f"wg{s_}", f"wu{s_}"] + [f"xg{c}" for c in range(KC)], writes=[f"ps{bg}", f"ps{bu}"])
                tmp = ytmp[:, n % 2, 0:nsl]
                P.op("act", lambda h, tmp=tmp, bg=bg: h.activation(tmp, ps[bg][:, 0:nsl], AF.Silu), reads=[f"ps{bg}"], writes=[f"sg{n % 2}"])
                P.op("dve", lambda h, tmp=tmp, bu=bu, fc=fc: h.tensor_tensor(hTs[hs][:, fc, 0:nsl], ps[bu][:, 0:nsl], tmp, ALU.mult),
                     reads=[f"ps{bu}", f"sg{n % 2}"], writes=[f"hTs{hs}"])

        def down(s_, hs, first, nst):
            for st_ in range(nst):
                for half in range(2):
                    n = cnt["d"]
                    cnt["d"] += 1
                    b_ = 4 + n % 4

                    def mm(h, st_=st_, half=half, b_=b_):
                        last = None
                        for fc in range(4):
                            last = h.matmul(ps[b_], hTs[hs][:, fc, st_ * 128:(st_ + 1) * 128], wst[s_]["d"][:, fc, half * 512:(half + 1) * 512],
                                            start=(fc == 0), stop=(fc == 3))
                        return last
                    P.op("pe", mm, reads=[f"wd{s_}", f"hTs{hs}"], writes=[f"ps{b_}"])
                    ya = yacc[:, st_, half * 512:(half + 1) * 512]
                    if first:
                        P.op("dve", lambda h, ya=ya, b_=b_: h.tensor_copy(ya, ps[b_]), reads=[f"ps{b_}"], writes=[f"yacc{st_}"])
                    else:
                        P.op("dve", lambda h, ya=ya, b_=b_: h.tensor_tensor(ya, ps[b_], ya, ALU.add),
                             reads=[f"ps{b_}", f"yacc{st_}"], writes=[f"yacc{st_}"])

        def block(e, off, nsl):
            nst = nsl // 128
            P.op("dve", lambda h: h.tensor_scalar(rsh[:], rankm[:, :, e], float(-off), None, ALU.add),
                 reads=["rankm"], writes=["rsh"])
            s0 = load_w(e, 0)
            for t in range(NT):
                sl_ = selt[t % 2]
                P.op("dve", lambda h, sl_=sl_, t=t: h.tensor_scalar(sl_[:, 0:nsl], iota512[:, 0:nsl], rsh[:, t:t + 1], None, ALU.is_equal),
                     reads=["iota", "rsh"], writes=[f"selt{t % 2}"])

                def mm(h, sl_=sl_, t=t):
                    last = None
                    for c in range(KC):
                        last = h.matmul(ps[c][:, 0:nsl], xtok[:, t, c * 128:(c + 1) * 128], sl_[:, 0:nsl], start=(t == 0), stop=(t == NT - 1))
                    return last
                P.op("pe", mm, reads=[f"selt{t % 2}", f"xtok{t}"], writes=[f"ps{c}" for c in range(KC)])
            for c in range(KC):
                if c % 2 == 0:
                    P.op("act", lambda h, c=c: h.activation(xg[:, c, 0:nsl], ps[c][:, 0:nsl], AF.Copy), reads=[f"ps{c}"], writes=[f"xg{c}"])
                else:
                    P.op("dve", lambda h, c=c: h.tensor_copy(xg[:, c, 0:nsl], ps[c][:, 0:nsl]), reads=[f"ps{c}"], writes=[f"xg{c}"])
            stages = [s0]
            hidden(s0, 0, nsl)
            for fb in range(1, nblk):
                stages.append(load_w(e, fb))
                hidden(stages[fb], fb % 2, nsl)
                down(stages[fb - 1], (fb - 1) % 2, fb - 1 == 0, nst)
            down(stages[nblk - 1], (nblk - 1) % 2, False, nst)
            for st_ in range(nst):
                if st_ % 2 == 0:
                    P.op("act", lambda h, st_=st_: h.activation(yslot[:, st_, :], yacc[:, st_, :], AF.Copy),
                         reads=[f"yacc{st_}"], writes=[f"yslot{st_}"])
                else:
                    P.op("pool", lambda h, st_=st_: h.tensor_copy(yslot[:, st_, :], yacc[:, st_, :]),
                         reads=[f"yacc{st_}"], writes=[f"yslot{st_}"])
            def sc_stage1(t):
                rb_ = 4 + t % 2
                P.op("dve", lambda h, t=t: h.tensor_scalar(Rb[:], ones[:], rsh[:, t:t + 1], None, ALU.mult),
                     reads=["ones", "rsh"], writes=["Rb"])
                P.op("pe", lambda h, rb_=rb_: h.matmul(ps[rb_][:, 0:128], Rb[:], ident[:], start=True, stop=True),
                     reads=["Rb", "ident"], writes=[f"ps{rb_}"])
                sT = selT[t % 2]

                if "dvemk" in VAR:
                    def mk(h, sT=sT, rb_=rb_):
                        last = None
                        for st_ in range(nst):
                            last = h.tensor_scalar(sT[:, st_, :], ps[rb_][:, 0:128], slotid[:, st_:st_ + 1], None, ALU.is_equal)
                        return last
                    P.op("dve", mk, reads=[f"ps{rb_}", "slotid"], writes=[f"selT{t % 2}"])
                else:
                    ta = ytmp[:, t % 2, :].rearrange("p (s n) -> p s n", s=4)

                    def mk1(h, rb_=rb_, ta=ta):
                        last = None
                        for st_ in range(nst):
                            last = h.activation(ta[:, st_, :], ps[rb_][:, 0:128], AF.Abs, bias=negslot[:, st_:st_ + 1], scale=1.0)
                        return last

                    def mk2(h, sT=sT, ta=ta):
                        return h.activation(sT[:, 0:nst, :], ta[:, 0:nst, :], AF.Relu, bias=one_t, scale=-1.0)
                    P.op("act", mk1, reads=[f"ps{rb_}", "negslot"], writes=[f"sg{t % 2}"])
                    P.op("act", mk2, reads=[f"sg{t % 2}", "ones"], writes=[f"selT{t % 2}"])

            def sc_stage2(t):
                sT = selT[t % 2]
                for half in range(2):
                    ob_ = 6 + half

                    def mm(h, sT=sT, half=half, ob_=ob_):
                        last = None
                        for st_ in range(nst):
                            last = h.matmul(ps[ob_], sT[:, st_, :], yslot[:, st_, half * 512:(half + 1) * 512],
                                            start=(st_ == 0), stop=(st_ == nst - 1))
                        return last
                    P.op("pe", mm, reads=[f"selT{t % 2}"] + [f"yslot{i}" for i in range(nst)], writes=[f"ps{ob_}"])
                    xs = X[:, t, half * 512:(half + 1) * 512]
                    P.op("dve", lambda h, xs=xs, ob_=ob_, t=t: h.scalar_tensor_tensor(xs, ps[ob_], comb[:, t, e:e + 1], xs, ALU.mult, ALU.add),
                         reads=[f"ps{ob_}", f"X{t}a", "comb"], writes=[f"X{t}a"])

            sc_stage1(0)
            for t in range(NT):
                if t + 1 < NT:
                    sc_stage1(t + 1)
                sc_stage2(t)

        BLOCKS = [(0, 512), (512, 128), (640, 384), (1024, 512), (1536, 512)]
        for e in range(NEXP):
            P.regload(("pe", "act", "dve", "pool"), cnti[0:1, e:e + 1], reads=["cnti"])
            ncond = 0
            for (off, nsl) in BLOCKS:
                if off > 0:
                    P.begin_cond(off)
                    ncond += 1
                block(e, off, nsl)
            for _ in range(ncond):
                P.end_cond()

    def do_layer(layer):
        P.start_phase(f"l{layer}p1")
        wv = w_in_d[layer].rearrange("(c p) n -> p c n", p=128)
        ws = [M.at(R_B, [128, KC, 512], BF16, "ws0"), M.at(R_B + 8 * KB, [128, KC, 512], BF16, "ws1")]
        PT = [M.at(R_B + 16 * KB + i * 1280, [128, 640], BF16, f"PT{i}") for i in range(3)]
        qT = M.at(R_A, [128, 4, S], BF16, "qT")
        kT = M.at(R_A + 16 * KB, [128, 2, S], BF16, "kT")
        VaNA = M.at(R_A + 24 * KB, [128, NT, 4, 65], BF16, "VaNA")
        wload(prot[:], cst_d[:, 0:128], "c_prot", ["prot"])
        wload(maskPN[:], cst_d[:, 128:1152].rearrange("p (a n) -> p a n", a=2), "c_mask", ["maskPN"])
        P.op("sp", lambda h: h.dma_start(out=esink[:], in_=sink_d[layer].partition_broadcast(128)),
             writes=["esink"], dma="c_esink")
        P.op("act", lambda h: h.activation(esink[:], esink[:], AF.Exp), reads=["esink"], writes=["esink"])

        def do_group(grp):
            is_na = grp < 2
            base = grp * 768
            nh = 4 if is_na else 2
            if is_na:
                Va = VaNA
                bias = M.at(R_A + 33 * KB, [128, 4, 1536], BF16, "nabias")
                wload(bias[:], nabias_d[layer][:, grp * 4:(grp + 1) * 4, :], "bias", ["bias"])
            else:
                Va = M.at(R_A + 20 * KB, [128, NT, 2, 65], BF16, "VaSW")
                cs = M.at(R_A + 24 * KB + 512, [128, 2, S], F32, "cs")
                rtmp = M.at(R_A + 41 * KB, [128, 2, 512], F32, "rtmp")
                qb = M.at(R_A + 45 * KB, [128, 512], BF16, "qb")
                qb2 = M.at(R_A + 46 * KB, [128, 512], BF16, "qb2")
                P.op("sp", lambda h: h.dma_start(out=cs[:], in_=cs_d.rearrange("a p n -> p a n")),
                     writes=["cs"], dma="c_cs")
            if grp != 1:
                P.op("pool", lambda h, Va=Va: h.memset(Va[:, :, :, 64:65], 1.0), writes=["vones"])
            wload(ws[0][:], wv[:, :, base:base + 512], "ws0", ["ws0"])
            wload(ws[1][:, :, 0:256], wv[:, :, base + 512:base + 768], "ws1", ["ws1"])

            def evac_plain(dst, scale, wres):
                def f(tb, b):
                    d = dst[:, tb * 512:(tb + 1) * 512]
                    if tb % 2 == 0:
                        P.op("act", lambda h: h.activation(d, ps[b], AF.Copy, scale=scale),
                             reads=[f"ps{b}"], writes=[wres + str(tb)])
                    else:
                        P.op("dve", lambda h: h.tensor_scalar(d, ps[b], scale, None, ALU.mult),
                             reads=[f"ps{b}"], writes=[wres + str(tb)])
                return f

            def evac_rot(dst, scale, wres):
                def f(tb, b):
                    d = dst[:, tb * 512:(tb + 1) * 512]
                    o_ = 0
                    rt_ = rtmp if o_ == 0 else ytmp
                    qb_ = qb if o_ == 0 else qb2
                    pb_ = 5 if o_ == 0 else 4
                    P.op("act", lambda h: h.activation(qb_[:], ps[b], AF.Copy), reads=[f"ps{b}"], writes=[f"qb{o_}"])
                    P.op("pe", lambda h: h.matmul(ps[pb_], prot[:], qb_[:], start=True, stop=True),
                         reads=[f"qb{o_}", "prot"], writes=[f"ps{pb_}"])
                    P.op("dve", lambda h: h.scalar_tensor_tensor(rt_[:, 0, :], ps[b], scale, cs[:, 0, tb * 512:(tb + 1) * 512],
                                                                 ALU.mult, ALU.mult),
                         reads=[f"ps{b}", "cs"], writes=[f"rtmp0_{o_}"])
                    P.op("dve", lambda h: h.scalar_tensor_tensor(rt_[:, 1, :], ps[pb_], scale, cs[:, 1, tb * 512:(tb + 1) * 512],
                                                                 ALU.mult, ALU.mult),
                         reads=[f"ps{pb_}", "cs"], writes=[f"rtmp1_{o_}"])
                    P.op("pool", lambda h: h.tensor_tensor(d, rt_[:, 0, :], rt_[:, 1, :], ALU.add),
                         reads=[f"rtmp0_{o_}", f"rtmp1_{o_}"], writes=[wres + str(tb)])
                return f

            if is_na:
                for c in range(2):
                    gemm_fm(ws[0], c * 128, None, evac_plain(qT[:, c, :], 0.125, f"q{c}_"), (6, 7), "ws0", xT_res, "q")
                for c in range(2):
                    gemm_fm(ws[0], 256 + c * 128, None, evac_plain(kT[:, c, :], 1.0, f"k{c}_"), (6, 7), "ws0", xT_res, "k")
            else:
                ev = evac_plain if "norot" in VAR else evac_rot
                for c in range(4):
                    gemm_fm(ws[0], c * 128, None, ev(qT[:, c, :], 0.125, f"q{c}_"), (6, 7), "ws0", xT_res, "q")
                gemm_fm(ws[1], 0, None, ev(kT[:, 0, :], 1.0, "k0_"), (6, 7), "ws1", xT_res, "k")
            vcol0 = 0 if is_na else 128
            vn = nh * 64
            for t in range(NT):
                b = 6 + t % 2

                def mmv(h, t=t, b=b):
                    last = None
                    for k in range(KC):
                        last = h.matmul(ps[b][:, 0:vn], xT[:, k, t * 128:(t + 1) * 128], ws[1][:, k, vcol0:vcol0 + vn],
                                        start=(k == 0), stop=(k == KC - 1))
                    return last
                P.op("pe", mmv, reads=["ws1", f"xT{t}"], writes=[f"ps{b}"])
                dstv = Va[:, t, :, 0:64]
                srcv = ps[b][:, 0:vn].rearrange("p (h d) -> p h d", h=nh)
                if t % 2 == 0:
                    P.op("act", lambda h, dstv=dstv, srcv=srcv: h.activation(dstv, srcv, AF.Copy),
                         reads=[f"ps{b}", "vones"], writes=[f"v{t}"])
                else:
                    P.op("dve", lambda h, dstv=dstv, srcv=srcv: h.tensor_copy(dstv, srcv),
                         reads=[f"ps{b}", "vones"], writes=[f"v{t}"])

            if debug == f"p1proj{grp}" and layer == 0:
                dbg("dbg_q", qT[:], [128, 4, S], BF16, reads=[f"q{c}_{tb}" for c in range(4) for tb in range(4)])
                dbg("dbg_k", kT[:], [128, 2, S], BF16, reads=[f"k{c}_{tb}" for c in range(2) for tb in range(4)])
                dbg("dbg_v", Va[:], [128, NT, nh, 65], BF16, reads=[f"v{t}" for t in range(NT)])
                raise StopBuild()

            steps = []
            if is_na:
                for i in range(NT):
                    if i < 2:
                        js, unm = list(range(3, -1, -1)), True
                    elif i >= 14:
                        js, unm = list(range(15, 11, -1)), True
                    else:
                        js, unm = list(range(i + 2, i - 3, -1)), False
                    for hh in range(4):
                        steps.append((i, hh, js, unm))
            else:
                for n in range(NT):
                    for kv in range(2):
                        js = [j for j in (n - 1, n, n + 1) if 0 <= j < NT]
                        for j in js:
                            steps.append((n, kv, [j], j - n))
            nsteps = len(steps)

            def emit_S(si):
                sb = (0, 2)[si % 2]
                if is_na:
                    i, hh, js, unm = steps[si]
                    c, po = hh // 2, (hh % 2) * 64

                    def f(h):
                        last = None
                        for jj, j in enumerate(js):
                            o = PS[:, sb + jj // 4, (jj % 4) * 128:(jj % 4 + 1) * 128]
                            h.matmul(o, kT[po:po + 64, c, j * 128:(j + 1) * 128], qT[po:po + 64, c, i * 128:(i + 1) * 128],
                                     start=(jj % 4 == 0), stop=False, skip_group_check=True)
                        dl0 = js[0] - i
                        col0 = (640 + 64 * (6 - 2 * dl0)) if unm else (64 * (4 - 2 * dl0))
                        n0 = min(len(js), 4) * 128
                        last = h.matmul(PS[:, sb, 0:n0], identb[:], bias[:, hh, col0:col0 + n0], start=False, stop=True,
                                        skip_group_check=True)
                        if len(js) > 4:
                            last = h.matmul(PS[:, sb + 1, 0:128], identb[:], bias[:, hh, col0 + 512:col0 + 640], start=False, stop=True,
                                            skip_group_check=True)
                        return last
                    rd = ["bias", "identb", f"q{c}_{i // 4}"] + [f"k{c}_{j // 4}" for j in js]
                    P.op("pe", f, reads=rd, writes=[f"ps{sb}", f"ps{sb + 1}"])
                else:
                    n, kv, js, rel = steps[si]
                    j = js[0]
                    po = kv * 64

                    def f(h):
                        o = ps[sb]
                        last = h.matmul(o, kT[po:po + 64, 0, j * 128:(j + 1) * 128], qT[po:po + 64, :, n * 128:(n + 1) * 128],
                                        start=True, stop=(rel == 0))
                        if rel != 0:
                            last = h.matmul(o, identb[:], maskPN[:, 0 if rel < 0 else 1, :], start=False, stop=True)
                        return last
                    rd = ["maskPN", "identb", f"k0_{j // 4}"] + [f"q{c}_{n // 4}" for c in range(4)]
                    P.op("pe", f, reads=rd, writes=[f"ps{sb}"])

            def emit_exp(si):
                sb = (0, 2)[si % 2]
                pt = PT[si % 3]
                if is_na:
                    nj = len(steps[si][2])
                    src = PS[:, sb:sb + 2, :].rearrange("p a n -> p (a n)")[:, 0:nj * 128]
                    P.op("act", lambda h: h.activation(pt[:, 0:nj * 128], src, AF.Exp),
                         reads=[f"ps{sb}", f"ps{sb + 1}"], writes=[f"PT{si % 3}"])
                else:
                    P.op("act", lambda h: h.activation(pt[:, 0:512], ps[sb], AF.Exp),
                         reads=[f"ps{sb}"], writes=[f"PT{si % 3}"])

            def emit_PV(si):
                pt = PT[si % 3]
                if is_na:
                    i, hh, js, unm = steps[si]
                    ob = 4 + i % 2

                    def f(h):
                        last = None
                        for jj, j in enumerate(js):
                            last = h.matmul(ps[ob][:, hh * 65:(hh + 1) * 65], pt[:, jj * 128:(jj + 1) * 128], Va[:, j, hh, :],
                                            start=(jj == 0), stop=(jj == len(js) - 1), skip_group_check=True)
                        return last
                    P.op("pe", f, reads=[f"PT{si % 3}"] + [f"v{j}" for j in js], writes=[f"ps{ob}"])
                    if hh == 3:
                        emit_norm(i, [(ob, 4)])
                else:
                    n, kv, js, rel = steps[si]
                    j = js[0]
                    obs = (4, 5) if n % 2 == 0 else (1, 3)
                    ob = obs[kv]
                    first = (j == max(0, n - 1))
                    lastj = (j == min(NT - 1, n + 1))

                    def f(h):
                        last = None
                        for g in range(4):
                            last = h.matmul(ps[ob][:, g * 65:(g + 1) * 65], pt[:, g * 128:(g + 1) * 128], Va[:, j, kv, :],
                                            start=(first and g == 0), stop=lastj, skip_group_check=True)
                        return last
                    P.op("pe", f, reads=[f"PT{si % 3}", f"v{j}"], writes=[f"ps{ob}"])
                    if kv == 1 and lastj:
                        emit_norm(n, [(obs[0], 4), (obs[1], 4)])

            def emit_norm(i, pieces):
                sl = i % 2
                nheads = sum(p_[1] for p_ in pieces)
                h0 = 0
                for (ob, nh_) in pieces:
                    o3 = ps[ob][:, 0:nh_ * 65].rearrange("p (h d) -> p h d", d=65)
                    rd_ = rden[:, sl, h0:h0 + nh_]
                    if is_na:
                        P.op("dve", lambda h, rd_=rd_, o3=o3: h.reciprocal(rd_, o3[:, :, 64]),
                             reads=[f"ps{ob}"], writes=[f"rden{sl}_{h0}"])
                    else:
                        P.op("dve", lambda h, rd_=rd_, o3=o3, h0=h0, nh_=nh_: h.tensor_tensor(rd_, o3[:, :, 64], esink[:, h0:h0 + nh_], ALU.add),
                             reads=[f"ps{ob}", "esink"], writes=[f"rden{sl}_{h0}"])
                        P.op("dve", lambda h, rd_=rd_: h.reciprocal(rd_, rd_), reads=[f"rden{sl}_{h0}"], writes=[f"rden{sl}_{h0}"])

                    def fn(h, o3=o3, h0=h0, nh_=nh_):
                        last = None
                        for hd in range(nh_):
                            last = h.tensor_scalar(ytmp[:, sl, (h0 + hd) * 64:(h0 + hd + 1) * 64], o3[:, hd, 0:64],
                                                   rden[:, sl, h0 + hd:h0 + hd + 1], None, ALU.mult)
                        return last
                    P.op("dve", fn, reads=[f"ps{ob}", f"rden{sl}_{h0}"], writes=[f"ytmp{sl}_{h0}"])
                    h0 += nh_
                nchunk = nheads // 2
                tb_ = 6 + i % 2
                yt = ytmp[:, sl, :]

                def tp(h):
                    last = None
                    for c in range(nchunk):
                        last = h.transpose(ps[tb_][:, c * 128:(c + 1) * 128], yt[:, c * 128:(c + 1) * 128], ident[:])
                    return last
                P.op("pe", tp, reads=[f"ytmp{sl}_{hh_}" for hh_ in range(0, nheads, 4)] + ["ident"], writes=[f"ps{tb_}"])
                ydst = yT[0 if is_na else 1]
                c0 = grp * 2 if is_na else 0
                dst = ydst[:, c0:c0 + nchunk, i * 128:(i + 1) * 128]
                srcp = ps[tb_][:, 0:nchunk * 128].rearrange("p (c n) -> p c n", c=nchunk)
                P.op("act", lambda h: h.activation(dst, srcp, AF.Copy), reads=[f"ps{tb_}"], writes=[f"yT{i}"])

            if debug and debug.startswith("att"):
                _, g_, n_, what = debug.split("_")
                if int(g_) == grp:
                    for si in range(int(n_)):
                        emit_S(si)
                        if "E" in what:
                            emit_exp(si)
                        if "P" in what:
                            emit_PV(si)
                    raise StopBuild()
            emit_S(0)
            emit_exp(0)
            for si in range(nsteps):
                if si + 1 < nsteps:
                    emit_S(si + 1)
                    emit_exp(si + 1)
                emit_PV(si)
        for grp in range(3):
            do_group(grp)
            if grp >= 1:
                P.barrier()
        if debug == "p1" and layer == 0:
            dbg("dbg_yna", yT[0][:], [128, 4, S], BF16, reads=[])
            dbg("dbg_ysw", yT[1][:], [128, 4, S], BF16, reads=[])
            raise StopBuild()

        P.start_phase(f"l{layer}p2")
        zT = M.at(R_A, [128, KC, S], BF16, "zT")
        wo = M.at(R_A + 32 * KB, [128, KC, D], BF16, "wo")
        p2w = [M.at(R_B + i * 6 * KB, [128, 24, 128], BF16, f"p2w{i}") for i in range(2)]
        load_gb(1 + 2 * layer)
        P.op("sp", lambda h: h.dma_start(out=bgate[:], in_=bgate_d[layer]), writes=["bgate"], dma="c_bgate")
        wov = wout_d[layer].rearrange("(c p) n -> p c n", p=128)
        wbv = [wbr_d[layer, br].rearrange("(c p) n -> p c n", p=128) for br in range(2)]

        def load_p2w(c):
            s_ = c % 2
            cs_ = slice(c * 128, (c + 1) * 128)
            wload(p2w[s_][:, 0:4, :], wbv[0][:, :, cs_], f"p2w{s_}", [])
            wload(p2w[s_][:, 4:8, :], wbv[1][:, :, cs_], f"p2w{s_}", [])
            wload(p2w[s_][:, 8:16, :], wv[:, :, 2304 + c * 128:2304 + (c + 1) * 128], f"p2w{s_}", [])
            wload(p2w[s_][:, 16:24, :], wv[:, :, 3328 + c * 128:3328 + (c + 1) * 128], f"p2w{s_}", [f"p2w{s_}"])

        def load_p2w(c):
            s_ = c % 2
            cs_ = slice(c * 128, (c + 1) * 128)
            for (r0, r1, srcap) in ((0, 4, wbv[0][:, :, cs_]), (4, 8, wbv[1][:, :, cs_]),
                                    (8, 16, wv[:, :, 2304 + c * 128:2304 + (c + 1) * 128]),
                                    (16, 24, wv[:, :, 3328 + c * 128:3328 + (c + 1) * 128])):
                wload(p2w[s_][:, r0:r1, :], srcap, f"p2w{s_}", [f"p2w{s_}"])

        load_p2w(0)
        load_p2w(1)
        wload(wo[:, :, 0:512], wov[:, :, 0:512], "wo", ["wo"])
        wload(wo[:, :, 512:1024], wov[:, :, 512:1024], "wo", ["wo"])
        step = 0
        for c in range(KC):
            s_ = c % 2
            w_ = p2w[s_]
            for tb in range(4):
                bk = (0, 1, 2, 3) if step % 2 == 0 else (4, 5, 6, 7)
                tsl = slice(tb * 512, (tb + 1) * 512)

                def mm(h, w_=w_, bk=bk, tsl=tsl):
                    last = None
                    for k in range(4):
                        last = h.matmul(ps[bk[0]], w_[:, k, :], yT[0][:, k, tsl], start=(k == 0), stop=(k == 3))
                    for k in range(4):
                        last = h.matmul(ps[bk[1]], w_[:, 4 + k, :], yT[1][:, k, tsl], start=(k == 0), stop=(k == 3))
                    for k in range(KC):
                        last = h.matmul(ps[bk[2]], w_[:, 8 + k, :], xT[:, k, tsl], start=(k == 0), stop=(k == KC - 1))
                    for k in range(KC):
                        last = h.matmul(ps[bk[3]], w_[:, 16 + k, :], xT[:, k, tsl], start=(k == 0), stop=(k == KC - 1))
                    return last
                P.op("pe", mm, reads=[f"p2w{s_}"], writes=[f"ps{b_}" for b_ in bk])
                t0_, t1_ = ytmp[:, 0, :], ytmp[:, 1, :]
                P.op("act", lambda h, bk=bk, c=c: h.activation(t0_, ps[bk[2]], AF.Sigmoid, bias=bgate[:, c:c + 1], scale=1.0),
                     reads=[f"ps{bk[2]}", "bgate"], writes=["g0"])
                P.op("dve", lambda h, bk=bk: h.tensor_tensor(t0_, ps[bk[0]], t0_, ALU.mult),
                     reads=[f"ps{bk[0]}", "g0"], writes=["g0"])
                P.op("act", lambda h, bk=bk, c=c: h.activation(t1_, ps[bk[3]], AF.Sigmoid, bias=bgate[:, 8 + c:9 + c], scale=1.0),
                     reads=[f"ps{bk[3]}", "bgate"], writes=["g1"])
                P.op("dve", lambda h, bk=bk: h.tensor_tensor(t1_, ps[bk[1]], t1_, ALU.mult),
                     reads=[f"ps{bk[1]}", "g1"], writes=["g1"])
                P.op("pool", lambda h, c=c, tsl=tsl: h.tensor_tensor(zT[:, c, tsl], t0_, t1_, ALU.add),
                     reads=["g0", "g1"], writes=[f"z{tb}"])
                step += 1
            if c + 2 < KC:
                load_p2w(c + 2)

        if debug == "p2z" and layer == 0:
            dbg("dbg_z", zT[:], [128, KC, S], BF16, reads=[f"z{tb}" for tb in range(4)])
            raise StopBuild()

        x32 = M.at(R_Y, [128, KC, 128], F32, "x32")
        xtok = xtok_g
        if layer % 2 == 1:
            P.barrier()
        if layer % 2 == 1:
            P.op("sp", lambda h: h.dma_start(out=wr[:], in_=wr_d.rearrange("(c p) e -> p c e", p=128)),
                 writes=["wr"], dma="c_wr")
        def wo_pre(t):
            gb_ = (0, 1) if t % 2 == 0 else (2, 3)

            def mmo(h, t=t, gb_=gb_):
                last = None
                for half in range(2):
                    for k in range(KC):
                        last = h.matmul(ps[gb_[half]], zT[:, k, t * 128:(t + 1) * 128], wo[:, k, half * 512:(half + 1) * 512],
                                        start=(k == 0), stop=(k == KC - 1))
                return last
            P.op("pe", mmo, reads=["wo", f"z{t // 4}"], writes=[f"ps{gb_[0]}", f"ps{gb_[1]}"])
            for half in range(2):
                xs = X[:, t, half * 512:(half + 1) * 512]
                P.op("dve", lambda h, xs=xs, b_=gb_[half]: h.scalar_tensor_tensor(xs, xs, ALPHA, ps[b_], ALU.mult, ALU.add),
                     reads=[f"ps{gb_[half]}", f"X{t}"], writes=[f"X{t}"])
            ln_A(t)

        wo_pre(0)
        for t in range(NT):
            if t + 1 < NT:
                wo_pre(t + 1)
            if layer % 2 == 1:
                ln_B(t, (4, 5) if t % 2 == 0 else (6, 7), "act" if t % 2 == 0 else "dve",
                     router=x32, do_T=False, xtok=xtok)
            else:
                ln_B(t, (4, 5) if t % 2 == 0 else (6, 7), "act" if t % 2 == 0 else "dve")
        if layer % 2 == 1:
            router_finish()
        P.barrier()
        if debug == "p2" and layer == 0:
            dbg("dbg_X", X[:], [128, NT, D], F32, reads=[])
            dbg("dbg_xT", xT[:], [128, KC, S], BF16, reads=[])
            raise StopBuild()

        P.start_phase(f"l{layer}p3")
        load_gb(2 + 2 * layer)
        for t in range(NT):
            P.op("act", lambda h, t=t: h.activation(X[:, t, :], X[:, t, :], AF.Copy, scale=ALPHA),
                 reads=[f"X{t}"], writes=[f"X{t}"])
        if layer % 2 == 0 or "densemoe" in VAR:
            if layer % 2 == 0:
                passes = [(fg_d, fu_d, fd_d, FF_DENSE, None)]
            else:
                passes = [(mg_d[e], mu_d[e], md_d[e], FF_EXP, e) for e in range(NEXP)]
            items = []
            for (g_d, u_d, d_d, F_, e_) in passes:
                nblk = (F_ + 511) // 512
                for fb in range(nblk):
                    items.append((g_d, u_d, d_d, fb, min(4, (F_ - fb * 512) // 128), e_))
            wst = [dict(g=M.at(R_A + s_ * 24 * KB, [128, KC, 512], BF16, f"wg{s_}"),
                        u=M.at(R_A + s_ * 24 * KB + 8 * KB, [128, KC, 512], BF16, f"wu{s_}"),
                        d=M.at(R_A + s_ * 24 * KB + 16 * KB, [128, 4, D], BF16, f"wd{s_}")) for s_ in range(2)]
            hT = [M.at(R_Y + s_ * 16 * KB, [128, 4, S], BF16, f"hT{s_}") for s_ in range(2)]

            def ffn_load(ii):
                g_d, u_d, d_d, fb, nfc, e_ = items[ii]
                s_ = ii % 2
                n_ = nfc * 128
                gv = g_d.rearrange("(c p) n -> p c n", p=128)
                uv = u_d.rearrange("(c p) n -> p c n", p=128)
                dv = d_d[fb * 512:fb * 512 + n_, :].rearrange("(c p) n -> p c n", p=128)
                wload(wst[s_]["g"][:, :, 0:n_], gv[:, :, fb * 512:fb * 512 + n_], f"wg{s_}", [f"wg{s_}"])
                wload(wst[s_]["u"][:, :, 0:n_], uv[:, :, fb * 512:fb * 512 + n_], f"wu{s_}", [f"wu{s_}"])
                for hf_ in range(2):
                    wload(wst[s_]["d"][:, 0:nfc, hf_ * 512:(hf_ + 1) * 512], dv[:, :, hf_ * 512:(hf_ + 1) * 512], f"wd{s_}", [f"wd{s_}"])

            hstep = {"n": 0}

            def ffn_hidden(ii):
                g_d, u_d, d_d, fb, nfc, e_ = items[ii]
                s_ = ii % 2
                for fc in range(nfc):
                    for tb in range(4):
                        n = hstep["n"]
                        hstep["n"] += 1
                        bg, bu = (0, 1) if n % 2 == 0 else (2, 3)
                        tsl = slice(tb * 512, (tb + 1) * 512)

                        def mm(h, fc=fc, tsl=tsl, bg=bg, bu=bu, s_=s_):
                            last = None
                            for k in range(KC):
                                last = h.matmul(ps[bg], wst[s_]["g"][:, k, fc * 128:(fc + 1) * 128], xT[:, k, tsl],
                                                start=(k == 0), stop=(k == KC - 1))
                            for k in range(KC):
                                last = h.matmul(ps[bu], wst[s_]["u"][:, k, fc * 128:(fc + 1) * 128], xT[:, k, tsl],
                                                start=(k == 0), stop=(k == KC - 1))
                            return last
                        P.op("pe", mm, reads=[f"wg{s_}", f"wu{s_}"], writes=[f"ps{bg}", f"ps{bu}"])
                        tmp = ytmp[:, n % 2, :]
                        P.op("act", lambda h, tmp=tmp, bg=bg: h.activation(tmp, ps[bg], AF.Silu),
                             reads=[f"ps{bg}"], writes=[f"sg{n % 2}"])
                        P.op("dve", lambda h, tmp=tmp, bu=bu, fc=fc, tsl=tsl, s_=s_: h.tensor_tensor(hT[s_][:, fc, tsl], ps[bu], tmp, ALU.mult),
                             reads=[f"ps{bu}", f"sg{n % 2}"], writes=[f"hT{s_}_{tb}"])

            dstep = {"n": 0}

            def ffn_down(ii):
                g_d, u_d, d_d, fb, nfc, e_ = items[ii]
                s_ = ii % 2
                for t in range(NT):
                    for half in range(2):
                        n = dstep["n"]
                        dstep["n"] += 1
                        b_ = 4 + n % 4

                        def mm(h, t=t, half=half, b_=b_, s_=s_, nfc=nfc):
                            last = None
                            for fc in range(nfc):
                                last = h.matmul(ps[b_], hT[s_][:, fc, t * 128:(t + 1) * 128], wst[s_]["d"][:, fc, half * 512:(half + 1) * 512],
                                                start=(fc == 0), stop=(fc == nfc - 1))
                            return last
                        P.op("pe", mm, reads=[f"wd{s_}", f"hT{s_}_{t // 4}"], writes=[f"ps{b_}"])
                        xs = X[:, t, half * 512:(half + 1) * 512]
                        sc_ = 1.0 if e_ is None else comb[:, t, e_:e_ + 1]
                        P.op("dve", lambda h, xs=xs, b_=b_, sc_=sc_: h.scalar_tensor_tensor(xs, ps[b_], sc_, xs, ALU.mult, ALU.add),
                             reads=[f"ps{b_}", f"X{t}", "comb"], writes=[f"X{t}"])

            ffn_load(0)
            if len(items) > 1:
                ffn_load(1)
            for ii in range(len(items)):
                ffn_hidden(ii)
                if ii > 0:
                    ffn_down(ii - 1)
                    if ii + 1 < len(items):
                        ffn_load(ii + 1)
            ffn_down(len(items) - 1)

        else:
            moe_routed(layer)
            P.barrier()
        ln_A(0)
        for t in range(NT):
            if t + 1 < NT:
                ln_A(t + 1)
            ln_B(t, (0, 1) if t % 2 == 0 else (2, 3), "act" if t % 2 == 0 else "dve", do_T=(layer + 1 < DEPTH))
            if layer + 1 == DEPTH and debug is None:
                P.op("sp", lambda h, t=t: h.dma_start(out=out_d[t * 128:(t + 1) * 128, :], in_=X[:, t, :]),
                     reads=[f"X{t}"], dma="out")
        P.barrier()
        if debug == f"l{layer}":
            dbg("dbg_X", X[:], [128, NT, D], F32, reads=[])
            raise StopBuild()

    try:
        for layer in range(n_layers):
            do_layer(layer)
    except StopBuild:
        P.barrier()

    P.start_phase("pout")
    if debug is not None or n_layers < DEPTH:
        for t in range(NT):
            P.op("sp", lambda h, t=t: h.dma_start(out=out_d[t * 128:(t + 1) * 128, :], in_=X[:, t, :]),
                 reads=[f"X{t}"], dma="out")
    P.final_wait("sp")
    P.flush()
    return nc, dbg_outs


def _perm_w_in(w_in):
    idx = []
    for hf in range(2):
        idx += list(range(256 * hf, 256 * hf + 256))
        idx += list(range(512 + 256 * hf, 512 + 256 * hf + 256))
        idx += list(range(1024 + 256 * hf, 1024 + 256 * hf + 256))
    for c in range(4):
        idx += list(range(1536 + c * 64, 1536 + c * 64 + 64))
        idx += list(range(1536 + (c + 4) * 64, 1536 + (c + 4) * 64 + 64))
    idx += list(range(2048, 4352))
    return np.ascontiguousarray(w_in[:, :, np.asarray(idx)])


def _na_bias_tables(rpb):
    L = rpb.shape[0]
    kc = np.arange(64)[:, None]
    qc = np.arange(64)[None, :]
    qcs = np.clip(qc - 8, 0, 48)
    colvalid = (kc >= qcs) & (kc < qcs + 16)
    dc = np.clip(kc - qc + 15, 0, 30)
    out = np.full((L, 128, 8, 1536), NEG, np.float32)

    def Cmat(l, h, e, masked):
        if e < -7 or e > 7 or (masked and not (-4 <= e <= 3)):
            return np.full((64, 64), NEG, np.float32)
        return np.where(colvalid, rpb[l, h, e + 7][dc], NEG).astype(np.float32)

    for l in range(L):
        for h in range(8):
            for masked, e_hi, n, col0 in ((True, 4, 10, 0), (False, 6, 14, 640)):
                for idx in range(n):
                    e = e_hi - idx
                    for kl in range(2):
                        out[l, kl * 64:(kl + 1) * 64, h, col0 + idx * 64:col0 + (idx + 1) * 64] = Cmat(l, h, e + kl, masked)
    return out


def _const_tables():
    pos = np.arange(S, dtype=np.float32)
    inv_freq = (1.0 / (500000.0 ** (np.arange(0, 16, 2, dtype=np.float32) / 16.0))).astype(np.float32)
    ang = pos[:, None] * inv_freq[None, :]
    cos = np.cos(ang).astype(np.float32).T
    sin = np.sin(ang).astype(np.float32).T
    cs = np.zeros((2, 128, S), np.float32)
    cs[0] = 1.0
    for half in range(2):
        o = half * 64
        cs[0, o:o + 8] = cos
        cs[0, o + 8:o + 16] = cos
        cs[1, o:o + 8] = -sin
        cs[1, o + 8:o + 16] = sin
    cst = np.zeros((128, 1152), np.float32)
    for m in range(128):
        d = m % 64
        partner = m + 8 if d < 8 else (m - 8 if d < 16 else m)
        cst[partner, m] = 1.0
    ki = np.arange(128)[:, None]
    qi = np.arange(128)[None, :]
    mP = np.where(ki >= qi, 0.0, NEG).astype(np.float32)
    mN = np.where(ki <= qi, 0.0, NEG).astype(np.float32)
    cst[:, 128:640] = np.tile(mP, (1, 4))
    cst[:, 640:1152] = np.tile(mN, (1, 4))
    return cs, cst


def _const2():
    c = np.zeros((128, 644), np.float32)
    k = np.arange(128)[:, None]
    m = np.arange(128)[None, :]
    c[:, 0:128] = (k < m).astype(np.float32)
    c[:, 128:132] = np.arange(4)[None, :] * 128 + np.arange(128)[:, None]
    c[:, 132:644] = np.arange(512)[None, :]
    return c


def make_in_maps(inputs):
    f = lambda a: np.ascontiguousarray(np.asarray(a, dtype=np.float32))
    x = f(inputs["x"])
    lnp = np.stack([
        np.stack([f(inputs["emb_ln_g"]), f(inputs["emb_ln_b"])]),
        np.stack([f(inputs["ln1_g"])[0], f(inputs["ln1_b"])[0]]),
        np.stack([f(inputs["ln2_g"])[0], f(inputs["ln2_b"])[0]]),
        np.stack([f(inputs["ln1_g"])[1], f(inputs["ln1_b"])[1]]),
        np.stack([f(inputs["ln2_g"])[1], f(inputs["ln2_b"])[1]]),
    ]).astype(np.float32)
    ident = np.eye(128, dtype=np.float32)
    cs, cst = _const_tables()
    shared = {"lnp": lnp, "ident": ident, "w_in": _perm_w_in(f(inputs["w_in"])),
              "nabias": _na_bias_tables(f(inputs["na_rpb"])), "cs": cs, "cst": cst,
              "sink": f(inputs["sw_sink"]),
              "bgate": np.ascontiguousarray(f(inputs["b_gate"]).reshape(DEPTH, 16, 128).transpose(0, 2, 1)),
              "wbr": np.ascontiguousarray(np.stack([f(inputs["w_branch_na"]), f(inputs["w_branch_sw"])], axis=1)),
              "wout": f(inputs["w_out"]),
              "ffn_g": f(inputs["ffn_w_gate"])[0], "ffn_u": f(inputs["ffn_w_up"])[0], "ffn_d": f(inputs["ffn_w_down"])[0],
              "moe_g": f(inputs["moe_w_gate"])[0], "moe_u": f(inputs["moe_w_up"])[0], "moe_d": f(inputs["moe_w_down"])[0],
              "wr": f(inputs["moe_router"])[0], "cst2": _const2()}
    maps = []
    for c in range(NCORES):
        m = dict(shared)
        m["x"] = np.ascontiguousarray(x[c])
        maps.append(m)
    return maps


_CACHE = {}


def kernel(**inputs):
    if "nc" not in _CACHE:
        _CACHE["nc"] = build()[0]
    nc = _CACHE["nc"]
    in_maps = make_in_maps(inputs)
    res = run_bass_kernel_spmd(nc, in_maps, core_ids=list(range(NCORES)))
    out = np.stack([np.asarray(r["out"], dtype=np.float32).reshape(S, D) for r in res.results], axis=0)
    return out
```
